# Optimizing a Trainium2 kernel written in Bass

```python
import jax, jax.numpy as jnp
from jax import lax
import numpy as np

D_MODEL = 1024
BATCH = 8
SEQ = 2048
DEPTH = 1
DEC_BATCH = 128
DEC_SEQ = 1
PAST_LEN = 16384
PAGE_SIZE = 128

EXPAND = 2
D_INNER = EXPAND * D_MODEL
HEAD_DIM = 64
N_HEADS = D_INNER // HEAD_DIM
N_GROUPS = 8
HEADS_PER_GROUP = N_HEADS // N_GROUPS
D_STATE = 128
CONV_WIDTH = 4
CONV_DIM = D_INNER + 2 * N_GROUPS * D_STATE
CHUNK = 128
POOL_DIM = D_MODEL
POOL_WINDOWS = (2, 4, 8, 16)
N_POOL_GROUPS = len(POOL_WINDOWS)
POOL_GROUP_DIM = POOL_DIM // N_POOL_GROUPS
POOL_BUF = max(POOL_WINDOWS) - 1
D_FF = -(-8 * D_MODEL // (3 * 256)) * 256
IN_WIDTHS = (D_INNER, CONV_DIM, N_HEADS, POOL_DIM, D_MODEL, D_MODEL)
IN_DIM = sum(IN_WIDTHS)
EPS = 1e-6

kernel_name = 'hybrid_ssd_pool_gated_decoder_step'


def rms_norm(x, g):
    xf = x.astype(jnp.float32)
    y = xf * lax.rsqrt(jnp.mean(xf * xf, axis=-1, keepdims=True) + EPS)
    return (y * g.astype(jnp.float32)).astype(x.dtype)


def causal_conv(u, prev, conv_w, conv_b):
    L = u.shape[1]
    cat = jnp.concatenate([prev.astype(u.dtype), u], axis=1)
    out = conv_b
    for k in range(CONV_WIDTH):
        out = out + cat[:, k:k + L] * conv_w[k]
    return jax.nn.silu(out), cat[:, -(CONV_WIDTH - 1):]


def ssd_scan(x, dt, a, bmat, cmat, h0):
    bsz, L = x.shape[:2]
    q = min(CHUNK, L)
    n_c = -(-L // q)
    pad = n_c * q - L

    def prep(t):
        t = jnp.pad(t.astype(jnp.float32), [(0, 0), (0, pad)] + [(0, 0)] * (t.ndim - 2))
        return jnp.moveaxis(t.reshape((bsz, n_c, q) + t.shape[2:]), 1, 0)

    xs = prep(x.reshape(bsz, L, N_GROUPS, HEADS_PER_GROUP, HEAD_DIM))
    dts = prep(dt.reshape(bsz, L, N_GROUPS, HEADS_PER_GROUP))
    bs = prep(bmat)
    cs = prep(cmat)
    a_gr = a.astype(jnp.float32).reshape(N_GROUPS, HEADS_PER_GROUP)
    causal = jnp.tril(jnp.ones((q, q), dtype=bool))[None, :, :, None, None]

    def step(h, inp):
        xc, dtc, bc, cc = inp
        acum = jnp.cumsum(dtc * a_gr, axis=1)
        seg = acum[:, :, None] - acum[:, None, :]
        decay = jnp.exp(jnp.where(causal, seg, -jnp.inf))
        cb = jnp.einsum('bign,bjgn->bijg', cc, bc)
        m = cb[..., None] * decay * dtc[:, None]
        y = jnp.einsum('bijgr,bjgrp->bigrp', m, xc)
        y = y + jnp.einsum('bign,bgrpn->bigrp', cc, h) * jnp.exp(acum)[..., None]
        w_end = jnp.exp(acum[:, -1:] - acum) * dtc
        h_new = h * jnp.exp(acum[:, -1])[..., None, None] + jnp.einsum('bjgr,bjgn,bjgrp->bgrpn', w_end, bc, xc)
        return h_new, y

    h0g = h0.astype(jnp.float32).reshape(bsz, N_GROUPS, HEADS_PER_GROUP, HEAD_DIM, D_STATE)
    h_fin, ys = lax.scan(step, h0g, (xs, dts, bs, cs))
    y = jnp.moveaxis(ys, 0, 1).reshape(bsz, n_c * q, N_HEADS, HEAD_DIM)[:, :L]
    return y, h_fin.reshape(bsz, N_HEADS, HEAD_DIM, D_STATE)


def pool_mix(u, prev, start_pos, w_pool_group, pool_scale):
    bsz, L = u.shape[:2]
    cat = jnp.concatenate([prev.astype(u.dtype), u], axis=1)
    csum = jnp.pad(jnp.cumsum(cat.astype(jnp.float32), axis=1), ((0, 0), (1, 0), (0, 0)))
    pos = start_pos + jnp.arange(L)
    outs = []
    for gi, w in enumerate(POOL_WINDOWS):
        sl = slice(gi * POOL_GROUP_DIM, (gi + 1) * POOL_GROUP_DIM)
        end = csum[:, POOL_BUF + 1:POOL_BUF + 1 + L, sl]
        beg = csum[:, POOL_BUF + 1 - w:POOL_BUF + 1 - w + L, sl]
        cnt = jnp.minimum(pos + 1, w).astype(jnp.float32)[None, :, None]
        outs.append((end - beg) / cnt - u[..., sl].astype(jnp.float32))
    d = jnp.stack(outs, axis=2).astype(u.dtype)
    mixed = jnp.einsum('blgc,gcd->blgd', d, w_pool_group).reshape(bsz, L, POOL_DIM)
    return mixed * pool_scale, cat[:, -POOL_BUF:]


def hybrid_layer(x, conv_prev, ssm_prev, pool_prev, start_pos, p):
    bsz, L = x.shape[:2]
    xn = rms_norm(x, p['norm_mix_pre'])
    proj = xn @ p['w_in']
    idx = np.cumsum(IN_WIDTHS)[:-1].tolist()
    z, xbc, dt_raw, u_pool, gate_a, gate_b = jnp.split(proj, idx, axis=-1)
    xbc_c, conv_new = causal_conv(xbc, conv_prev, p['conv_w'], p['conv_b'])
    xs, bm, cm = jnp.split(xbc_c, [D_INNER, D_INNER + N_GROUPS * D_STATE], axis=-1)
    xs = xs.reshape(bsz, L, N_HEADS, HEAD_DIM)
    dt = jax.nn.softplus(dt_raw.astype(jnp.float32) + p['dt_bias'].astype(jnp.float32))
    a = -jnp.exp(p['a_log'].astype(jnp.float32))
    y, ssm_new = ssd_scan(xs, dt, a, bm.reshape(bsz, L, N_GROUPS, D_STATE),
                          cm.reshape(bsz, L, N_GROUPS, D_STATE), ssm_prev)
    y = y + p['d_skip'].astype(jnp.float32)[:, None] * xs.astype(jnp.float32)
    y = y.reshape(bsz, L, D_INNER).astype(x.dtype)
    y_a = rms_norm(y * jax.nn.silu(z), p['ssm_norm'])
    y_b, pool_new = pool_mix(u_pool, pool_prev, start_pos, p['w_pool_group'], p['pool_scale'])
    merged = jax.nn.sigmoid(gate_a) * (y_a @ p['w_branch_a']) + jax.nn.sigmoid(gate_b) * (y_b @ p['w_branch_b'])
    h = x + rms_norm(merged @ p['w_out'], p['norm_mix_post'])
    gu = rms_norm(h, p['norm_ffn_pre']) @ p['w_ffn_in']
    g, up = jnp.split(gu, [D_FF], axis=-1)
    f = (jax.nn.silu(g) * up) @ p['w_ffn_out']
    out = h + rms_norm(f, p['norm_ffn_post'])
    return out, conv_new, ssm_new, pool_new


def setup_inputs(seed: int = 0) -> dict:
    key = jax.random.key(seed)
    ks = jax.random.split(key, 24)
    f32 = jnp.float32
    nrm = lambda k, s, sc: jax.random.normal(k, s, f32) * sc
    gain = lambda k, n: 1.0 + nrm(k, (DEPTH, n), 0.05)
    dt0 = jnp.exp(jax.random.uniform(ks[10], (DEPTH, N_HEADS), f32, np.log(1e-3), np.log(1e-1)))
    return {
        'x_prompt': nrm(ks[0], (BATCH, SEQ, D_MODEL), 1.0),
        'x_sample': nrm(ks[1], (DEC_BATCH, DEC_SEQ, D_MODEL), 1.0),
        'state_conv': nrm(ks[2], (DEPTH, DEC_BATCH, CONV_WIDTH - 1, CONV_DIM), 1.0),
        'state_ssm': nrm(ks[3], (DEPTH, DEC_BATCH, N_HEADS, HEAD_DIM, D_STATE), 0.1),
        'state_pool': nrm(ks[4], (DEPTH, DEC_BATCH, POOL_BUF, POOL_DIM), 1.0),
        'norm_mix_pre': gain(ks[5], D_MODEL),
        'norm_mix_post': gain(ks[6], D_MODEL),
        'norm_ffn_pre': gain(ks[7], D_MODEL),
        'norm_ffn_post': gain(ks[8], D_MODEL),
        'w_in': nrm(ks[9], (DEPTH, D_MODEL, IN_DIM), D_MODEL ** -0.5),
        'conv_w': nrm(ks[11], (DEPTH, CONV_WIDTH, CONV_DIM), CONV_WIDTH ** -0.5),
        'conv_b': nrm(ks[12], (DEPTH, CONV_DIM), 0.01),
        'dt_bias': dt0 + jnp.log(-jnp.expm1(-dt0)),
        'a_log': jnp.log(jax.random.uniform(ks[13], (DEPTH, N_HEADS), f32, 1.0, 16.0)),
        'd_skip': 1.0 + nrm(ks[14], (DEPTH, N_HEADS), 0.1),
        'ssm_norm': gain(ks[15], D_INNER),
        'w_pool_group': nrm(ks[16], (DEPTH, N_POOL_GROUPS, POOL_GROUP_DIM, POOL_GROUP_DIM), POOL_GROUP_DIM ** -0.5),
        'pool_scale': 1.0 + nrm(ks[17], (DEPTH, POOL_DIM), 0.1),
        'w_branch_a': nrm(ks[18], (DEPTH, D_INNER, D_MODEL), D_INNER ** -0.5),
        'w_branch_b': nrm(ks[19], (DEPTH, POOL_DIM, D_MODEL), POOL_DIM ** -0.5),
        'w_out': nrm(ks[20], (DEPTH, D_MODEL, D_MODEL), D_MODEL ** -0.5),
        'w_ffn_in': nrm(ks[21], (DEPTH, D_MODEL, 2 * D_FF), D_MODEL ** -0.5),
        'w_ffn_out': nrm(ks[22], (DEPTH, D_FF, D_MODEL), D_FF ** -0.5),
    }


def reference(x_prompt, x_sample, state_conv, state_ssm, state_pool,
              norm_mix_pre, norm_mix_post, norm_ffn_pre, norm_ffn_post,
              w_in, conv_w, conv_b, dt_bias, a_log, d_skip, ssm_norm,
              w_pool_group, pool_scale, w_branch_a, w_branch_b, w_out,
              w_ffn_in, w_ffn_out):
    h_p, h_s = x_prompt, x_sample
    bp = x_prompt.shape[0]
    conv_p, ssm_p, pool_p, conv_s, ssm_s, pool_s = [], [], [], [], [], []
    for l in range(DEPTH):
        p = dict(norm_mix_pre=norm_mix_pre[l], norm_mix_post=norm_mix_post[l],
                 norm_ffn_pre=norm_ffn_pre[l], norm_ffn_post=norm_ffn_post[l],
                 w_in=w_in[l], conv_w=conv_w[l], conv_b=conv_b[l], dt_bias=dt_bias[l],
                 a_log=a_log[l], d_skip=d_skip[l], ssm_norm=ssm_norm[l],
                 w_pool_group=w_pool_group[l], pool_scale=pool_scale[l],
                 w_branch_a=w_branch_a[l], w_branch_b=w_branch_b[l], w_out=w_out[l],
                 w_ffn_in=w_ffn_in[l], w_ffn_out=w_ffn_out[l])
        zc = jnp.zeros((bp, CONV_WIDTH - 1, CONV_DIM), x_prompt.dtype)
        zs = jnp.zeros((bp, N_HEADS, HEAD_DIM, D_STATE), jnp.float32)
        zq = jnp.zeros((bp, POOL_BUF, POOL_DIM), x_prompt.dtype)
        h_p, c1, s1, q1 = hybrid_layer(h_p, zc, zs, zq, 0, p)
        h_s, c2, s2, q2 = hybrid_layer(h_s, state_conv[l], state_ssm[l], state_pool[l], PAST_LEN, p)
        conv_p.append(c1); ssm_p.append(s1); pool_p.append(q1)
        conv_s.append(c2); ssm_s.append(s2); pool_s.append(q2)
    new_conv_prompt = jnp.stack(conv_p)
    new_ssm_prompt = jnp.stack(ssm_p)
    new_pool_prompt = jnp.stack(pool_p)
    new_conv_sample = jnp.stack(conv_s)
    new_ssm_sample = jnp.stack(ssm_s)
    new_pool_sample = jnp.stack(pool_s)
    return (h_p, h_s, new_conv_prompt, new_ssm_prompt, new_pool_prompt,
            new_conv_sample, new_ssm_sample, new_pool_sample)
```

```python
import numpy as np
from contextlib import ExitStack
import concourse.bass as bass
import concourse.mybir as mybir
from concourse.bass_utils import run_bass_kernel_spmd

F32 = mybir.dt.float32
BF16 = mybir.dt.bfloat16
AF = mybir.ActivationFunctionType
ALU = mybir.AluOpType

NCORES = 8
D = 1024
DI = 2048
NH = 32
HD = 64
NG = 8
DS = 128
CD = 4096
DFF = 2816
SEQ = 2048
NSB = 16
IN_DIM = 9248
C_Z, C_XBC, C_DT, C_POOL, C_GA, C_GB = 0, 2048, 6144, 6176, 7200, 8224
EPS = 1e-6
NEG = -30000.0
POOLW = (2, 4, 8, 16)

CP_GPRE, CP_GFFN, CP_PS, CP_GSSM, CP_CW, CP_CB = 0, 8, 16, 24, 40, 168
NCOL = 200
RP_GPOST, RP_GFPOST, RP_DTB, RP_ALOG, RP_DSK = 0, 1024, 2048, 2080, 2112
NROW = 2144
CS_ID, CS_TRI, CS_NEG, CS_ONE, CS_INV = 0, 128, 256, 384, 512
NCST = 512 + 128


class Buf:
    __slots__ = ("name", "w", "r")

    def __init__(self, name=""):
        self.name = name
        self.w = None
        self.r = {}


class Eng:
    def __init__(self, name, sem):
        self.name = name
        self.sem = sem
        self.n = 0
        self.seen = {}
        self.prog = []
        self.hist = {}


class Sched:
    def __init__(self, nc, stack, n_dma_sems=40):
        self.nc = nc
        self.stack = stack
        self.fence_toks = []
        self.n_once = 0
        self.E = {}
        for nm in ("pe", "act", "dve", "pool", "sp"):
            self.E[nm] = Eng(nm, stack.enter_context(nc.semaphore("s_" + nm)))
        self.dsems = [stack.enter_context(nc.semaphore("d%d" % i)) for i in range(n_dma_sems)]
        self.dval = [0] * n_dma_sems
        self.dhist = {}
        self.dnext = 0
        self.count = 0
        self.budget = None
        self.skip = set()

    def _wait(self, eng, tok):
        key, val, sem = tok
        if eng.seen.get(key, 0) >= val:
            return
        eng.prog.append(("wait", sem, val))
        eng.seen[key] = val
        h = self.dhist.get((key, val)) if isinstance(key, tuple) else self.E[key].hist.get(val)
        if h:
            for k, v in h.items():
                if eng.seen.get(k, 0) < v:
                    eng.seen[k] = v

    def _deps(self, eng, reads, writes):
        toks = []
        for b in reads:
            if b.w is not None:
                toks.append(b.w)
        for b in writes:
            if b.w is not None:
                toks.append(b.w)
            toks.extend(b.r.values())
        for t in toks:
            if eng.name == "pe" and t[0] == "pe":
                continue
            self._wait(eng, t)

    def _update(self, tok, reads, writes):
        for b in reads:
            o = b.r.get(tok[0])
            if o is None or o[1] < tok[1]:
                b.r[tok[0]] = tok
        for b in writes:
            b.w = tok
            b.r = {}

    def op(self, en, fn, reads=(), writes=()):
        self.count += 1
        if (self.budget is not None and self.count > self.budget) or self.count in self.skip:
            return None
        eng = self.E[en]
        self._deps(eng, reads, writes)
        eng.n += 1
        eng.prog.append(("op", fn, (eng.sem, 1)))
        eng.hist[eng.n] = dict(eng.seen)
        tok = (en, eng.n, eng.sem)
        self._update(tok, reads, writes)
        return tok

    def dma(self, en, out, in_, reads=(), writes=(), wait_toks=(), fence=True, once=False, **kw):
        self.count += 1
        if self.budget is not None and self.count > self.budget:
            return None
        eng = self.E[en]
        for wt_ in wait_toks:
            if wt_ is not None:
                self._wait(eng, wt_)
        if once:
            self.n_once += 1
            sem = self.stack.enter_context(self.nc.semaphore("o%d" % self.n_once))
            key = ("o", self.n_once)
            self._deps(eng, reads, writes)
            val = 16
        else:
            i = self.dnext
            self.dnext = (self.dnext + 1) % len(self.dsems)
            key = ("d", i)
            sem = self.dsems[i]
            if self.dval[i] > 0:
                self._wait(eng, (key, self.dval[i], sem))
            self._deps(eng, reads, writes)
            self.dval[i] += 16
            val = self.dval[i]

        def fn(h, out=out, in_=in_, kw=kw):
            return h.dma_start(out=out, in_=in_, **kw)

        eng.prog.append(("op", fn, (sem, 16)))
        self.dhist[(key, val)] = dict(eng.seen)
        tok = (key, val, sem)
        self._update(tok, reads, writes)
        if fence:
            self.fence_toks.append(tok)
        else:
            self.loose_toks = getattr(self, "loose_toks", {})
            self.loose_toks[key] = tok
        return tok

    def barrier(self, fence_buf, fence_fn):
        return

    def finish(self):
        sp = self.E["sp"]
        for i, s in enumerate(self.dsems):
            if self.dval[i] > 0:
                self._wait(sp, (("d", i), self.dval[i], s))
        for tok in getattr(self, "loose_toks", {}).values():
            self._wait(sp, tok)
        for nm, e in self.E.items():
            if nm != "sp" and e.n > 0:
                self._wait(sp, (nm, e.n, e.sem))
        nc = self.nc
        with nc.Block() as block:
            def replay(eng):
                def run(h):
                    for item in eng.prog:
                        if item[0] == "wait":
                            h.wait_ge(item[1], item[2])
                        else:
                            ins = item[1](h)
                            ins.then_inc(item[2][0], item[2][1])
                return run
            block.tensor(replay(self.E["pe"]))
            block.scalar(replay(self.E["act"]))
            block.vector(replay(self.E["dve"]))
            block.gpsimd(replay(self.E["pool"]))
            block.sync(replay(self.E["sp"]))


class Arena:
    def __init__(self, t, nbytes):
        self.t = t
        self.nbytes = nbytes
        self.off = 0
        self.hist = []

    def reset(self, to=0):
        self.off = to

    def alloc(self, shape, dt, parts=128, at=None):
        esz = 4 if dt == F32 else 2
        n = 1
        for s in shape:
            n *= s
        nb = n * esz
        nb_al = (nb + 63) // 64 * 64
        base = self.off if at is None else at
        assert base + nb_al <= self.nbytes, ("arena overflow", base, nb_al, self.nbytes)
        lo, hi = base, base + nb_al
        a = self.t[0:parts, base // 2:(base + nb) // 2]
        if at is None:
            self.off += nb_al
        buf = Buf()
        keep = []
        for (o0, o1, ob) in self.hist:
            if o0 < hi and lo < o1:
                toks = list(ob.r.values())
                if ob.w is not None:
                    toks.append(ob.w)
                for tok in toks:
                    o = buf.r.get(tok[0])
                    if o is None or o[1] < tok[1]:
                        buf.r[tok[0]] = tok
                if lo <= o0 and o1 <= hi:
                    continue
            keep.append((o0, o1, ob))
        keep.append((lo, hi, buf))
        self.hist = keep
        if dt == F32:
            a = a.bitcast(F32)
        if len(shape) == 2:
            a = a.rearrange("p (a b) -> p a b", a=shape[0])
        elif len(shape) == 3:
            a = a.rearrange("p (a b c) -> p a b c", a=shape[0], b=shape[1])
        return a, buf


class _Stop(Exception):
    pass


S_ref = [None]


def build_program(stop=None, budget=None, verbose=False, skip=()):
    nc = bass.Bass("TRN2", target_bir_lowering=False)
    stage_ctr = [0]

    def checkpoint():
        stage_ctr[0] += 1
        if verbose:
            print("checkpoint", stage_ctr[0], "ops so far", S_ref[0].count)
        if stop is not None and stage_ctr[0] > stop:
            raise _Stop()

    def din(name, shape):
        return nc.dram_tensor(name, shape, F32, kind="ExternalInput").ap()

    def dout(name, shape):
        return nc.dram_tensor(name, shape, F32, kind="ExternalOutput").ap()

    xp = din("xp", [SEQ, D])
    xsm = din("xs", [NSB, D])
    sconv = din("sconv", [NSB, 3, CD])
    sssm = din("sssm", [NSB, DI, DS])
    spool = din("spool", [NSB, 15, D])
    w_in = din("w_in", [D, IN_DIM])
    w_pg = din("w_pg", [D, 256])
    w_a = din("w_a", [DI, D])
    w_b = din("w_b", [D, D])
    w_o = din("w_o", [D, D])
    w_fi = din("w_fi", [D, 2 * DFF])
    w_fo = din("w_fo", [DFF, D])
    colp = din("colp", [128, NCOL])
    rowp = din("rowp", [1, NROW])
    cst = din("cst", [128, NCST])
    selc = din("selc", [128, NSB * 128])

    yp = dout("yp", [SEQ, D])
    ysm = dout("ys", [NSB, D])
    ncp = dout("ncp", [3, CD])
    nsp = dout("nsp", [DI, DS])
    npp = dout("npp", [15, D])
    ncs = dout("ncs", [NSB, 3, CD])
    nss = dout("nss", [NSB, DI, DS])
    nps = dout("nps", [NSB, 15, D])

    def dscr(name, shape):
        return nc.dram_tensor(name, shape, BF16, kind="Internal").ap()

    sc_in = dscr("sc_in", [D, IN_DIM])
    sc_pg = dscr("sc_pg", [D, 256])
    sc_a = dscr("sc_a", [DI, D])
    sc_b = dscr("sc_b", [D, D])
    sc_o = dscr("sc_o", [D, D])
    sc_fi = dscr("sc_fi", [D, 2 * DFF])
    sc_fo = dscr("sc_fo", [DFF, D])

    with ExitStack() as st:
        S = Sched(nc, st)
        S.budget = budget
        S.skip = set(skip)
        S_ref[0] = S

        def sbuf(name, shape, dt):
            return st.enter_context(nc.sbuf_tensor(name, shape, dt))

        cstf = sbuf("cstf", [128, NCST], F32)
        colt = sbuf("colt", [128, NCOL], F32)
        rowbc = sbuf("rowbc", [128, NROW], F32)
        identb = sbuf("identb", [128, 128], BF16)
        neg4b = sbuf("neg4b", [128, 4, 128], BF16)
        diagD = sbuf("diagD", [128, NH, 128], BF16)
        abc = sbuf("abc", [128, NH], F32)
        mhalf = sbuf("mhalf", [128, 1], F32)
        wdt = sbuf("wdt", [128, 8, 32], BF16)
        wpg = sbuf("wpg", [128, 8, 256], BF16)
        hT = sbuf("hT", [128, DI], F32)
        hTb = sbuf("hTb", [128, DI], BF16)
        histc = sbuf("histc", [128, 32, 3], F32)
        histp = sbuf("histp", [128, 8, 15], F32)
        junk_t = sbuf("junk", [128, 3, 1024], BF16)
        junk_bufs = [Buf("junk%d" % i) for i in range(3)]
        junk_n = [0]

        def JK():
            i = junk_n[0] % 3
            junk_n[0] += 1
            return junk_t[:, i, :], junk_bufs[i]
        fence = sbuf("fence", [128, 1], F32)
        NSLOT = 4
        slots = [sbuf("wslot%d" % i, [128, 8, 512], BF16) for i in range(NSLOT)]
        slot_bufs = [Buf("slot%d" % i) for i in range(NSLOT)]
        R1B, R2B, R3B = 40 * 1024, 50 * 1024, 40 * 1024
        R1 = Arena(sbuf("R1", [128, R1B // 2], BF16), R1B)
        R2 = Arena(sbuf("R2", [128, R2B // 2], BF16), R2B)
        R3 = Arena(sbuf("R3", [128, R3B // 2], BF16), R3B)

        identf = cstf[:, CS_ID:CS_ID + 128]
        trif = cstf[:, CS_TRI:CS_TRI + 128]
        negf = cstf[:, CS_NEG:CS_NEG + 128]
        onesf = cstf[:, CS_ONE:CS_ONE + 128]
        invc = cstf[:, CS_INV:CS_INV + 128].rearrange("p (c k) -> p c k", c=8)
        gpost_bc = rowbc[:, RP_GPOST:RP_GPOST + D]
        gfpost_bc = rowbc[:, RP_GFPOST:RP_GFPOST + D]
        dtb_bc = rowbc[:, RP_DTB:RP_DTB + NH]
        alog_bc = rowbc[:, RP_ALOG:RP_ALOG + NH]
        dsk_bc = rowbc[:, RP_DSK:RP_DSK + NH]

        B_const = Buf("const")
        B_hT, B_hTb, B_histc, B_histp, B_fence = Buf(), Buf(), Buf(), Buf(), Buf()
        B_in = Buf("inputs")

        psb = [st.enter_context(nc.psum_tensor("ps%d" % i, [128, 512], F32)) for i in range(8)]
        psbuf = [Buf("ps%d" % i) for i in range(8)]
        psn = [0]

        def PS():
            i = psn[0] % 8
            psn[0] += 1
            return psb[i], psbuf[i]

        def act(out, in_, func, reads, writes, **kw):
            return S.op("act", lambda h: h.activation(out=out, in_=in_, func=func, **kw), reads, writes)

        def act_sq(sl, in_, reads, writes, accum_out):
            jk, B_jk = JK()
            parts = in_.shape[0]
            n = 1
            for d_ in in_.shape[1:]:
                n *= d_
            o = jk[0:parts, 0:n]
            if len(in_.shape) == 3:
                o = o.rearrange("p (a b) -> p a b", a=in_.shape[1])
            return act(o, in_, AF.Square, list(reads), list(writes) + [B_jk], accum_out=accum_out)

        def tt(en, out, in0, in1, op, reads, writes):
            return S.op(en, lambda h: h.tensor_tensor(out=out, in0=in0, in1=in1, op=op), reads, writes)

        def ts(en, out, in0, s1, s2, op0, op1, reads, writes):
            if s2 is None:
                return S.op(en, lambda h: h.tensor_scalar(out=out, in0=in0, scalar1=s1, scalar2=None, op0=op0), reads, writes)
            return S.op(en, lambda h: h.tensor_scalar(out=out, in0=in0, scalar1=s1, scalar2=s2, op0=op0, op1=op1), reads, writes)

        def stt(en, out, in0, sc, in1, op0, op1, reads, writes):
            return S.op(en, lambda h: h.scalar_tensor_tensor(out=out, in0=in0, scalar=sc, in1=in1, op0=op0, op1=op1), reads, writes)

        def cp(en, out, in_, reads, writes):
            return S.op(en, lambda h: h.tensor_copy(out=out, in_=in_), reads, writes)

        def mmgroup(pairs, reads, writes):
            def fn(h):
                ins = None
                for (o, l, r, s0, s1) in pairs:
                    ins = h.matmul(o, lhsT=l, rhs=r, start=s0, stop=s1, skip_group_check=True)
                return ins
            return S.op("pe", fn, reads, writes)

        def trgroup(items, ident, reads, writes):
            def fn(h):
                ins = None
                for (o, i_) in items:
                    ins = h.transpose(out=o, in_=i_, identity=ident)
                return ins
            return S.op("pe", fn, reads, writes)

        def rstd_from(ssq_ap, rstd_ap, n, bufs_r, buf_w):
            ts("pool", rstd_ap, ssq_ap, 1.0 / n, EPS, ALU.mult, ALU.add, bufs_r, [buf_w])
            tt("pool", rstd_ap, rstd_ap, mhalf[0:rstd_ap.shape[0], :], ALU.pow, [buf_w, B_const], [buf_w])

        def do_barrier():
            S.barrier(B_fence, lambda h: h.memset(fence[:], 0.0))
            R1.reset()
            R2.reset()
            R3.reset()

        S.dma("sp", cstf[:], cst[:, :], writes=[B_const])
        S.dma("sp", colt[:], colp[:, :], writes=[B_const])
        S.dma("sp", rowbc[:], rowp.partition_broadcast(128).rearrange("p a n -> p (a n)"), writes=[B_const])
        cp("dve", identb[:], identf, [B_const], [B_const])
        cp("dve", neg4b[:], negf.unsqueeze(1).to_broadcast([128, 4, 128]), [B_const], [B_const])
        S.op("pool", lambda h: h.memset(mhalf[:], -0.5), writes=[B_const])
        act(abc[:], alog_bc, AF.Exp, [B_const], [B_const])
        ts("dve", abc[:], abc[:], -1.0, None, ALU.mult, None, [B_const], [B_const])
        for hh in range(NH):
            ts("dve", diagD[:, hh, :], identf, dsk_bc[:, hh:hh + 1], None, ALU.mult, None, [B_const], [B_const])
        S.op("dve", lambda h: h.memset(hT[:], 0.0), writes=[B_hT])
        S.op("pool", lambda h: h.memset(hTb[:], 0.0), writes=[B_hTb])
        S.op("dve", lambda h: h.memset(histc[:], 0.0), writes=[B_histc])
        S.op("pool", lambda h: h.memset(histp[:], 0.0), writes=[B_histp])

        class WB:
            pass

        scr_uid = [0]

        def mk_blocks(src, scr, K, c0, c1, cw=512):
            out = []
            nkc = K // 128
            for k0 in range(0, nkc, 8):
                nk = min(8, nkc - k0)
                row = []
                for cc in range(c0, c1, cw):
                    b = WB()
                    b.src, b.scr = src, scr
                    b.r0, b.r1 = k0 * 128, (k0 + nk) * 128
                    b.c0, b.c1 = cc, min(cc + cw, c1)
                    b.nk, b.nc = nk, b.c1 - b.c0
                    b.buf = Buf()
                    scr_uid[0] += 1
                    b.scr_t = nc.dram_tensor("wt%d" % scr_uid[0], [128, b.nk, b.nc], BF16, kind="Internal").ap()
                    row.append(b)
                out.append(row)
            return out

        W_xbc = mk_blocks(w_in, sc_in, D, C_XBC, C_DT)[0]
        W_z = mk_blocks(w_in, sc_in, D, C_Z, C_XBC)[0]
        W_dt = mk_blocks(w_in, sc_in, D, C_DT, C_POOL)[0]
        W_pool = mk_blocks(w_in, sc_in, D, C_POOL, C_GA)[0]
        W_ga = mk_blocks(w_in, sc_in, D, C_GA, C_GB)[0]
        W_gb = mk_blocks(w_in, sc_in, D, C_GB, IN_DIM)[0]
        W_pg = mk_blocks(w_pg, sc_pg, D, 0, 256)[0]
        W_a = mk_blocks(w_a, sc_a, DI, 0, D)
        W_b = mk_blocks(w_b, sc_b, D, 0, D)[0]
        W_o = mk_blocks(w_o, sc_o, D, 0, D)[0]
        W_fg = mk_blocks(w_fi, sc_fi, D, 0, DFF)[0]
        W_fu = mk_blocks(w_fi, sc_fi, D, DFF, 2 * DFF)[0]
        W_fo = mk_blocks(w_fo, sc_fo, DFF, 0, D)

        cast_order = W_dt + W_pg + W_xbc + W_z + W_pool + W_ga + W_gb
        for cbk in range(2):
            cast_order += [W_a[0][cbk], W_a[1][cbk], W_b[cbk]]
        cast_order += W_o
        for i in range(len(W_fg)):
            cast_order += [W_fg[i], W_fu[i]]
        for cbk in range(2):
            cast_order += [W_fo[0][cbk], W_fo[1][cbk], W_fo[2][cbk]]
        for i, b in enumerate(cast_order):
            b.cast_idx = i
        cast_toks = []

        def ensure_cast(n):
            n = min(n, len(cast_order))
            while len(cast_toks) < n:
                b = cast_order[len(cast_toks)]
                prev = cast_toks[-8] if len(cast_toks) >= 8 else None
                cast_toks.append(S.dma("pool", b.scr_t[:, :, :], b.src[b.r0:b.r1, b.c0:b.c1].rearrange("(k p) f -> p k f", p=128),
                                       reads=[B_in], writes=[b.buf],
                                       wait_toks=[prev], fence=False, once=True))

        ensure_cast(6)
        if stop is not None and stop <= 0:
            S.finish()
            return nc

        slot_n = [0]

        w_prefetch = []

        def prefetch_w(b):
            sl, sb_ = get_w(b, _direct=True)
            w_prefetch.append((b, sl, sb_))

        def get_w(b, _direct=False):
            if not _direct and w_prefetch:
                pb, sl, sb_ = w_prefetch.pop(0)
                assert pb is b, "weight prefetch order mismatch"
                return sl, sb_
            ensure_cast(b.cast_idx + 10)
            i = slot_n[0] % NSLOT
            slot_n[0] += 1
            sl, sb_ = slots[i], slot_bufs[i]
            S.dma("sp", sl[:, 0:b.nk, 0:b.nc], b.scr_t[:, :, :],
                  reads=[b.buf], writes=[sb_], fence=False, max_dma_last_dim=2048)
            return sl, sb_

        S.dma("sp", wdt[:], W_dt[0].scr_t[:, :, :], reads=[W_dt[0].buf], writes=[B_const])
        S.dma("sp", wpg[:], W_pg[0].scr_t[:, :, :], reads=[W_pg[0].buf], writes=[B_const])

        gpre_c = colt[:, CP_GPRE:CP_GPRE + 8]
        gffn_c = colt[:, CP_GFFN:CP_GFFN + 8]
        ps_c = colt[:, CP_PS:CP_PS + 8]
        gssm_c = colt[:, CP_GSSM:CP_GSSM + 16]
        cw_c = colt[:, CP_CW:CP_CW + 128].rearrange("p (c k) -> p c k", c=32)
        cb_c = colt[:, CP_CB:CP_CB + 32]

        def pipeline(items, stages, lag=1):
            n, ns = len(items), len(stages)
            for step in range(n + (ns - 1) * lag):
                for k in range(ns - 1, -1, -1):
                    i = step - k * lag
                    if 0 <= i < n and stages[k] is not None:
                        stages[k](items[i])

        class Ctx:
            pass

        pre_xT = [None]
        A_R2_BASE = 25 * 1024 + 512
        A_R3_XT = 32 * 1024

        def stageA_begin(mode, blk):
            prompt = mode == "prompt"
            T = 128 if prompt else NSB
            NT = 4 if prompt else 1
            TB = T * NT
            tok0 = blk * TB
            x_src = xp if prompt else xsm
            c = Ctx()
            c.T, c.NT = T, NT
            c.xT, c.B_xT = R3.alloc([8, TB], BF16, at=A_R3_XT)
            off = A_R2_BASE
            a_xt, a_xn, a_sq = [], [], []
            for i in range(NT):
                a_xt.append(R2.alloc([D], F32, parts=T, at=off))
                off += 4096
            for i in range(NT):
                a_xn.append(R2.alloc([D], BF16, parts=T, at=off))
                off += 2048
            for i in range(NT):
                a_sq.append(R2.alloc([2], F32, parts=T, at=off))
                off += 64
            c.xn = a_xn
            for t in range(NT):
                xt, B_xt = a_xt[t]
                xn, B_xn = a_xn[t]
                sq, B_sq = a_sq[t]
                S.dma("sp", xt, x_src[tok0 + t * T: tok0 + (t + 1) * T, :], reads=[B_in], writes=[B_xt])
                act_sq('', xt, [B_xt], [B_sq], accum_out=sq[:, 0:1])
                rstd_from(sq[:, 0:1], sq[:, 1:2], D, [B_sq], B_sq)
                act(xn, xt, AF.Copy, [B_xt, B_sq], [B_xn], scale=sq[:, 1:2])
            return c

        def stageA_finish(c):
            T, NT = c.T, c.NT
            for t in range(NT):
                xn, B_xn = c.xn[t]
                pt, B_pt = PS()
                ptb = pt[:].bitcast(BF16).rearrange("p (c t) -> p c t", c=8)
                trgroup([(ptb[:, cc, 0:T], xn[:, cc * 128:(cc + 1) * 128]) for cc in range(8)], identb[0:T, 0:T], [B_xn, B_const], [B_pt])
                tt("dve", c.xT[:, :, t * T:(t + 1) * T], ptb[:, :, 0:T], gpre_c.unsqueeze(2).to_broadcast([128, 8, T]), ALU.mult,
                   [B_pt, B_const], [c.B_xT])

        def run_block(mode, blk):
            prompt = mode == "prompt"
            T = 128 if prompt else NSB
            NT = 4 if prompt else 1
            TB = T * NT
            tok0 = blk * TB
            x_src = xp if prompt else xsm
            y_dst = yp if prompt else ysm
            first = prompt and blk == 0
            last = prompt and blk == SEQ // 512 - 1

            do_barrier()
            szb, B_szb = R3.alloc([NT, DI], BF16, parts=T)
            yaT, B_yaT = R3.alloc([16, TB], BF16)
            if pre_xT[0] is not None:
                xT, B_xT = pre_xT[0]
                pre_xT[0] = None
            else:
                actx = stageA_begin(mode, blk)
                stageA_finish(actx)
                xT, B_xT = actx.xT, actx.B_xT
            dtv = None
            if prompt:
                dtv = [R2.alloc([NT, NH], F32) for _ in range(12)]
            r2_mark = R2.off
            o3s = [R2.alloc([512], F32) for _ in range(1)]

            checkpoint()
            dtc = None
            if prompt:
                dtc = dt_chain(NT, T, xT, B_xT, dtv)
            xs_tok, B_xs = R1.alloc([NT, DI], BF16, parts=T)
            b_tok, B_bt = R1.alloc([NT, NG * DS], BF16, parts=T)
            bT, B_bT = R1.alloc([NG, TB], BF16)
            cT, B_cT = R1.alloc([NG, TB], BF16)
            if not prompt:
                sct, B_sct = R3.alloc([CD], F32, parts=T)
                scT, B_scT = R2.alloc([3, 32, T], F32)
                uraw, B_uraw = R2.alloc([32, T], F32)
                for k in range(3):
                    S.dma("sp", sct, sconv[:, k, :], reads=[B_in, B_fence], writes=[B_sct])
                    for c0 in range(0, 32, 16):
                        pt, B_pt = PS()
                        ptv = pt[:, 0:16 * T].rearrange("p (c t) -> p c t", c=16)
                        trgroup([(ptv[:, c, :], sct[:, (c0 + c) * 128:(c0 + c + 1) * 128]) for c in range(16)],
                                identf[0:T, 0:T], [B_sct, B_const], [B_pt])
                        cp("dve", scT[:, k, c0:c0 + 16, :], ptv, [B_pt], [B_scT])
            ub = [R2.alloc([TB + 3], F32) for _ in range(5)]
            accs = [R2.alloc([TB], F32) for _ in range(5)]
            xfm = [R2.alloc([TB], BF16) for _ in range(2)]
            items = []
            for wi, wb in enumerate(W_xbc):
                for cc in range(4):
                    it = Ctx()
                    it.c, it.cc, it.wb = wi * 4 + cc, cc, wb
                    items.append(it)
            cur_slot = [None]

            def sB0(it):
                if it.cc == 0:
                    cur_slot[0] = get_w(it.wb)
                sl, B_sl = cur_slot[0]
                it.ps, it.B_ps = PS()
                mmgroup([(it.ps[:, 0:TB], sl[:, k, it.cc * 128:(it.cc + 1) * 128], xT[:, k, :], k == 0, k == 7) for k in range(8)],
                        [B_sl, B_xT], [it.B_ps])

            def sB1(it):
                c = it.c
                it.u, it.B_u = ub[c % 5]
                it.acc, it.B_acc = accs[c % 5]
                if prompt:
                    cp("pool", it.u[:, 0:3], histc[:, c, :], [B_histc], [it.B_u])
                    act(it.u[:, 3:3 + TB], it.ps[:, 0:TB], AF.Copy, [it.B_ps], [it.B_u])
                    cp("pool", histc[:, c, :], it.u[:, TB:TB + 3], [it.B_u], [B_histc])
                    it.taps = [it.u[:, k:k + TB] for k in range(4)]
                    it.tap_r = [it.B_u]
                else:
                    act(uraw[:, c, :], it.ps[:, 0:TB], AF.Copy, [it.B_ps], [B_uraw])
                    it.taps = [scT[:, 0, c, :], scT[:, 1, c, :], scT[:, 2, c, :], uraw[:, c, :]]
                    it.tap_r = [B_scT, B_uraw]

            def mk_tap(k):
                def f(it):
                    c = it.c
                    if k == 0:
                        act(it.acc, it.taps[0], AF.Identity, it.tap_r + [B_const], [it.B_acc], scale=cw_c[:, c, 0:1], bias=cb_c[:, c:c + 1])
                    else:
                        stt("dve", it.acc, it.taps[k], cw_c[:, c, k:k + 1], it.acc, ALU.mult, ALU.add, it.tap_r + [B_const, it.B_acc], [it.B_acc])
                return f

            def sB6(it):
                c = it.c
                if c < 16:
                    xf, B_xf = xfm[c % 2]
                    act(xf, it.acc, AF.Silu, [it.B_acc], [B_xf])
                    it.tr = (xf, B_xf, xs_tok, B_xs, c * 128)
                elif c < 24:
                    act(bT[:, c - 16, :], it.acc, AF.Silu, [it.B_acc], [B_bT])
                    it.tr = (bT[:, c - 16, :], B_bT, b_tok, B_bt, (c - 16) * 128)
                else:
                    act(cT[:, c - 24, :], it.acc, AF.Silu, [it.B_acc], [B_cT])
                    it.tr = None

            def sB7(it):
                if it.tr is None:
                    return
                src, B_src, dst, B_dst, col = it.tr
                it.pt, it.B_pt = PS()
                it.ptb = it.pt[:].bitcast(BF16)[0:T, 0:NT * 128].rearrange("p (t f) -> p t f", t=NT)
                trgroup([(it.ptb[:, t, :], src[:, t * T:(t + 1) * T]) for t in range(NT)], identb[:], [B_src, B_const], [it.B_pt])

            def sB8(it):
                if it.tr is None:
                    return
                src, B_src, dst, B_dst, col = it.tr
                act(dst[:, :, col:col + 128], it.ptb, AF.Copy, [it.B_pt], [B_dst])

            pipeline(items, [sB0, sB1, mk_tap(0), mk_tap(1), mk_tap(2), mk_tap(3), sB6, sB7, sB8])
            if prompt:
                dt_chain2(NT, dtv, *dtc)
            for q, wb in enumerate(W_z):
                sl, B_sl = get_w(wb)
                for t in range(NT):
                    ps_, B_ps = PS()
                    mmgroup([(ps_[0:T, :], xT[:, k, t * T:(t + 1) * T], sl[:, k, :], k == 0, k == 7) for k in range(8)],
                            [B_sl, B_xT], [B_ps])
                    act(szb[:, t, q * 512:(q + 1) * 512], ps_[0:T, :], AF.Silu, [B_ps], [B_szb])
            if last:
                for c0 in range(0, 32, 4):
                    pt, B_pt = PS()
                    trgroup([(pt[0:3, c * 128:(c + 1) * 128], histc[:, c0 + c, :]) for c in range(4)], identf, [B_histc, B_const], [B_pt])
                    o3, B_o3 = o3s[0]
                    o3 = o3[0:3, :]
                    cp("dve", o3, pt[0:3, :], [B_pt], [B_o3])
                    S.dma("sp", ncp[:, c0 * 128:(c0 + 4) * 128], o3, reads=[B_o3], writes=[Buf()])
            if not prompt:
                S.dma("sp", ncs[:, 0:2, :], sconv[:, 1:3, :], reads=[B_in], writes=[Buf()])
                for c0 in range(0, 32, 4):
                    pt, B_pt = PS()
                    trgroup([(pt[0:T, c * 128:(c + 1) * 128], uraw[:, c0 + c, :]) for c in range(4)], identf, [B_uraw, B_const], [B_pt])
                    o3, B_o3 = o3s[0]
                    o3 = o3[0:T, :]
                    cp("dve", o3, pt[0:T, :], [B_pt], [B_o3])
                    S.dma("sp", ncs[:, 2, c0 * 128:(c0 + 4) * 128], o3, reads=[B_o3], writes=[Buf()])

            checkpoint()
            S.barrier(B_fence, lambda h: h.memset(fence[:], 0.0))
            R2.reset(r2_mark)
            if prompt:
                ssd_prompt(T, NT, TB, xT, B_xT, xs_tok, B_xs, b_tok, B_bt, bT, B_bT, cT, B_cT, szb, B_szb, yaT, B_yaT, last, dtv)
            else:
                ssd_sample(T, xT, B_xT, xs_tok, B_xs, b_tok, B_bt, bT, B_bT, cT, B_cT, szb, B_szb, yaT, B_yaT)

            checkpoint()
            S.barrier(B_fence, lambda h: h.memset(fence[:], 0.0))
            R1.reset()
            R2.reset()
            dT, B_dT = R1.alloc([8, TB], BF16)
            ybT, B_ybT = R1.alloc([8, TB], BF16)
            mT, B_mT = R1.alloc([8, TB], BF16)
            sga, B_sga = R1.alloc([8, TB], BF16)
            sgb, B_sgb = R1.alloc([8, TB], BF16)
            if prompt:
                ubp = [R2.alloc([TB + 15], F32) for _ in range(2)]
                sA = [R2.alloc([TB + 15], F32) for _ in range(2)]
                sB = [R2.alloc([TB + 15], F32) for _ in range(2)]
                t16 = [R2.alloc([16], F32) for _ in range(2)]
                ci = 0
                for wb in W_pool:
                    sl, B_sl = get_w(wb)
                    for cc in range(4):
                        c = ci
                        ci += 1
                        u, B_u = ubp[c % 2]
                        a_, B_a = sA[c % 2]
                        b_, B_b = sB[c % 2]
                        ps_, B_ps = PS()
                        mmgroup([(ps_[:, 0:TB], sl[:, k, cc * 128:(cc + 1) * 128], xT[:, k, :], k == 0, k == 7) for k in range(8)],
                                [B_sl, B_xT], [B_ps])
                        cp("pool", u[:, 0:15], histp[:, c, :], [B_histp], [B_u])
                        act(u[:, 15:15 + TB], ps_[:, 0:TB], AF.Copy, [B_ps], [B_u])
                        cp("pool", histp[:, c, :], u[:, TB:TB + 15], [B_u], [B_histp])
                        w = POOLW[c // 2]
                        L = TB + 15
                        tt("dve", a_[:, 1:L], u[:, 1:L], u[:, 0:L - 1], ALU.add, [B_u], [B_a])
                        cur, B_cur, oth, B_oth = a_, B_a, b_, B_b
                        sh = 2
                        while sh < w:
                            lo = 2 * sh - 1
                            tt("dve", oth[:, lo:L], cur[:, lo:L], cur[:, lo - sh:L - sh], ALU.add, [B_cur], [B_oth])
                            cur, B_cur, oth, B_oth = oth, B_oth, cur, B_cur
                            sh *= 2
                        stt("dve", dT[:, c, :], cur[:, 15:L], 1.0 / w, u[:, 15:L], ALU.mult, ALU.subtract, [B_cur, B_u], [B_dT])
                        if first:
                            tq, B_tq = t16[c % 2]
                            tt("dve", tq, cur[:, 15:31], invc[:, c, :], ALU.mult, [B_cur, B_const], [B_tq])
                            tt("dve", dT[:, c, 0:16], tq, u[:, 15:31], ALU.subtract, [B_tq, B_u, B_dT], [B_dT])
                if last:
                    for c0 in range(0, 8, 4):
                        pt, B_pt = PS()
                        trgroup([(pt[0:15, c * 128:(c + 1) * 128], histp[:, c0 + c, :]) for c in range(4)], identf, [B_histp, B_const], [B_pt])
                        o3, B_o3 = R2.alloc([512], F32, parts=15)
                        cp("dve", o3, pt[0:15, :], [B_pt], [B_o3])
                        S.dma("sp", npp[:, c0 * 128:(c0 + 4) * 128], o3, reads=[B_o3], writes=[Buf()])
            else:
                spt, B_spt = R2.alloc([15, 256], F32, parts=T)
                ut, B_ut = R2.alloc([D], F32, parts=T)
                sm, B_sm = R2.alloc([D], F32, parts=T)
                dtk, B_dtk = R2.alloc([D], BF16, parts=T)
                for q, wb in enumerate(W_pool):
                    sl, B_sl = get_w(wb)
                    ps_, B_ps = PS()
                    mmgroup([(ps_[0:T, :], xT[:, k, :], sl[:, k, :], k == 0, k == 7) for k in range(8)], [B_sl, B_xT], [B_ps])
                    cp("dve", ut[:, q * 512:(q + 1) * 512], ps_[0:T, :], [B_ps], [B_ut])
                S.dma("sp", nps[:, 0:14, :], spool[:, 1:15, :], reads=[B_in], writes=[Buf()])
                S.dma("sp", nps[:, 14, :], ut, reads=[B_ut], writes=[Buf()])
                for gi, w in enumerate(POOLW):
                    fs = slice(gi * 256, (gi + 1) * 256)
                    S.dma("sp", spt[:, 0:w - 1, :], spool[:, 16 - w:15, fs], reads=[B_in, B_fence], writes=[B_spt])
                    tt("dve", sm[:, fs], spt[:, 0, :], ut[:, fs], ALU.add, [B_spt, B_ut], [B_sm])
                    for kk in range(1, w - 1):
                        tt("dve", sm[:, fs], sm[:, fs], spt[:, kk, :], ALU.add, [B_spt, B_sm], [B_sm])
                    stt("dve", dtk[:, fs], sm[:, fs], 1.0 / w, ut[:, fs], ALU.mult, ALU.subtract, [B_sm, B_ut], [B_dtk])
                pt, B_pt = PS()
                ptb = pt[:].bitcast(BF16)[:, 0:8 * T].rearrange("p (c t) -> p c t", c=8)
                trgroup([(ptb[:, c, :], dtk[:, c * 128:(c + 1) * 128]) for c in range(8)], identb[0:T, 0:T], [B_dtk, B_const], [B_pt])
                cp("dve", dT, ptb, [B_pt], [B_dT])
            for (wl, dst, B_dst) in ((W_ga, sga, B_sga), (W_gb, sgb, B_sgb)):
                ci = 0
                for wb in wl:
                    sl, B_sl = get_w(wb)
                    for cc in range(4):
                        ps_, B_ps = PS()
                        mmgroup([(ps_[:, 0:TB], sl[:, k, cc * 128:(cc + 1) * 128], xT[:, k, :], k == 0, k == 7) for k in range(8)],
                                [B_sl, B_xT], [B_ps])
                        act(dst[:, ci, :], ps_[:, 0:TB], AF.Sigmoid, [B_ps], [B_dst])
                        ci += 1
            for oc in range(8):
                gi = oc // 2
                ps_, B_ps = PS()
                mmgroup([(ps_[:, 0:TB], wpg[:, gi * 2 + kc, (oc % 2) * 128:(oc % 2 + 1) * 128], dT[:, gi * 2 + kc, :], kc == 0, kc == 1)
                         for kc in range(2)], [B_const, B_dT], [B_ps])
                act(ybT[:, oc, :], ps_[:, 0:TB], AF.Copy, [B_ps, B_const], [B_ybT], scale=ps_c[:, oc:oc + 1])
            t1s = [R2.alloc([TB], F32) for _ in range(2)]
            t2s = [R2.alloc([TB], F32) for _ in range(2)]
            for cbk in range(2):
                sa0, B_sa0 = get_w(W_a[0][cbk])
                sa1, B_sa1 = get_w(W_a[1][cbk])
                sbb, B_sbb = get_w(W_b[cbk])
                for cc in range(4):
                    oc = cbk * 4 + cc
                    t1, B_t1 = t1s[oc % 2]
                    t2, B_t2 = t2s[oc % 2]
                    pa, B_pa = PS()
                    mmgroup([(pa[:, 0:TB], (sa0 if k < 8 else sa1)[:, k % 8, cc * 128:(cc + 1) * 128], yaT[:, k, :], k == 0, k == 15)
                             for k in range(16)], [B_sa0, B_sa1, B_yaT], [B_pa])
                    tt("dve", t1, pa[:, 0:TB], sga[:, oc, :], ALU.mult, [B_pa, B_sga], [B_t1])
                    pb, B_pb = PS()
                    mmgroup([(pb[:, 0:TB], sbb[:, k, cc * 128:(cc + 1) * 128], ybT[:, k, :], k == 0, k == 7) for k in range(8)],
                            [B_sbb, B_ybT], [B_pb])
                    tt("dve", t2, pb[:, 0:TB], sgb[:, oc, :], ALU.mult, [B_pb, B_sgb], [B_t2])
                    tt("pool", mT[:, oc, :], t1, t2, ALU.add, [B_t1, B_t2], [B_mT])

            checkpoint()
            S.barrier(B_fence, lambda h: h.memset(fence[:], 0.0))
            R2.reset()
            R3.reset()
            htl = [R3.alloc([D], F32, parts=T) for _ in range(NT)]
            hnT, B_hnT = R3.alloc([8, TB], BF16)
            so0, B_so0 = get_w(W_o[0])
            so1, B_so1 = get_w(W_o[1])
            fitems = []
            for t in range(NT):
                it = Ctx()
                it.t = t
                it.ht, it.B_ht = htl[t]
                it.sq, it.B_sq = R2.alloc([4], F32, parts=T)
                it.tmp, it.B_tmp = R2.alloc([D], F32, parts=T)
                it.sq2, it.B_sq2 = R2.alloc([2], F32, parts=T)
                it.hn, it.B_hn = R2.alloc([D], BF16, parts=T)
                fitems.append(it)
            so_l = ((so0, B_so0), (so1, B_so1))

            def sF0(it):
                t = it.t
                S.dma("sp", it.ht, x_src[tok0 + t * T: tok0 + (t + 1) * T, :], reads=[B_in, B_fence], writes=[it.B_ht])
                it.po = []
                for cbk, (so, B_so) in enumerate(so_l):
                    po, B_po = PS()
                    mmgroup([(po[0:T, :], mT[:, k, t * T:(t + 1) * T], so[:, k, :], k == 0, k == 7) for k in range(8)],
                            [B_so, B_mT], [B_po])
                    it.po.append((po, B_po))

            def sF1(it):
                for cbk, (po, B_po) in enumerate(it.po):
                    act_sq('0:T, 0:512', po[0:T, :], [B_po], [it.B_sq], accum_out=it.sq[:, cbk:cbk + 1])
                    tt("dve", it.tmp[:, cbk * 512:(cbk + 1) * 512], po[0:T, :], gpost_bc[0:T, cbk * 512:(cbk + 1) * 512], ALU.mult,
                       [B_po, B_const, it.B_sq], [it.B_tmp])

            def sF2(it):
                tt("pool", it.sq[:, 2:3], it.sq[:, 0:1], it.sq[:, 1:2], ALU.add, [it.B_sq], [it.B_sq])
                rstd_from(it.sq[:, 2:3], it.sq[:, 3:4], D, [it.B_sq], it.B_sq)
                stt("dve", it.ht, it.tmp, it.sq[:, 3:4], it.ht, ALU.mult, ALU.add, [it.B_tmp, it.B_sq, it.B_ht], [it.B_ht])

            def sF3(it):
                act_sq('0:T, 0:D', it.ht, [it.B_ht], [it.B_sq2], accum_out=it.sq2[:, 0:1])

            def sF4(it):
                rstd_from(it.sq2[:, 0:1], it.sq2[:, 1:2], D, [it.B_sq2], it.B_sq2)
                act(it.hn, it.ht, AF.Copy, [it.B_ht, it.B_sq2], [it.B_hn], scale=it.sq2[:, 1:2])

            def sF5(it):
                it.pt, it.B_pt = PS()
                it.ptb = it.pt[:].bitcast(BF16).rearrange("p (c t) -> p c t", c=8)
                trgroup([(it.ptb[:, c, 0:T], it.hn[:, c * 128:(c + 1) * 128]) for c in range(8)], identb[0:T, 0:T], [it.B_hn, B_const], [it.B_pt])

            def sF6(it):
                t = it.t
                tt("dve", hnT[:, :, t * T:(t + 1) * T], it.ptb[:, :, 0:T], gffn_c.unsqueeze(2).to_broadcast([128, 8, T]), ALU.mult,
                   [it.B_pt, B_const], [B_hnT])

            pipeline(fitems, [sF0, sF1, sF2, sF3, sF4, sF5, sF6])
            checkpoint()
            R2.reset()
            R1.reset()
            S.barrier(B_fence, lambda h: h.memset(fence[:], 0.0))
            actT, B_actT = R1.alloc([22, TB], BF16)
            sgs = [R2.alloc([TB], F32) for _ in range(2)]
            fc = 0
            for i in range(len(W_fg)):
                sg_, B_sg_ = get_w(W_fg[i])
                su_, B_su_ = get_w(W_fu[i])
                for cc in range(W_fg[i].nc // 128):
                    sgt, B_sgt = sgs[fc % 2]
                    pg, B_pg = PS()
                    mmgroup([(pg[:, 0:TB], sg_[:, k, cc * 128:(cc + 1) * 128], hnT[:, k, :], k == 0, k == 7) for k in range(8)],
                            [B_sg_, B_hnT], [B_pg])
                    act(sgt, pg[:, 0:TB], AF.Silu, [B_pg], [B_sgt])
                    pu, B_pu = PS()
                    mmgroup([(pu[:, 0:TB], su_[:, k, cc * 128:(cc + 1) * 128], hnT[:, k, :], k == 0, k == 7) for k in range(8)],
                            [B_su_, B_hnT], [B_pu])
                    tt("dve", actT[:, fc, :], sgt, pu[:, 0:TB], ALU.mult, [B_sgt, B_pu], [B_actT])
                    fc += 1
            fsb = [R2.alloc([D], F32, parts=T) for _ in range(NT)]
            sq3 = [R2.alloc([4], F32, parts=T) for _ in range(NT)]
            nxt = None
            if prompt:
                nxt = ("prompt", blk + 1) if blk + 1 < SEQ // 512 else ("sample", 0)
            nctx = None
            for cbk in range(2):
                sws = [get_w(W_fo[kg][cbk]) for kg in range(3)]
                if cbk == 0 and nxt is not None:
                    nctx = stageA_begin(*nxt)
                if cbk == 1 and nctx is not None:
                    stageA_finish(nctx)
                    pre_xT[0] = (nctx.xT, nctx.B_xT)
                for t in range(NT):
                    pf, B_pf = PS()
                    mmgroup([(pf[0:T, :], actT[:, k, t * T:(t + 1) * T], sws[k // 8][0][:, k % 8, :], k == 0, k == 21) for k in range(22)],
                            [sws[0][1], sws[1][1], sws[2][1], B_actT], [B_pf])
                    act_sq('', pf[0:T, :], [B_pf], [sq3[t][1]], accum_out=sq3[t][0][:, cbk:cbk + 1])
                    tt("dve", fsb[t][0][:, cbk * 512:(cbk + 1) * 512], pf[0:T, :], gfpost_bc[0:T, cbk * 512:(cbk + 1) * 512], ALU.mult,
                       [B_pf, B_const, sq3[t][1]], [fsb[t][1]])
            if nxt is not None:
                for wb in W_xbc[:3]:
                    prefetch_w(wb)
            for t in range(NT):
                sq, B_sq = sq3[t]
                f_, B_f = fsb[t]
                ht, B_ht = htl[t]
                tt("pool", sq[:, 2:3], sq[:, 0:1], sq[:, 1:2], ALU.add, [B_sq], [B_sq])
                rstd_from(sq[:, 2:3], sq[:, 3:4], D, [B_sq], B_sq)
                stt("dve", f_, f_, sq[:, 3:4], ht, ALU.mult, ALU.add, [B_f, B_sq, B_ht], [B_f])
                S.dma("sp", y_dst[tok0 + t * T: tok0 + (t + 1) * T, :], f_, reads=[B_f], writes=[Buf()])

        def dt_chain(NT, T, xT, B_xT, dtv):
            (v, B_v), (av, B_av), (e_, B_e), (l_, B_l), (dt_, B_dt), (dta, B_dta), (lndt, B_lndt), (nb, B_nb), \
                (eac, B_eac), (wl, B_wl), (wend, B_wend), (elast, B_elast) = dtv
            pa, B_pa = PS()
            pdt = pa[:, 0:NT * NH].rearrange("p (t h) -> p t h", t=NT)
            pac = pa[:, 128:128 + NT * NH].rearrange("p (t h) -> p t h", t=NT)
            ptot = pa[:, 256:256 + NT * NH].rearrange("p (t h) -> p t h", t=NT)
            mmgroup([(pdt[:, t, :], xT[:, k, t * T:(t + 1) * T], wdt[:, k, :], k == 0, k == 7) for t in range(NT) for k in range(8)],
                    [B_xT, B_const], [B_pa])
            tt("dve", v, pdt, dtb_bc.unsqueeze(1).to_broadcast([128, NT, NH]), ALU.add, [B_pa, B_const], [B_v])
            act(av, v, AF.Abs, [B_v], [B_av])
            act(e_, av, AF.Exp, [B_av], [B_e], scale=-1.0)
            act(l_, e_, AF.Ln, [B_e], [B_l], bias=1.0)
            stt("dve", dt_, v, 0.0, l_, ALU.max, ALU.add, [B_v, B_l], [B_dt])
            tt("dve", dta, dt_, abc[:].unsqueeze(1).to_broadcast([128, NT, NH]), ALU.mult, [B_dt, B_const], [B_dta])
            act(lndt, dt_, AF.Ln, [B_dt], [B_lndt])
            return pa, B_pa, pac, ptot

        def dt_chain2(NT, dtv, pa, B_pa, pac, ptot):
            (v, B_v), (av, B_av), (e_, B_e), (l_, B_l), (dt_, B_dt), (dta, B_dta), (lndt, B_lndt), (nb, B_nb), \
                (eac, B_eac), (wl, B_wl), (wend, B_wend), (elast, B_elast) = dtv
            pa, B_pa = PS()
            pac = pa[:, 128:128 + NT * NH].rearrange("p (t h) -> p t h", t=NT)
            ptot = pa[:, 256:256 + NT * NH].rearrange("p (t h) -> p t h", t=NT)
            pairs = []
            for t in range(NT):
                pairs.append((pac[:, t, :], trif, dta[:, t, :], True, True))
                pairs.append((ptot[:, t, :], onesf, dta[:, t, :], True, True))
            mmgroup(pairs, [B_dta, B_const], [B_pa])
            tt("dve", nb, lndt, pac, ALU.subtract, [B_lndt, B_pa], [B_nb])
            act(eac, pac, AF.Exp, [B_pa, B_nb], [B_eac])
            tt("dve", wl, ptot, nb, ALU.add, [B_pa, B_nb, B_eac], [B_wl])
            act(elast, ptot, AF.Exp, [B_pa, B_wl], [B_elast])
            act(wend, wl, AF.Exp, [B_wl], [B_wend])

        def ssd_prompt(T, NT, TB, xT, B_xT, xs_tok, B_xs, b_tok, B_bt, bT, B_bT, cT, B_cT, szb, B_szb, yaT, B_yaT, last, dtv):
            (v, B_v), (av, B_av), (e_, B_e), (l_, B_l), (dt_, B_dt), (dta, B_dta), (lndt, B_lndt), (nb, B_nb), \
                (eac, B_eac), (wl, B_wl), (wend, B_wend), (elast, B_elast) = dtv
            d2s = [R2.alloc([4, 128], F32) for _ in range(3)]
            dcs = [R2.alloc([4, 128], BF16) for _ in range(3)]
            mts = [R2.alloc([4, 128], BF16) for _ in range(3)]
            cbss = [R2.alloc([128], F32) for _ in range(3)]
            yts = [R2.alloc([256], F32) for _ in range(3)]
            yz, B_yz = R2.alloc([DI], F32)
            yns = [R2.alloc([DI], BF16) for _ in range(1)]
            xws = [R2.alloc([DI], BF16) for _ in range(2)]
            sq4s = [R2.alloc([16], F32) for _ in range(2)]
            o4b = [R2.alloc([512], F32) for _ in range(2)]
            pool_n = {"seg": 0, "cb": 0, "py": 0, "gen": 0}
            pool_banks = {"seg": [0, 1, 2], "cb": [3, 4], "py": [5, 6], "gen": [7]}

            def PSP(kind):
                lst = pool_banks[kind]
                i = lst[pool_n[kind] % len(lst)]
                pool_n[kind] += 1
                return psb[i], psbuf[i]

            items = []
            for t in range(NT):
                for g in range(NG):
                    it = Ctx()
                    it.t, it.g, it.i = t, g, t * NG + g
                    it.tsl = slice(t * T, (t + 1) * T)
                    items.append(it)

            def sA(it):
                t, g = it.t, it.g
                if g == 0:
                    xw, B_xw = xws[t % 2]
                    tt("pool", xw.rearrange("p (a b) -> p a b", a=NH), xs_tok[:, t, :].rearrange("p (a b) -> p a b", a=NH),
                       wend[:, t, :].unsqueeze(2).to_broadcast([128, NH, 64]), ALU.mult, [B_xs, B_wend], [B_xw])
                it.d2, it.B_d2 = d2s[it.i % 3]
                tt("pool", it.d2, trif.unsqueeze(1).to_broadcast([128, 4, 128]),
                   dta[:, t, g * 4:(g + 1) * 4].unsqueeze(2).to_broadcast([128, 4, 128]), ALU.mult, [B_const, B_dta], [it.B_d2])

            def sB(it):
                t, g = it.t, it.g
                it.pseg, it.B_pseg = PSP("seg")
                mmgroup([(it.pseg[:], identb[:], neg4b[:].rearrange("p a b -> p (a b)"), True, False),
                         (it.pseg[:], onesf, it.d2.rearrange("p a b -> p (a b)"), False, True)], [B_const, it.B_d2], [it.B_pseg])
                it.pcb, it.B_pcb = PSP("cb")
                mmgroup([(it.pcb[:, 0:128], bT[:, g, it.tsl], cT[:, g, it.tsl], True, True)], [B_bT, B_cT], [it.B_pcb])

            def sC(it):
                t, g = it.t, it.g
                it.dc, it.B_dc = dcs[it.i % 3]
                it.cbs, it.B_cbs = cbss[it.i % 3]
                act(it.cbs, it.pcb[:, 0:128], AF.Copy, [it.B_pcb], [it.B_cbs])
                psv = it.pseg[:].rearrange("p (a b) -> p a b", a=4)
                for hh in range(4):
                    act(it.dc[:, hh, :], psv[:, hh, :], AF.Exp, [it.B_pseg, B_nb] if hh in (0, 3) else [], [it.B_dc] if hh in (0, 3) else [],
                        bias=nb[:, t, g * 4 + hh:g * 4 + hh + 1])

            def sD(it):
                it.mt, it.B_mt = mts[it.i % 3]
                tt("dve", it.mt, it.dc, it.cbs.unsqueeze(1).to_broadcast([128, 4, 128]), ALU.mult, [it.B_dc, it.B_cbs], [it.B_mt])

            def sE(it):
                t, g = it.t, it.g
                it.pyy, it.B_pyy = PSP("py")
                pairs = []
                for hh in range(4):
                    hd = g * 4 + hh
                    col = hh * 64
                    pairs.append((it.pyy[:, col:col + 64], it.mt[:, hh, :], xs_tok[:, t, hd * 64:(hd + 1) * 64], hh == 0, False))
                    pairs.append((it.pyy[:, col:col + 64], diagD[:, hd, :], xs_tok[:, t, hd * 64:(hd + 1) * 64], False, False))
                pairs.append((it.pyy[:, 256:512], cT[:, g, it.tsl], hTb[:, g * 256:(g + 1) * 256], False, True))
                mmgroup(pairs, [it.B_mt, B_xs, B_const, B_cT, B_hTb], [it.B_pyy])
                qq = g // 2
                hs = hT[:, qq * 512:(qq + 1) * 512]
                if g % 2 == 0:
                    tt("pool", hs.rearrange("p (a b) -> p a b", a=8), hs.rearrange("p (a b) -> p a b", a=8),
                       elast[:, t, qq * 8:(qq + 1) * 8].unsqueeze(2).to_broadcast([128, 8, 64]), ALU.mult, [B_hT, B_elast], [B_hT])
                else:
                    xw, B_xw = xws[t % 2]
                    pu, B_pu = PSP("gen")
                    mmgroup([(pu[:, gg * 256:(gg + 1) * 256], b_tok[:, t, (2 * qq + gg) * 128:(2 * qq + gg + 1) * 128],
                              xw[:, (2 * qq + gg) * 256:(2 * qq + gg + 1) * 256], gg == 0, gg == 1) for gg in range(2)],
                            [B_bt, B_xw], [B_pu])
                    tt("dve", hs, hs, pu[:], ALU.add, [B_hT, B_pu], [B_hT])
                    act(hTb[:, qq * 512:(qq + 1) * 512], hs, AF.Copy, [B_hT], [B_hTb])

            def sF(it):
                t, g = it.t, it.g
                yt, B_yt = yts[it.i % 3]
                sq4, B_sq4 = sq4s[t % 2]
                tt("dve", yt.rearrange("p (a b) -> p a b", a=4), it.pyy[:, 256:512].rearrange("p (a b) -> p a b", a=4),
                   eac[:, t, g * 4:(g + 1) * 4].unsqueeze(2).to_broadcast([128, 4, 64]), ALU.mult, [it.B_pyy, B_eac], [B_yt])
                tt("dve", yt, yt, it.pyy[:, 0:256], ALU.add, [B_yt, it.B_pyy], [B_yt])
                tt("dve", yz[:, g * 256:(g + 1) * 256], yt, szb[:, t, g * 256:(g + 1) * 256], ALU.mult, [B_yt, B_szb], [B_yz])
                act_sq('', yz[:, g * 256:(g + 1) * 256], [B_yz], [B_sq4], accum_out=sq4[:, g:g + 1])
                if g == NG - 1:
                    S.op("dve", lambda h, sq4=sq4: h.tensor_reduce(out=sq4[:, 8:9], in_=sq4[:, 0:8], axis=mybir.AxisListType.X, op=ALU.add),
                         [B_sq4], [B_sq4])
                    rstd_from(sq4[:, 8:9], sq4[:, 9:10], DI, [B_sq4], B_sq4)
                    it.yn, it.B_yn = yns[0]
                    act(it.yn, yz, AF.Copy, [B_yz, B_sq4], [it.B_yn], scale=sq4[:, 9:10])

            def sG(it):
                if it.g != NG - 1:
                    return
                for c0 in range(0, 16, 8):
                    pt, B_pt = PSP("gen")
                    ptb = pt[:].bitcast(BF16).rearrange("p (c t) -> p c t", c=8)
                    trgroup([(ptb[:, c, :], it.yn[:, (c0 + c) * 128:(c0 + c + 1) * 128]) for c in range(8)], identb[:], [it.B_yn, B_const], [B_pt])
                    for c in range(8):
                        act(yaT[:, c0 + c, it.tsl], ptb[:, c, :], AF.Copy, [B_pt, B_const] if c in (0, 7) else [], [B_yaT] if c in (0, 7) else [],
                            scale=gssm_c[:, c0 + c:c0 + c + 1])

            pipeline(items, [sA, sB, sC, sD, sE, sF, None, sG], lag=2)
            if last:
                for c0 in range(0, 16, 4):
                    pt, B_pt = PS()
                    trgroup([(pt[:, c * 128:(c + 1) * 128], hT[:, (c0 + c) * 128:(c0 + c + 1) * 128]) for c in range(4)], identf, [B_hT, B_const], [B_pt])
                    o4, B_o4 = o4b[(c0 // 4) % 2]
                    cp("dve", o4, pt[:], [B_pt], [B_o4])
                    S.dma("sp", nsp[c0 * 128:(c0 + 4) * 128, :].rearrange("(c p) n -> p c n", p=128), o4.rearrange("p (a b) -> p a b", a=4),
                          reads=[B_o4], writes=[Buf()])

        def ssd_sample(T, xT, B_xT, xs_tok, B_xs, b_tok, B_bt, bT, B_bT, cT, B_cT, szb, B_szb, yaT, B_yaT):
            sm_, B_sm = R2.alloc([8, NH], F32, parts=T)
            v, l_, dt_, dta, da, av, e_, _u = [sm_[:, i, :] for i in range(8)]
            pa, B_pa = PS()
            mmgroup([(pa[0:T, 0:32], xT[:, k, :], wdt[:, k, :], k == 0, k == 7) for k in range(8)], [B_xT, B_const], [B_pa])
            tt("dve", v, pa[0:T, 0:32], dtb_bc[0:T, :], ALU.add, [B_pa, B_const], [B_sm])
            act(av, v, AF.Abs, [B_sm], [B_sm])
            act(e_, av, AF.Exp, [B_sm], [B_sm], scale=-1.0)
            act(l_, e_, AF.Ln, [B_sm], [B_sm], bias=1.0)
            stt("dve", dt_, v, 0.0, l_, ALU.max, ALU.add, [B_sm], [B_sm])
            tt("dve", dta, dt_, abc[0:T, :], ALU.mult, [B_sm, B_const], [B_sm])
            act(da, dta, AF.Exp, [B_sm], [B_sm])
            rep, B_rep = R2.alloc([DI], F32, parts=T)
            repv = rep.rearrange("b (p c) -> b c p", c=16)
            cols, B_cols = R2.alloc([3, 16, T], F32)
            for j, srcv in enumerate((da, dt_, dsk_bc[0:T, :])):
                cp("dve", rep.rearrange("p (a b) -> p a b", a=NH), srcv.unsqueeze(2).to_broadcast([T, NH, 64]), [B_sm, B_const], [B_rep])
                pt, B_pt = PS()
                ptv = pt[:, 0:16 * T].rearrange("p (c t) -> p c t", c=16)
                trgroup([(ptv[:, c, :], repv[:, c, :]) for c in range(16)], identf[0:T, 0:T], [B_rep, B_const], [B_pt])
                cp("dve", cols[:, j, :, :], ptv, [B_pt], [B_cols])
            xcol, B_xcol = R2.alloc([16, T], F32)
            xsv = xs_tok[:, 0, :].rearrange("b (p c) -> b c p", c=16)
            pt, B_pt = PS()
            ptb = pt[:].bitcast(BF16)[:, 0:16 * T].rearrange("p (c t) -> p c t", c=16)
            trgroup([(ptb[:, c, :], xsv[:, c, :]) for c in range(16)], identb[0:T, 0:T], [B_xs, B_const], [B_pt])
            cp("dve", xcol, ptb, [B_pt], [B_xcol])
            dtx, B_dtx = R2.alloc([16, T], F32)
            tt("dve", dtx, xcol, cols[:, 1, :, :], ALU.mult, [B_xcol, B_cols], [B_dtx])
            bc2, B_bc2 = R2.alloc([2, 128], BF16)
            pt, B_pt = PS()
            ptb = pt[:].bitcast(BF16)[:, 0:256].rearrange("p (a n) -> p a n", a=2)
            trgroup([(ptb[:, 0, :], bT.rearrange("n g b -> n (g b)")), (ptb[:, 1, :], cT.rearrange("n g b -> n (g b)"))], identb[:],
                    [B_bT, B_cT, B_const], [B_pt])
            cp("dve", bc2, ptb, [B_pt], [B_bc2])
            self32, B_self32 = R2.alloc([T, 128], F32)
            sel, B_sel = R2.alloc([T, 128], BF16)
            S.dma("sp", self32.rearrange("p a b -> p (a b)"), selc[:, :], reads=[B_in], writes=[B_self32])
            cp("dve", sel, self32, [B_self32], [B_sel])
            ycol, B_ycol = R2.alloc([16, T], F32)
            junkfs = [R2.alloc([128], F32) for _ in range(3)]
            hbuf = [R1.alloc([16, DS], F32) for _ in range(4)]
            def load_state(bq):
                hbq, B_hbq = hbuf[bq % 4]
                S.dma("sp", hbq, sssm[bq].rearrange("(p c) n -> p c n", c=16), reads=[B_in], writes=[B_hbq])

            for bq in range(3):
                load_state(bq)
            for b in range(T):
                hb, B_hb = hbuf[b % 4]
                pB, B_pB = PS()
                pC, B_pC = PS()
                mmgroup([(pB[:, 0:128], sel[:, b, :], bc2[:, 0, :], True, True)], [B_sel, B_bc2], [B_pB])
                mmgroup([(pC[:, 0:128], sel[:, b, :], bc2[:, 1, :], True, True)], [B_sel, B_bc2], [B_pC])
                for c in range(16):
                    act(hb[:, c, :], hb[:, c, :], AF.Copy, [B_hb, B_cols] if c in (0, 15) else [],
                        [B_hb] if c in (0, 15) else [], scale=cols[:, 0, c, b:b + 1])
                for c in range(16):
                    S.op("dve", lambda h, c=c, b=b, pB=pB, hb=hb: h.scalar_tensor_tensor(
                        out=hb[:, c, :], in0=pB[:, 0:128], scalar=dtx[:, c, b:b + 1], in1=hb[:, c, :],
                        op0=ALU.mult, op1=ALU.add), [B_hb, B_pB, B_dtx] if c in (0, 15) else [],
                        [B_hb] if c in (0, 15) else [])
                S.dma("sp", nss[b].rearrange("(p c) n -> p c n", c=16), hb, reads=[B_hb], writes=[Buf()])
                if b + 3 < T:
                    load_state(b + 3)
                for c in range(16):
                    jf, B_jf = junkfs[c % 3]
                    S.op("dve", lambda h, c=c, b=b, pC=pC, hb=hb, jf=jf: h.scalar_tensor_tensor(
                        out=jf[:, 0:128], in0=hb[:, c, :], scalar=1.0, in1=pC[:, 0:128],
                        op0=ALU.mult, op1=ALU.mult, accum_out=ycol[:, c, b:b + 1]), [B_hb, B_pC] if c in (0, 15) else [],
                        ([B_ycol] if c in (0, 15) else []) + [B_jf])
            dcol, B_dcol = R2.alloc([16, T], F32)
            tt("dve", dcol, cols[:, 2, :, :], xcol, ALU.mult, [B_cols, B_xcol], [B_dcol])
            tt("dve", ycol, ycol, dcol, ALU.add, [B_ycol, B_dcol], [B_ycol])
            ytok, B_ytok = R2.alloc([DI], F32, parts=T)
            ytv = ytok.rearrange("b (p c) -> b c p", c=16)
            szv = szb[:, 0, :].rearrange("b (p c) -> b c p", c=16)
            for c0 in range(0, 16, 4):
                pt, B_pt = PS()
                trgroup([(pt[0:T, c * 128:(c + 1) * 128], ycol[:, c0 + c, :]) for c in range(4)], identf, [B_ycol, B_const], [B_pt])
                tt("dve", ytv[:, c0:c0 + 4, :], pt[0:T, :].rearrange("b (c p) -> b c p", c=4), szv[:, c0:c0 + 4, :], ALU.mult,
                   [B_pt, B_szb], [B_ytok])
            sq, B_sq = R2.alloc([4], F32, parts=T)
            act_sq('', ytok[:, 0:1024], [B_ytok], [B_sq], accum_out=sq[:, 0:1])
            act_sq('', ytok[:, 1024:2048], [B_ytok], [B_sq], accum_out=sq[:, 2:3])
            tt("pool", sq[:, 0:1], sq[:, 0:1], sq[:, 2:3], ALU.add, [B_sq], [B_sq])
            rstd_from(sq[:, 0:1], sq[:, 1:2], DI, [B_sq], B_sq)
            yn, B_yn = R2.alloc([DI], BF16, parts=T)
            act(yn, ytok, AF.Copy, [B_ytok, B_sq], [B_yn], scale=sq[:, 1:2])
            pt, B_pt = PS()
            ptb = pt[:].bitcast(BF16)[:, 0:16 * T].rearrange("p (c t) -> p c t", c=16)
            trgroup([(ptb[:, c, :], yn[:, c * 128:(c + 1) * 128]) for c in range(16)], identb[0:T, 0:T], [B_yn, B_const], [B_pt])
            tt("dve", yaT, ptb, gssm_c.unsqueeze(2).to_broadcast([128, 16, T]), ALU.mult, [B_pt, B_const], [B_yaT])

        try:
            checkpoint()
            for blk in range(SEQ // 512):
                run_block("prompt", blk)
            run_block("sample", 0)
        except _Stop:
            pass
        S.finish()
    return nc


def _consts():
    c = np.zeros((128, NCST), np.float32)
    k = np.arange(128)[:, None]
    i = np.arange(128)[None, :]
    c[:, CS_ID:CS_ID + 128] = (k == i)
    c[:, CS_TRI:CS_TRI + 128] = (k <= i)
    c[:, CS_NEG:CS_NEG + 128] = np.where(i < k, NEG, 0.0)
    c[:, CS_ONE:CS_ONE + 128] = 1.0
    inv = np.zeros((8, 16), np.float32)
    for ch in range(8):
        w = POOLW[ch // 2]
        inv[ch] = 1.0 / np.minimum(np.arange(16) + 1, w)
    c[:, CS_INV:CS_INV + 128] = inv.reshape(1, 128)
    return c


_NC_CACHE = {}


def kernel(x_prompt, x_sample, state_conv, state_ssm, state_pool,
           norm_mix_pre, norm_mix_post, norm_ffn_pre, norm_ffn_post,
           w_in, conv_w, conv_b, dt_bias, a_log, d_skip, ssm_norm,
           w_pool_group, pool_scale, w_branch_a, w_branch_b, w_out,
           w_ffn_in, w_ffn_out):
    f = lambda a: np.ascontiguousarray(np.asarray(a, dtype=np.float32))
    fm = lambda v: f(v).reshape(-1, 128).T
    colp = np.zeros((128, NCOL), np.float32)
    colp[:, CP_GPRE:CP_GPRE + 8] = fm(norm_mix_pre[0])
    colp[:, CP_GFFN:CP_GFFN + 8] = fm(norm_ffn_pre[0])
    colp[:, CP_PS:CP_PS + 8] = fm(pool_scale[0])
    colp[:, CP_GSSM:CP_GSSM + 16] = fm(ssm_norm[0])
    cw = f(conv_w[0]).reshape(4, 32, 128).transpose(2, 1, 0)
    colp[:, CP_CW:CP_CW + 128] = cw.reshape(128, 128)
    colp[:, CP_CB:CP_CB + 32] = fm(conv_b[0])
    rowp = np.concatenate([f(norm_mix_post[0]), f(norm_ffn_post[0]), f(dt_bias[0]), f(a_log[0]), f(d_skip[0])])[None, :]
    rowp = f(rowp)
    cst = _consts()
    kk = np.arange(128)[:, None, None]
    bb = np.arange(NSB)[None, :, None]
    mm_ = np.arange(128)[None, None, :]
    selc = (((kk % NSB) == bb) & ((kk // NSB) == (mm_ // 16))).astype(np.float32).reshape(128, NSB * 128)
    shared = {
        "w_in": f(w_in[0]), "w_pg": f(w_pool_group[0]).reshape(D, 256), "w_a": f(w_branch_a[0]), "w_b": f(w_branch_b[0]),
        "w_o": f(w_out[0]), "w_fi": f(w_ffn_in[0]), "w_fo": f(w_ffn_out[0]), "colp": colp, "rowp": rowp, "cst": cst, "selc": selc,
    }
    xpr = f(x_prompt)
    xsr = f(x_sample).reshape(128, D)
    sc = f(state_conv[0])
    ss = f(state_ssm[0]).reshape(128, DI, DS)
    sp = f(state_pool[0])
    in_maps = []
    for c in range(NCORES):
        m = dict(shared)
        m["xp"] = xpr[c]
        m["xs"] = xsr[c * NSB:(c + 1) * NSB]
        m["sconv"] = sc[c * NSB:(c + 1) * NSB]
        m["sssm"] = ss[c * NSB:(c + 1) * NSB]
        m["spool"] = sp[c * NSB:(c + 1) * NSB]
        in_maps.append(m)
    if "nc" not in _NC_CACHE:
        _NC_CACHE["nc"] = build_program()
    nc = _NC_CACHE["nc"]
    res = run_bass_kernel_spmd(nc, in_maps, core_ids=list(range(NCORES)))
    R = res.results
    y_prompt = np.stack([R[c]["yp"] for c in range(NCORES)])
    y_sample = np.concatenate([R[c]["ys"] for c in range(NCORES)]).reshape(128, 1, D)
    ncp_ = np.stack([R[c]["ncp"] for c in range(NCORES)])[None]
    nsp_ = np.stack([R[c]["nsp"].reshape(NH, HD, DS) for c in range(NCORES)])[None]
    npp_ = np.stack([R[c]["npp"] for c in range(NCORES)])[None]
    ncs_ = np.concatenate([R[c]["ncs"] for c in range(NCORES)])[None]
    nss_ = np.concatenate([R[c]["nss"] for c in range(NCORES)]).reshape(1, 128, NH, HD, DS)
    nps_ = np.concatenate([R[c]["nps"] for c in range(NCORES)])[None]
    return (y_prompt.astype(np.float32), y_sample.astype(np.float32), ncp_.astype(np.float32), nsp_.astype(np.float32),
            npp_.astype(np.float32), ncs_.astype(np.float32), nss_.astype(np.float32), nps_.astype(np.float32))
```

```python
import numpy as np
from contextlib import ExitStack
import concourse.bass as bass
import concourse.mybir as mybir
from concourse.bass_utils import run_bass_kernel_spmd

F32 = mybir.dt.float32
BF16 = mybir.dt.bfloat16
AF = mybir.ActivationFunctionType
ALU = mybir.AluOpType

NCORES = 8
D = 1024
DI = 2048
NH = 32
HD = 64
NG = 8
DS = 128
CD = 4096
DFF = 2816
SEQ = 2048
NSB = 16
IN_DIM = 9248
C_Z, C_XBC, C_DT, C_POOL, C_GA, C_GB = 0, 2048, 6144, 6176, 7200, 8224
EPS = 1e-6
NEG = -30000.0
POOLW = (2, 4, 8, 16)

CP_GPRE, CP_GFFN, CP_PS, CP_GSSM, CP_CW, CP_CB = 0, 8, 16, 24, 40, 168
NCOL = 200
RP_GPOST, RP_GFPOST, RP_DTB, RP_ALOG, RP_DSK = 0, 1024, 2048, 2080, 2112
NROW = 2144
CS_ID, CS_TRI, CS_NEG, CS_ONE, CS_INV = 0, 128, 256, 384, 512
NCST = 512 + 128


class Buf:
    __slots__ = ("name", "w", "r")

    def __init__(self, name=""):
        self.name = name
        self.w = None
        self.r = {}


class Eng:
    def __init__(self, name, sem):
        self.name = name
        self.sem = sem
        self.n = 0
        self.seen = {}
        self.prog = []
        self.hist = {}


class Sched:
    def __init__(self, nc, stack, n_dma_sems=40):
        self.nc = nc
        self.stack = stack
        self.fence_toks = []
        self.n_once = 0
        self.E = {}
        for nm in ("pe", "act", "dve", "pool", "sp"):
            self.E[nm] = Eng(nm, stack.enter_context(nc.semaphore("s_" + nm)))
        self.dsems = [stack.enter_context(nc.semaphore("d%d" % i)) for i in range(n_dma_sems)]
        self.dval = [0] * n_dma_sems
        self.dhist = {}
        self.dnext = 0
        self.count = 0
        self.budget = None
        self.skip = set()

    def _wait(self, eng, tok):
        key, val, sem = tok
        if eng.seen.get(key, 0) >= val:
            return
        eng.prog.append(("wait", sem, val))
        eng.seen[key] = val
        h = self.dhist.get((key, val)) if isinstance(key, tuple) else self.E[key].hist.get(val)
        if h:
            for k, v in h.items():
                if eng.seen.get(k, 0) < v:
                    eng.seen[k] = v

    def _deps(self, eng, reads, writes):
        toks = []
        for b in reads:
            if b.w is not None:
                toks.append(b.w)
        for b in writes:
            if b.w is not None:
                toks.append(b.w)
            toks.extend(b.r.values())
        for t in toks:
            if eng.name == "pe" and t[0] == "pe":
                continue
            self._wait(eng, t)

    def _update(self, tok, reads, writes):
        for b in reads:
            o = b.r.get(tok[0])
            if o is None or o[1] < tok[1]:
                b.r[tok[0]] = tok
        for b in writes:
            b.w = tok
            b.r = {}

    def op(self, en, fn, reads=(), writes=()):
        self.count += 1
        if (self.budget is not None and self.count > self.budget) or self.count in self.skip:
            return None
        eng = self.E[en]
        self._deps(eng, reads, writes)
        eng.n += 1
        eng.prog.append(("op", fn, (eng.sem, 1)))
        eng.hist[eng.n] = dict(eng.seen)
        tok = (en, eng.n, eng.sem)
        self._update(tok, reads, writes)
        return tok

    def dma(self, en, out, in_, reads=(), writes=(), wait_toks=(), fence=True, once=False, **kw):
        self.count += 1
        if self.budget is not None and self.count > self.budget:
            return None
        eng = self.E[en]
        for wt_ in wait_toks:
            if wt_ is not None:
                self._wait(eng, wt_)
        if once:
            self.n_once += 1
            sem = self.stack.enter_context(self.nc.semaphore("o%d" % self.n_once))
            key = ("o", self.n_once)
            self._deps(eng, reads, writes)
            val = 16
        else:
            i = self.dnext
            self.dnext = (self.dnext + 1) % len(self.dsems)
            key = ("d", i)
            sem = self.dsems[i]
            if self.dval[i] > 0:
                self._wait(eng, (key, self.dval[i], sem))
            self._deps(eng, reads, writes)
            self.dval[i] += 16
            val = self.dval[i]

        def fn(h, out=out, in_=in_, kw=kw):
            return h.dma_start(out=out, in_=in_, **kw)

        eng.prog.append(("op", fn, (sem, 16)))
        self.dhist[(key, val)] = dict(eng.seen)
        tok = (key, val, sem)
        self._update(tok, reads, writes)
        if fence:
            self.fence_toks.append(tok)
        else:
            self.loose_toks = getattr(self, "loose_toks", {})
            self.loose_toks[key] = tok
        return tok

    def barrier(self, fence_buf, fence_fn):
        return

    def finish(self):
        sp = self.E["sp"]
        for i, s in enumerate(self.dsems):
            if self.dval[i] > 0:
                self._wait(sp, (("d", i), self.dval[i], s))
        for tok in getattr(self, "loose_toks", {}).values():
            self._wait(sp, tok)
        for nm, e in self.E.items():
            if nm != "sp" and e.n > 0:
                self._wait(sp, (nm, e.n, e.sem))
        nc = self.nc
        with nc.Block() as block:
            def replay(eng):
                def run(h):
                    for item in eng.prog:
                        if item[0] == "wait":
                            h.wait_ge(item[1], item[2])
                        else:
                            ins = item[1](h)
                            ins.then_inc(item[2][0], item[2][1])
                return run
            block.tensor(replay(self.E["pe"]))
            block.scalar(replay(self.E["act"]))
            block.vector(replay(self.E["dve"]))
            block.gpsimd(replay(self.E["pool"]))
            block.sync(replay(self.E["sp"]))


class Arena:
    def __init__(self, t, nbytes):
        self.t = t
        self.nbytes = nbytes
        self.off = 0
        self.hist = []

    def reset(self, to=0):
        self.off = to

    def alloc(self, shape, dt, parts=128, at=None):
        esz = 4 if dt == F32 else 2
        n = 1
        for s in shape:
            n *= s
        nb = n * esz
        nb_al = (nb + 63) // 64 * 64
        base = self.off if at is None else at
        assert base + nb_al <= self.nbytes, ("arena overflow", base, nb_al, self.nbytes)
        lo, hi = base, base + nb_al
        a = self.t[0:parts, base // 2:(base + nb) // 2]
        if at is None:
            self.off += nb_al
        buf = Buf()
        keep = []
        for (o0, o1, ob) in self.hist:
            if o0 < hi and lo < o1:
                toks = list(ob.r.values())
                if ob.w is not None:
                    toks.append(ob.w)
                for tok in toks:
                    o = buf.r.get(tok[0])
                    if o is None or o[1] < tok[1]:
                        buf.r[tok[0]] = tok
                if lo <= o0 and o1 <= hi:
                    continue
            keep.append((o0, o1, ob))
        keep.append((lo, hi, buf))
        self.hist = keep
        if dt == F32:
            a = a.bitcast(F32)
        if len(shape) == 2:
            a = a.rearrange("p (a b) -> p a b", a=shape[0])
        elif len(shape) == 3:
            a = a.rearrange("p (a b c) -> p a b c", a=shape[0], b=shape[1])
        return a, buf


class _Stop(Exception):
    pass


S_ref = [None]


def build_program(stop=None, budget=None, verbose=False, skip=()):
    nc = bass.Bass("TRN2", target_bir_lowering=False)
    stage_ctr = [0]

    def checkpoint():
        stage_ctr[0] += 1
        if verbose:
            print("checkpoint", stage_ctr[0], "ops so far", S_ref[0].count)
        if stop is not None and stage_ctr[0] > stop:
            raise _Stop()

    def din(name, shape):
        return nc.dram_tensor(name, shape, F32, kind="ExternalInput").ap()

    def dout(name, shape):
        return nc.dram_tensor(name, shape, F32, kind="ExternalOutput").ap()

    xp = din("xp", [SEQ, D])
    xsm = din("xs", [NSB, D])
    sconv = din("sconv", [NSB, 3, CD])
    sssm = din("sssm", [NSB, DI, DS])
    spool = din("spool", [NSB, 15, D])
    w_in = din("w_in", [D, IN_DIM])
    w_pg = din("w_pg", [D, 256])
    w_a = din("w_a", [DI, D])
    w_b = din("w_b", [D, D])
    w_o = din("w_o", [D, D])
    w_fi = din("w_fi", [D, 2 * DFF])
    w_fo = din("w_fo", [DFF, D])
    colp = din("colp", [128, NCOL])
    rowp = din("rowp", [1, NROW])
    cst = din("cst", [128, NCST])
    selc = din("selc", [128, NSB * 128])

    yp = dout("yp", [SEQ, D])
    ysm = dout("ys", [NSB, D])
    ncp = dout("ncp", [3, CD])
    nsp = dout("nsp", [DI, DS])
    npp = dout("npp", [15, D])
    ncs = dout("ncs", [NSB, 3, CD])
    nss = dout("nss", [NSB, DI, DS])
    nps = dout("nps", [NSB, 15, D])

    def dscr(name, shape):
        return nc.dram_tensor(name, shape, BF16, kind="Internal").ap()

    sc_in = dscr("sc_in", [D, IN_DIM])
    sc_pg = dscr("sc_pg", [D, 256])
    sc_a = dscr("sc_a", [DI, D])
    sc_b = dscr("sc_b", [D, D])
    sc_o = dscr("sc_o", [D, D])
    sc_fi = dscr("sc_fi", [D, 2 * DFF])
    sc_fo = dscr("sc_fo", [DFF, D])

    with ExitStack() as st:
        S = Sched(nc, st)
        S.budget = budget
        S.skip = set(skip)
        S_ref[0] = S

        def sbuf(name, shape, dt):
            return st.enter_context(nc.sbuf_tensor(name, shape, dt))

        cstf = sbuf("cstf", [128, NCST], F32)
        colt = sbuf("colt", [128, NCOL], F32)
        rowbc = sbuf("rowbc", [128, NROW], F32)
        identb = sbuf("identb", [128, 128], BF16)
        neg4b = sbuf("neg4b", [128, 4, 128], BF16)
        diagD = sbuf("diagD", [128, NH, 128], BF16)
        abc = sbuf("abc", [128, NH], F32)
        mhalf = sbuf("mhalf", [128, 1], F32)
        wdt = sbuf("wdt", [128, 8, 32], BF16)
        wpg = sbuf("wpg", [128, 8, 256], BF16)
        hT = sbuf("hT", [128, DI], F32)
        hTb = sbuf("hTb", [128, DI], BF16)
        histc = sbuf("histc", [128, 32, 3], F32)
        histp = sbuf("histp", [128, 8, 15], F32)
        junk_t = sbuf("junk", [128, 3, 1024], BF16)
        junk_bufs = [Buf("junk%d" % i) for i in range(3)]
        junk_n = [0]

        def JK():
            i = junk_n[0] % 3
            junk_n[0] += 1
            return junk_t[:, i, :], junk_bufs[i]
        fence = sbuf("fence", [128, 1], F32)
        NSLOT = 4
        slots = [sbuf("wslot%d" % i, [128, 8, 512], BF16) for i in range(NSLOT)]
        slot_bufs = [Buf("slot%d" % i) for i in range(NSLOT)]
        R1B, R2B, R3B = 40 * 1024, 50 * 1024, 40 * 1024
        R1 = Arena(sbuf("R1", [128, R1B // 2], BF16), R1B)
        R2 = Arena(sbuf("R2", [128, R2B // 2], BF16), R2B)
        R3 = Arena(sbuf("R3", [128, R3B // 2], BF16), R3B)

        identf = cstf[:, CS_ID:CS_ID + 128]
        trif = cstf[:, CS_TRI:CS_TRI + 128]
        negf = cstf[:, CS_NEG:CS_NEG + 128]
        onesf = cstf[:, CS_ONE:CS_ONE + 128]
        invc = cstf[:, CS_INV:CS_INV + 128].rearrange("p (c k) -> p c k", c=8)
        gpost_bc = rowbc[:, RP_GPOST:RP_GPOST + D]
        gfpost_bc = rowbc[:, RP_GFPOST:RP_GFPOST + D]
        dtb_bc = rowbc[:, RP_DTB:RP_DTB + NH]
        alog_bc = rowbc[:, RP_ALOG:RP_ALOG + NH]
        dsk_bc = rowbc[:, RP_DSK:RP_DSK + NH]

        B_const = Buf("const")
        B_wres, B_wres2 = Buf("wdt"), Buf("wpg")
        B_hT, B_hTb, B_histc, B_histp, B_fence = Buf(), Buf(), Buf(), Buf(), Buf()
        B_in = Buf("inputs")

        psb = [st.enter_context(nc.psum_tensor("ps%d" % i, [128, 512], F32)) for i in range(8)]
        psbuf = [Buf("ps%d" % i) for i in range(8)]
        psn = [0]

        def PS():
            i = psn[0] % 8
            psn[0] += 1
            return psb[i], psbuf[i]

        def act(out, in_, func, reads, writes, **kw):
            return S.op("act", lambda h: h.activation(out=out, in_=in_, func=func, **kw), reads, writes)

        def act_sq(sl, in_, reads, writes, accum_out):
            jk, B_jk = JK()
            parts = in_.shape[0]
            n = 1
            for d_ in in_.shape[1:]:
                n *= d_
            o = jk[0:parts, 0:n]
            if len(in_.shape) == 3:
                o = o.rearrange("p (a b) -> p a b", a=in_.shape[1])
            return act(o, in_, AF.Square, list(reads), list(writes) + [B_jk], accum_out=accum_out)

        def tt(en, out, in0, in1, op, reads, writes):
            return S.op(en, lambda h: h.tensor_tensor(out=out, in0=in0, in1=in1, op=op), reads, writes)

        def ts(en, out, in0, s1, s2, op0, op1, reads, writes):
            if s2 is None:
                return S.op(en, lambda h: h.tensor_scalar(out=out, in0=in0, scalar1=s1, scalar2=None, op0=op0), reads, writes)
            return S.op(en, lambda h: h.tensor_scalar(out=out, in0=in0, scalar1=s1, scalar2=s2, op0=op0, op1=op1), reads, writes)

        def stt(en, out, in0, sc, in1, op0, op1, reads, writes):
            return S.op(en, lambda h: h.scalar_tensor_tensor(out=out, in0=in0, scalar=sc, in1=in1, op0=op0, op1=op1), reads, writes)

        def cp(en, out, in_, reads, writes):
            return S.op(en, lambda h: h.tensor_copy(out=out, in_=in_), reads, writes)

        def mmgroup(pairs, reads, writes):
            def fn(h):
                ins = None
                for (o, l, r, s0, s1) in pairs:
                    ins = h.matmul(o, lhsT=l, rhs=r, start=s0, stop=s1, skip_group_check=True)
                return ins
            return S.op("pe", fn, reads, writes)

        def trgroup(items, ident, reads, writes):
            def fn(h):
                ins = None
                for (o, i_) in items:
                    ins = h.transpose(out=o, in_=i_, identity=ident)
                return ins
            return S.op("pe", fn, reads, writes)

        def rstd_from(ssq_ap, rstd_ap, n, bufs_r, buf_w):
            ts("pool", rstd_ap, ssq_ap, 1.0 / n, EPS, ALU.mult, ALU.add, bufs_r, [buf_w])
            tt("pool", rstd_ap, rstd_ap, mhalf[0:rstd_ap.shape[0], :], ALU.pow, [buf_w, B_const], [buf_w])

        def do_barrier():
            S.barrier(B_fence, lambda h: h.memset(fence[:], 0.0))
            R1.reset()
            R2.reset()
            R3.reset()

        S.dma("sp", cstf[:], cst[:, :], writes=[B_const])
        S.dma("sp", colt[:], colp[:, :], writes=[B_const])
        S.dma("sp", rowbc[:], rowp.partition_broadcast(128).rearrange("p a n -> p (a n)"), writes=[B_const])
        cp("dve", identb[:], identf, [B_const], [B_const])
        cp("dve", neg4b[:], negf.unsqueeze(1).to_broadcast([128, 4, 128]), [B_const], [B_const])
        S.op("pool", lambda h: h.memset(mhalf[:], -0.5), writes=[B_const])
        act(abc[:], alog_bc, AF.Exp, [B_const], [B_const])
        ts("dve", abc[:], abc[:], -1.0, None, ALU.mult, None, [B_const], [B_const])
        for hh in range(NH):
            ts("dve", diagD[:, hh, :], identf, dsk_bc[:, hh:hh + 1], None, ALU.mult, None, [B_const], [B_const])
        S.op("dve", lambda h: h.memset(hT[:], 0.0), writes=[B_hT])
        S.op("pool", lambda h: h.memset(hTb[:], 0.0), writes=[B_hTb])
        S.op("dve", lambda h: h.memset(histc[:], 0.0), writes=[B_histc])
        S.op("pool", lambda h: h.memset(histp[:], 0.0), writes=[B_histp])

        class WB:
            pass

        scr_uid = [0]

        def mk_blocks(src, scr, K, c0, c1, cw=512):
            out = []
            nkc = K // 128
            for k0 in range(0, nkc, 8):
                nk = min(8, nkc - k0)
                row = []
                for cc in range(c0, c1, cw):
                    b = WB()
                    b.src, b.scr = src, scr
                    b.r0, b.r1 = k0 * 128, (k0 + nk) * 128
                    b.c0, b.c1 = cc, min(cc + cw, c1)
                    b.nk, b.nc = nk, b.c1 - b.c0
                    b.buf = Buf()
                    scr_uid[0] += 1
                    b.scr_t = nc.dram_tensor("wt%d" % scr_uid[0], [128, b.nk, b.nc], BF16, kind="Internal").ap()
                    row.append(b)
                out.append(row)
            return out

        W_xbc = mk_blocks(w_in, sc_in, D, C_XBC, C_DT)[0]
        W_z = mk_blocks(w_in, sc_in, D, C_Z, C_XBC)[0]
        W_dt = mk_blocks(w_in, sc_in, D, C_DT, C_POOL)[0]
        W_pool = mk_blocks(w_in, sc_in, D, C_POOL, C_GA)[0]
        W_ga = mk_blocks(w_in, sc_in, D, C_GA, C_GB)[0]
        W_gb = mk_blocks(w_in, sc_in, D, C_GB, IN_DIM)[0]
        W_pg = mk_blocks(w_pg, sc_pg, D, 0, 256)[0]
        W_a = mk_blocks(w_a, sc_a, DI, 0, D)
        W_b = mk_blocks(w_b, sc_b, D, 0, D)[0]
        W_o = mk_blocks(w_o, sc_o, D, 0, D)[0]
        W_fg = mk_blocks(w_fi, sc_fi, D, 0, DFF)[0]
        W_fu = mk_blocks(w_fi, sc_fi, D, DFF, 2 * DFF)[0]
        W_fo = mk_blocks(w_fo, sc_fo, DFF, 0, D)

        cast_order = W_dt + W_pg + W_xbc + W_z + W_pool + W_ga + W_gb
        for cbk in range(2):
            cast_order += [W_a[0][cbk], W_a[1][cbk], W_b[cbk]]
        cast_order += W_o
        for i in range(len(W_fg)):
            cast_order += [W_fg[i], W_fu[i]]
        for cbk in range(2):
            cast_order += [W_fo[0][cbk], W_fo[1][cbk], W_fo[2][cbk]]
        for i, b in enumerate(cast_order):
            b.cast_idx = i
        cast_toks = []

        def ensure_cast(n):
            n = min(n, len(cast_order))
            while len(cast_toks) < n:
                b = cast_order[len(cast_toks)]
                prev = cast_toks[-8] if len(cast_toks) >= 8 else None
                cast_toks.append(S.dma("pool", b.scr_t[:, :, :], b.src[b.r0:b.r1, b.c0:b.c1].rearrange("(k p) f -> p k f", p=128),
                                       reads=[B_in], writes=[b.buf],
                                       wait_toks=[prev], fence=False, once=True))

        ensure_cast(6)
        if stop is not None and stop <= 0:
            S.finish()
            return nc

        slot_n = [0]

        w_prefetch = []

        def prefetch_w(b):
            sl, sb_ = get_w(b, _direct=True)
            w_prefetch.append((b, sl, sb_))

        def get_w(b, _direct=False):
            if not _direct and w_prefetch:
                pb, sl, sb_ = w_prefetch.pop(0)
                assert pb is b, "weight prefetch order mismatch"
                return sl, sb_
            ensure_cast(b.cast_idx + 10)
            i = slot_n[0] % NSLOT
            slot_n[0] += 1
            sl, sb_ = slots[i], slot_bufs[i]
            S.dma("sp", sl[:, 0:b.nk, 0:b.nc], b.scr_t[:, :, :],
                  reads=[b.buf], writes=[sb_], fence=False, max_dma_last_dim=2048)
            return sl, sb_

        S.dma("sp", wdt[:], W_dt[0].scr_t[:, :, :], reads=[W_dt[0].buf], writes=[B_wres])
        S.dma("sp", wpg[:], W_pg[0].scr_t[:, :, :], reads=[W_pg[0].buf], writes=[B_wres2])

        gpre_c = colt[:, CP_GPRE:CP_GPRE + 8]
        gffn_c = colt[:, CP_GFFN:CP_GFFN + 8]
        ps_c = colt[:, CP_PS:CP_PS + 8]
        gssm_c = colt[:, CP_GSSM:CP_GSSM + 16]
        cw_c = colt[:, CP_CW:CP_CW + 128].rearrange("p (c k) -> p c k", c=32)
        cb_c = colt[:, CP_CB:CP_CB + 32]

        def pipeline(items, stages, lag=1):
            n, ns = len(items), len(stages)
            for step in range(n + (ns - 1) * lag):
                for k in range(ns - 1, -1, -1):
                    i = step - k * lag
                    if 0 <= i < n and stages[k] is not None:
                        stages[k](items[i])

        class Ctx:
            pass

        pre_xT = [None]
        A_R2_BASE = 25 * 1024 + 512
        A_R3_XT = 32 * 1024

        def stageA_begin(mode, blk):
            prompt = mode == "prompt"
            T = 128 if prompt else NSB
            NT = 4 if prompt else 1
            TB = T * NT
            tok0 = blk * TB
            x_src = xp if prompt else xsm
            c = Ctx()
            c.T, c.NT = T, NT
            c.xT, c.B_xT = R3.alloc([8, TB], BF16, at=A_R3_XT)
            off = A_R2_BASE
            a_xt, a_xn, a_sq = [], [], []
            for i in range(NT):
                a_xt.append(R2.alloc([D], F32, parts=T, at=off))
                off += 4096
            for i in range(NT):
                a_xn.append(R2.alloc([D], BF16, parts=T, at=off))
                off += 2048
            for i in range(NT):
                a_sq.append(R2.alloc([2], F32, parts=T, at=off))
                off += 64
            c.xn = a_xn
            for t in range(NT):
                xt, B_xt = a_xt[t]
                xn, B_xn = a_xn[t]
                sq, B_sq = a_sq[t]
                S.dma("sp", xt, x_src[tok0 + t * T: tok0 + (t + 1) * T, :], reads=[B_in], writes=[B_xt])
                act_sq('', xt, [B_xt], [B_sq], accum_out=sq[:, 0:1])
                rstd_from(sq[:, 0:1], sq[:, 1:2], D, [B_sq], B_sq)
                act(xn, xt, AF.Copy, [B_xt, B_sq], [B_xn], scale=sq[:, 1:2])
            return c

        def stageA_finish(c):
            T, NT = c.T, c.NT
            for t in range(NT):
                xn, B_xn = c.xn[t]
                pt, B_pt = PS()
                ptb = pt[:].bitcast(BF16).rearrange("p (c t) -> p c t", c=8)
                trgroup([(ptb[:, cc, 0:T], xn[:, cc * 128:(cc + 1) * 128]) for cc in range(8)], identb[0:T, 0:T], [B_xn, B_const], [B_pt])
                tt("dve", c.xT[:, :, t * T:(t + 1) * T], ptb[:, :, 0:T], gpre_c.unsqueeze(2).to_broadcast([128, 8, T]), ALU.mult,
                   [B_pt, B_const], [c.B_xT])

        def run_block(mode, blk):
            prompt = mode == "prompt"
            T = 128 if prompt else NSB
            NT = 4 if prompt else 1
            TB = T * NT
            tok0 = blk * TB
            x_src = xp if prompt else xsm
            y_dst = yp if prompt else ysm
            first = prompt and blk == 0
            last = prompt and blk == SEQ // 512 - 1

            do_barrier()
            szb, B_szb = R3.alloc([NT, DI], BF16, parts=T)
            yaT, B_yaT = R3.alloc([16, TB], BF16)
            if pre_xT[0] is not None:
                xT, B_xT = pre_xT[0]
                pre_xT[0] = None
            else:
                actx = stageA_begin(mode, blk)
                stageA_finish(actx)
                xT, B_xT = actx.xT, actx.B_xT
            dtv = None
            if prompt:
                dtv = [R2.alloc([NT, NH], F32) for _ in range(12)]
            r2_mark = R2.off
            o3s = [R2.alloc([512], F32) for _ in range(1)]

            checkpoint()
            dtc = None
            if prompt:
                dtc = dt_chain(NT, T, xT, B_xT, dtv)
            xs_tok, B_xs = R1.alloc([NT, DI], BF16, parts=T)
            b_tok, B_bt = R1.alloc([NT, NG * DS], BF16, parts=T)
            bT, B_bT = R1.alloc([NG, TB], BF16)
            cT, B_cT = R1.alloc([NG, TB], BF16)
            if not prompt:
                sct, B_sct = R3.alloc([CD], F32, parts=T)
                scT, B_scT = R2.alloc([3, 32, T], F32)
                uraw, B_uraw = R2.alloc([32, T], F32)
                for k in range(3):
                    S.dma("sp", sct, sconv[:, k, :], reads=[B_in, B_fence], writes=[B_sct])
                    for c0 in range(0, 32, 16):
                        pt, B_pt = PS()
                        ptv = pt[:, 0:16 * T].rearrange("p (c t) -> p c t", c=16)
                        trgroup([(ptv[:, c, :], sct[:, (c0 + c) * 128:(c0 + c + 1) * 128]) for c in range(16)],
                                identf[0:T, 0:T], [B_sct, B_const], [B_pt])
                        cp("dve", scT[:, k, c0:c0 + 16, :], ptv, [B_pt], [B_scT])
            ub = [R2.alloc([TB + 3], F32) for _ in range(5)]
            accs = [R2.alloc([TB], F32) for _ in range(5)]
            xfm = [R2.alloc([TB], BF16) for _ in range(2)]
            items = []
            for wi, wb in enumerate(W_xbc):
                for cc in range(4):
                    it = Ctx()
                    it.c, it.cc, it.wb = wi * 4 + cc, cc, wb
                    items.append(it)
            cur_slot = [None]

            def sB0(it):
                if it.cc == 0:
                    cur_slot[0] = get_w(it.wb)
                sl, B_sl = cur_slot[0]
                it.ps, it.B_ps = PS()
                mmgroup([(it.ps[:, 0:TB], sl[:, k, it.cc * 128:(it.cc + 1) * 128], xT[:, k, :], k == 0, k == 7) for k in range(8)],
                        [B_sl, B_xT], [it.B_ps])

            def sB1(it):
                c = it.c
                it.u, it.B_u = ub[c % 5]
                it.acc, it.B_acc = accs[c % 5]
                if prompt:
                    cp("pool", it.u[:, 0:3], histc[:, c, :], [B_histc], [it.B_u])
                    act(it.u[:, 3:3 + TB], it.ps[:, 0:TB], AF.Copy, [it.B_ps], [it.B_u])
                    cp("pool", histc[:, c, :], it.u[:, TB:TB + 3], [it.B_u], [B_histc])
                    it.taps = [it.u[:, k:k + TB] for k in range(4)]
                    it.tap_r = [it.B_u]
                else:
                    act(uraw[:, c, :], it.ps[:, 0:TB], AF.Copy, [it.B_ps], [B_uraw])
                    it.taps = [scT[:, 0, c, :], scT[:, 1, c, :], scT[:, 2, c, :], uraw[:, c, :]]
                    it.tap_r = [B_scT, B_uraw]

            def mk_tap(k):
                def f(it):
                    c = it.c
                    if k == 0:
                        act(it.acc, it.taps[0], AF.Identity, it.tap_r + [B_const], [it.B_acc], scale=cw_c[:, c, 0:1], bias=cb_c[:, c:c + 1])
                    else:
                        stt("dve", it.acc, it.taps[k], cw_c[:, c, k:k + 1], it.acc, ALU.mult, ALU.add, it.tap_r + [B_const, it.B_acc], [it.B_acc])
                return f

            def sB6(it):
                c = it.c
                if c < 16:
                    xf, B_xf = xfm[c % 2]
                    act(xf, it.acc, AF.Silu, [it.B_acc], [B_xf])
                    it.tr = (xf, B_xf, xs_tok, B_xs, c * 128)
                elif c < 24:
                    act(bT[:, c - 16, :], it.acc, AF.Silu, [it.B_acc], [B_bT])
                    it.tr = (bT[:, c - 16, :], B_bT, b_tok, B_bt, (c - 16) * 128)
                else:
                    act(cT[:, c - 24, :], it.acc, AF.Silu, [it.B_acc], [B_cT])
                    it.tr = None

            def sB7(it):
                if it.tr is None:
                    return
                src, B_src, dst, B_dst, col = it.tr
                it.pt, it.B_pt = PS()
                it.ptb = it.pt[:].bitcast(BF16)[0:T, 0:NT * 128].rearrange("p (t f) -> p t f", t=NT)
                trgroup([(it.ptb[:, t, :], src[:, t * T:(t + 1) * T]) for t in range(NT)], identb[:], [B_src, B_const], [it.B_pt])

            def sB8(it):
                if it.tr is None:
                    return
                src, B_src, dst, B_dst, col = it.tr
                act(dst[:, :, col:col + 128], it.ptb, AF.Copy, [it.B_pt], [B_dst])

            pipeline(items, [sB0, sB1, mk_tap(0), mk_tap(1), mk_tap(2), mk_tap(3), sB6, sB7, sB8])
            if prompt:
                dt_chain2(NT, dtv, *dtc)
            for q, wb in enumerate(W_z):
                sl, B_sl = get_w(wb)
                for t in range(NT):
                    ps_, B_ps = PS()
                    mmgroup([(ps_[0:T, :], xT[:, k, t * T:(t + 1) * T], sl[:, k, :], k == 0, k == 7) for k in range(8)],
                            [B_sl, B_xT], [B_ps])
                    act(szb[:, t, q * 512:(q + 1) * 512], ps_[0:T, :], AF.Silu, [B_ps], [B_szb])
            if last:
                for c0 in range(0, 32, 4):
                    pt, B_pt = PS()
                    trgroup([(pt[0:3, c * 128:(c + 1) * 128], histc[:, c0 + c, :]) for c in range(4)], identf, [B_histc, B_const], [B_pt])
                    o3, B_o3 = o3s[0]
                    o3 = o3[0:3, :]
                    cp("dve", o3, pt[0:3, :], [B_pt], [B_o3])
                    S.dma("sp", ncp[:, c0 * 128:(c0 + 4) * 128], o3, reads=[B_o3], writes=[Buf()])
            if not prompt:
                S.dma("sp", ncs[:, 0:2, :], sconv[:, 1:3, :], reads=[B_in], writes=[Buf()])
                for c0 in range(0, 32, 4):
                    pt, B_pt = PS()
                    trgroup([(pt[0:T, c * 128:(c + 1) * 128], uraw[:, c0 + c, :]) for c in range(4)], identf, [B_uraw, B_const], [B_pt])
                    o3, B_o3 = o3s[0]
                    o3 = o3[0:T, :]
                    cp("dve", o3, pt[0:T, :], [B_pt], [B_o3])
                    S.dma("sp", ncs[:, 2, c0 * 128:(c0 + 4) * 128], o3, reads=[B_o3], writes=[Buf()])

            checkpoint()
            S.barrier(B_fence, lambda h: h.memset(fence[:], 0.0))
            R2.reset(r2_mark)
            if prompt:
                ssd_prompt(T, NT, TB, xT, B_xT, xs_tok, B_xs, b_tok, B_bt, bT, B_bT, cT, B_cT, szb, B_szb, yaT, B_yaT, last, dtv)
            else:
                ssd_sample(T, xT, B_xT, xs_tok, B_xs, b_tok, B_bt, bT, B_bT, cT, B_cT, szb, B_szb, yaT, B_yaT)

            checkpoint()
            S.barrier(B_fence, lambda h: h.memset(fence[:], 0.0))
            R1.reset()
            R2.reset()
            dT, B_dT = R1.alloc([8, TB], BF16)
            ybT, B_ybT = R1.alloc([8, TB], BF16)
            mT, B_mT = R1.alloc([8, TB], BF16)
            sga, B_sga = R1.alloc([8, TB], BF16)
            sgb, B_sgb = R1.alloc([8, TB], BF16)
            if prompt:
                ubp = [R2.alloc([TB + 15], F32) for _ in range(2)]
                sA = [R2.alloc([TB + 15], F32) for _ in range(2)]
                sB = [R2.alloc([TB + 15], F32) for _ in range(2)]
                t16 = [R2.alloc([16], F32) for _ in range(2)]
                ci = 0
                for wb in W_pool:
                    sl, B_sl = get_w(wb)
                    for cc in range(4):
                        c = ci
                        ci += 1
                        u, B_u = ubp[c % 2]
                        a_, B_a = sA[c % 2]
                        b_, B_b = sB[c % 2]
                        ps_, B_ps = PS()
                        mmgroup([(ps_[:, 0:TB], sl[:, k, cc * 128:(cc + 1) * 128], xT[:, k, :], k == 0, k == 7) for k in range(8)],
                                [B_sl, B_xT], [B_ps])
                        cp("pool", u[:, 0:15], histp[:, c, :], [B_histp], [B_u])
                        act(u[:, 15:15 + TB], ps_[:, 0:TB], AF.Copy, [B_ps], [B_u])
                        cp("pool", histp[:, c, :], u[:, TB:TB + 15], [B_u], [B_histp])
                        w = POOLW[c // 2]
                        L = TB + 15
                        tt("dve", a_[:, 1:L], u[:, 1:L], u[:, 0:L - 1], ALU.add, [B_u], [B_a])
                        cur, B_cur, oth, B_oth = a_, B_a, b_, B_b
                        sh = 2
                        while sh < w:
                            lo = 2 * sh - 1
                            tt("dve", oth[:, lo:L], cur[:, lo:L], cur[:, lo - sh:L - sh], ALU.add, [B_cur], [B_oth])
                            cur, B_cur, oth, B_oth = oth, B_oth, cur, B_cur
                            sh *= 2
                        stt("dve", dT[:, c, :], cur[:, 15:L], 1.0 / w, u[:, 15:L], ALU.mult, ALU.subtract, [B_cur, B_u], [B_dT])
                        if first:
                            tq, B_tq = t16[c % 2]
                            tt("dve", tq, cur[:, 15:31], invc[:, c, :], ALU.mult, [B_cur, B_const], [B_tq])
                            tt("dve", dT[:, c, 0:16], tq, u[:, 15:31], ALU.subtract, [B_tq, B_u, B_dT], [B_dT])
                if last:
                    for c0 in range(0, 8, 4):
                        pt, B_pt = PS()
                        trgroup([(pt[0:15, c * 128:(c + 1) * 128], histp[:, c0 + c, :]) for c in range(4)], identf, [B_histp, B_const], [B_pt])
                        o3, B_o3 = R2.alloc([512], F32, parts=15)
                        cp("dve", o3, pt[0:15, :], [B_pt], [B_o3])
                        S.dma("sp", npp[:, c0 * 128:(c0 + 4) * 128], o3, reads=[B_o3], writes=[Buf()])
            else:
                spt, B_spt = R2.alloc([15, 256], F32, parts=T)
                ut, B_ut = R2.alloc([D], F32, parts=T)
                sm, B_sm = R2.alloc([D], F32, parts=T)
                dtk, B_dtk = R2.alloc([D], BF16, parts=T)
                for q, wb in enumerate(W_pool):
                    sl, B_sl = get_w(wb)
                    ps_, B_ps = PS()
                    mmgroup([(ps_[0:T, :], xT[:, k, :], sl[:, k, :], k == 0, k == 7) for k in range(8)], [B_sl, B_xT], [B_ps])
                    cp("dve", ut[:, q * 512:(q + 1) * 512], ps_[0:T, :], [B_ps], [B_ut])
                S.dma("sp", nps[:, 0:14, :], spool[:, 1:15, :], reads=[B_in], writes=[Buf()])
                S.dma("sp", nps[:, 14, :], ut, reads=[B_ut], writes=[Buf()])
                for gi, w in enumerate(POOLW):
                    fs = slice(gi * 256, (gi + 1) * 256)
                    S.dma("sp", spt[:, 0:w - 1, :], spool[:, 16 - w:15, fs], reads=[B_in, B_fence], writes=[B_spt])
                    tt("dve", sm[:, fs], spt[:, 0, :], ut[:, fs], ALU.add, [B_spt, B_ut], [B_sm])
                    for kk in range(1, w - 1):
                        tt("dve", sm[:, fs], sm[:, fs], spt[:, kk, :], ALU.add, [B_spt, B_sm], [B_sm])
                    stt("dve", dtk[:, fs], sm[:, fs], 1.0 / w, ut[:, fs], ALU.mult, ALU.subtract, [B_sm, B_ut], [B_dtk])
                pt, B_pt = PS()
                ptb = pt[:].bitcast(BF16)[:, 0:8 * T].rearrange("p (c t) -> p c t", c=8)
                trgroup([(ptb[:, c, :], dtk[:, c * 128:(c + 1) * 128]) for c in range(8)], identb[0:T, 0:T], [B_dtk, B_const], [B_pt])
                cp("dve", dT, ptb, [B_pt], [B_dT])
            for (wl, dst, B_dst) in ((W_ga, sga, B_sga), (W_gb, sgb, B_sgb)):
                ci = 0
                for wb in wl:
                    sl, B_sl = get_w(wb)
                    for cc in range(4):
                        ps_, B_ps = PS()
                        mmgroup([(ps_[:, 0:TB], sl[:, k, cc * 128:(cc + 1) * 128], xT[:, k, :], k == 0, k == 7) for k in range(8)],
                                [B_sl, B_xT], [B_ps])
                        act(dst[:, ci, :], ps_[:, 0:TB], AF.Sigmoid, [B_ps], [B_dst])
                        ci += 1
            for oc in range(8):
                gi = oc // 2
                ps_, B_ps = PS()
                mmgroup([(ps_[:, 0:TB], wpg[:, gi * 2 + kc, (oc % 2) * 128:(oc % 2 + 1) * 128], dT[:, gi * 2 + kc, :], kc == 0, kc == 1)
                         for kc in range(2)], [B_const, B_wres2, B_dT], [B_ps])
                act(ybT[:, oc, :], ps_[:, 0:TB], AF.Copy, [B_ps, B_const], [B_ybT], scale=ps_c[:, oc:oc + 1])
            t1s = [R2.alloc([TB], F32) for _ in range(2)]
            t2s = [R2.alloc([TB], F32) for _ in range(2)]
            for cbk in range(2):
                sa0, B_sa0 = get_w(W_a[0][cbk])
                sa1, B_sa1 = get_w(W_a[1][cbk])
                sbb, B_sbb = get_w(W_b[cbk])
                for cc in range(4):
                    oc = cbk * 4 + cc
                    t1, B_t1 = t1s[oc % 2]
                    t2, B_t2 = t2s[oc % 2]
                    pa, B_pa = PS()
                    mmgroup([(pa[:, 0:TB], (sa0 if k < 8 else sa1)[:, k % 8, cc * 128:(cc + 1) * 128], yaT[:, k, :], k == 0, k == 15)
                             for k in range(16)], [B_sa0, B_sa1, B_yaT], [B_pa])
                    tt("dve", t1, pa[:, 0:TB], sga[:, oc, :], ALU.mult, [B_pa, B_sga], [B_t1])
                    pb, B_pb = PS()
                    mmgroup([(pb[:, 0:TB], sbb[:, k, cc * 128:(cc + 1) * 128], ybT[:, k, :], k == 0, k == 7) for k in range(8)],
                            [B_sbb, B_ybT], [B_pb])
                    tt("dve", t2, pb[:, 0:TB], sgb[:, oc, :], ALU.mult, [B_pb, B_sgb], [B_t2])
                    tt("pool", mT[:, oc, :], t1, t2, ALU.add, [B_t1, B_t2], [B_mT])

            checkpoint()
            S.barrier(B_fence, lambda h: h.memset(fence[:], 0.0))
            R2.reset()
            R3.reset()
            htl = [R3.alloc([D], F32, parts=T) for _ in range(NT)]
            hnT, B_hnT = R3.alloc([8, TB], BF16)
            so0, B_so0 = get_w(W_o[0])
            so1, B_so1 = get_w(W_o[1])
            fitems = []
            for t in range(NT):
                it = Ctx()
                it.t = t
                it.ht, it.B_ht = htl[t]
                it.sq, it.B_sq = R2.alloc([4], F32, parts=T)
                it.tmp, it.B_tmp = R2.alloc([D], F32, parts=T)
                it.sq2, it.B_sq2 = R2.alloc([2], F32, parts=T)
                it.hn, it.B_hn = R2.alloc([D], BF16, parts=T)
                fitems.append(it)
            so_l = ((so0, B_so0), (so1, B_so1))

            def sF0(it):
                t = it.t
                S.dma("sp", it.ht, x_src[tok0 + t * T: tok0 + (t + 1) * T, :], reads=[B_in, B_fence], writes=[it.B_ht])
                it.po = []
                for cbk, (so, B_so) in enumerate(so_l):
                    po, B_po = PS()
                    mmgroup([(po[0:T, :], mT[:, k, t * T:(t + 1) * T], so[:, k, :], k == 0, k == 7) for k in range(8)],
                            [B_so, B_mT], [B_po])
                    it.po.append((po, B_po))

            def sF1(it):
                for cbk, (po, B_po) in enumerate(it.po):
                    act_sq('0:T, 0:512', po[0:T, :], [B_po], [it.B_sq], accum_out=it.sq[:, cbk:cbk + 1])
                    tt("dve", it.tmp[:, cbk * 512:(cbk + 1) * 512], po[0:T, :], gpost_bc[0:T, cbk * 512:(cbk + 1) * 512], ALU.mult,
                       [B_po, B_const, it.B_sq], [it.B_tmp])

            def sF2(it):
                tt("pool", it.sq[:, 2:3], it.sq[:, 0:1], it.sq[:, 1:2], ALU.add, [it.B_sq], [it.B_sq])
                rstd_from(it.sq[:, 2:3], it.sq[:, 3:4], D, [it.B_sq], it.B_sq)
                stt("dve", it.ht, it.tmp, it.sq[:, 3:4], it.ht, ALU.mult, ALU.add, [it.B_tmp, it.B_sq, it.B_ht], [it.B_ht])

            def sF3(it):
                act_sq('0:T, 0:D', it.ht, [it.B_ht], [it.B_sq2], accum_out=it.sq2[:, 0:1])

            def sF4(it):
                rstd_from(it.sq2[:, 0:1], it.sq2[:, 1:2], D, [it.B_sq2], it.B_sq2)
                act(it.hn, it.ht, AF.Copy, [it.B_ht, it.B_sq2], [it.B_hn], scale=it.sq2[:, 1:2])

            def sF5(it):
                it.pt, it.B_pt = PS()
                it.ptb = it.pt[:].bitcast(BF16).rearrange("p (c t) -> p c t", c=8)
                trgroup([(it.ptb[:, c, 0:T], it.hn[:, c * 128:(c + 1) * 128]) for c in range(8)], identb[0:T, 0:T], [it.B_hn, B_const], [it.B_pt])

            def sF6(it):
                t = it.t
                tt("dve", hnT[:, :, t * T:(t + 1) * T], it.ptb[:, :, 0:T], gffn_c.unsqueeze(2).to_broadcast([128, 8, T]), ALU.mult,
                   [it.B_pt, B_const], [B_hnT])

            pipeline(fitems, [sF0, sF1, sF2, sF3, sF4, sF5, sF6])
            checkpoint()
            R2.reset()
            R1.reset()
            S.barrier(B_fence, lambda h: h.memset(fence[:], 0.0))
            actT, B_actT = R1.alloc([22, TB], BF16)
            sgs = [R2.alloc([TB], F32) for _ in range(2)]
            fc = 0
            for i in range(len(W_fg)):
                sg_, B_sg_ = get_w(W_fg[i])
                su_, B_su_ = get_w(W_fu[i])
                for cc in range(W_fg[i].nc // 128):
                    sgt, B_sgt = sgs[fc % 2]
                    pg, B_pg = PS()
                    mmgroup([(pg[:, 0:TB], sg_[:, k, cc * 128:(cc + 1) * 128], hnT[:, k, :], k == 0, k == 7) for k in range(8)],
                            [B_sg_, B_hnT], [B_pg])
                    act(sgt, pg[:, 0:TB], AF.Silu, [B_pg], [B_sgt])
                    pu, B_pu = PS()
                    mmgroup([(pu[:, 0:TB], su_[:, k, cc * 128:(cc + 1) * 128], hnT[:, k, :], k == 0, k == 7) for k in range(8)],
                            [B_su_, B_hnT], [B_pu])
                    tt("dve", actT[:, fc, :], sgt, pu[:, 0:TB], ALU.mult, [B_sgt, B_pu], [B_actT])
                    fc += 1
            fsb = [R2.alloc([D], F32, parts=T) for _ in range(NT)]
            sq3 = [R2.alloc([4], F32, parts=T) for _ in range(NT)]
            nxt = None
            if prompt:
                nxt = ("prompt", blk + 1) if blk + 1 < SEQ // 512 else ("sample", 0)
            nctx = None
            for cbk in range(2):
                sws = [get_w(W_fo[kg][cbk]) for kg in range(3)]
                if cbk == 0 and nxt is not None:
                    nctx = stageA_begin(*nxt)
                if cbk == 1 and nctx is not None:
                    stageA_finish(nctx)
                    pre_xT[0] = (nctx.xT, nctx.B_xT)
                for t in range(NT):
                    pf, B_pf = PS()
                    mmgroup([(pf[0:T, :], actT[:, k, t * T:(t + 1) * T], sws[k // 8][0][:, k % 8, :], k == 0, k == 21) for k in range(22)],
                            [sws[0][1], sws[1][1], sws[2][1], B_actT], [B_pf])
                    act_sq('', pf[0:T, :], [B_pf], [sq3[t][1]], accum_out=sq3[t][0][:, cbk:cbk + 1])
                    tt("dve", fsb[t][0][:, cbk * 512:(cbk + 1) * 512], pf[0:T, :], gfpost_bc[0:T, cbk * 512:(cbk + 1) * 512], ALU.mult,
                       [B_pf, B_const, sq3[t][1]], [fsb[t][1]])
            if nxt is not None:
                for wb in W_xbc[:3]:
                    prefetch_w(wb)
            for t in range(NT):
                sq, B_sq = sq3[t]
                f_, B_f = fsb[t]
                ht, B_ht = htl[t]
                tt("pool", sq[:, 2:3], sq[:, 0:1], sq[:, 1:2], ALU.add, [B_sq], [B_sq])
                rstd_from(sq[:, 2:3], sq[:, 3:4], D, [B_sq], B_sq)
                stt("dve", f_, f_, sq[:, 3:4], ht, ALU.mult, ALU.add, [B_f, B_sq, B_ht], [B_f])
                S.dma("sp", y_dst[tok0 + t * T: tok0 + (t + 1) * T, :], f_, reads=[B_f], writes=[Buf()])

        def dt_chain(NT, T, xT, B_xT, dtv):
            (v, B_v), (av, B_av), (e_, B_e), (l_, B_l), (dt_, B_dt), (dta, B_dta), (lndt, B_lndt), (nb, B_nb), \
                (eac, B_eac), (wl, B_wl), (wend, B_wend), (elast, B_elast) = dtv
            pa, B_pa = PS()
            pdt = pa[:, 0:NT * NH].rearrange("p (t h) -> p t h", t=NT)
            pac = pa[:, 128:128 + NT * NH].rearrange("p (t h) -> p t h", t=NT)
            ptot = pa[:, 256:256 + NT * NH].rearrange("p (t h) -> p t h", t=NT)
            mmgroup([(pdt[:, t, :], xT[:, k, t * T:(t + 1) * T], wdt[:, k, :], k == 0, k == 7) for t in range(NT) for k in range(8)],
                    [B_xT, B_const, B_wres], [B_pa])
            tt("dve", v, pdt, dtb_bc.unsqueeze(1).to_broadcast([128, NT, NH]), ALU.add, [B_pa, B_const], [B_v])
            act(av, v, AF.Abs, [B_v], [B_av])
            act(e_, av, AF.Exp, [B_av], [B_e], scale=-1.0)
            act(l_, e_, AF.Ln, [B_e], [B_l], bias=1.0)
            stt("dve", dt_, v, 0.0, l_, ALU.max, ALU.add, [B_v, B_l], [B_dt])
            tt("dve", dta, dt_, abc[:].unsqueeze(1).to_broadcast([128, NT, NH]), ALU.mult, [B_dt, B_const], [B_dta])
            act(lndt, dt_, AF.Ln, [B_dt], [B_lndt])
            return pa, B_pa, pac, ptot

        def dt_chain2(NT, dtv, pa, B_pa, pac, ptot):
            (v, B_v), (av, B_av), (e_, B_e), (l_, B_l), (dt_, B_dt), (dta, B_dta), (lndt, B_lndt), (nb, B_nb), \
                (eac, B_eac), (wl, B_wl), (wend, B_wend), (elast, B_elast) = dtv
            pa, B_pa = PS()
            pac = pa[:, 128:128 + NT * NH].rearrange("p (t h) -> p t h", t=NT)
            ptot = pa[:, 256:256 + NT * NH].rearrange("p (t h) -> p t h", t=NT)
            pairs = []
            for t in range(NT):
                pairs.append((pac[:, t, :], trif, dta[:, t, :], True, True))
                pairs.append((ptot[:, t, :], onesf, dta[:, t, :], True, True))
            mmgroup(pairs, [B_dta, B_const], [B_pa])
            tt("dve", nb, lndt, pac, ALU.subtract, [B_lndt, B_pa], [B_nb])
            act(eac, pac, AF.Exp, [B_pa, B_nb], [B_eac])
            tt("dve", wl, ptot, nb, ALU.add, [B_pa, B_nb, B_eac], [B_wl])
            act(elast, ptot, AF.Exp, [B_pa, B_wl], [B_elast])
            act(wend, wl, AF.Exp, [B_wl], [B_wend])

        def ssd_prompt(T, NT, TB, xT, B_xT, xs_tok, B_xs, b_tok, B_bt, bT, B_bT, cT, B_cT, szb, B_szb, yaT, B_yaT, last, dtv):
            (v, B_v), (av, B_av), (e_, B_e), (l_, B_l), (dt_, B_dt), (dta, B_dta), (lndt, B_lndt), (nb, B_nb), \
                (eac, B_eac), (wl, B_wl), (wend, B_wend), (elast, B_elast) = dtv
            d2s = [R2.alloc([4, 128], F32) for _ in range(3)]
            dcs = [R2.alloc([4, 128], BF16) for _ in range(3)]
            mts = [R2.alloc([4, 128], BF16) for _ in range(3)]
            cbss = [R2.alloc([128], F32) for _ in range(3)]
            yts = [R2.alloc([256], F32) for _ in range(3)]
            yz, B_yz = R2.alloc([DI], F32)
            yns = [R2.alloc([DI], BF16) for _ in range(1)]
            xws = [R2.alloc([DI], BF16) for _ in range(2)]
            sq4s = [R2.alloc([16], F32) for _ in range(2)]
            o4b = [R2.alloc([512], F32) for _ in range(2)]
            pool_n = {"seg": 0, "cb": 0, "py": 0, "gen": 0}
            pool_banks = {"seg": [0, 1, 2], "cb": [3, 4], "py": [5, 6], "gen": [7]}

            def PSP(kind):
                lst = pool_banks[kind]
                i = lst[pool_n[kind] % len(lst)]
                pool_n[kind] += 1
                return psb[i], psbuf[i]

            items = []
            for t in range(NT):
                for g in range(NG):
                    it = Ctx()
                    it.t, it.g, it.i = t, g, t * NG + g
                    it.tsl = slice(t * T, (t + 1) * T)
                    items.append(it)

            def sA(it):
                t, g = it.t, it.g
                if g == 0:
                    xw, B_xw = xws[t % 2]
                    tt("pool", xw.rearrange("p (a b) -> p a b", a=NH), xs_tok[:, t, :].rearrange("p (a b) -> p a b", a=NH),
                       wend[:, t, :].unsqueeze(2).to_broadcast([128, NH, 64]), ALU.mult, [B_xs, B_wend], [B_xw])
                it.d2, it.B_d2 = d2s[it.i % 3]
                tt("pool", it.d2, trif.unsqueeze(1).to_broadcast([128, 4, 128]),
                   dta[:, t, g * 4:(g + 1) * 4].unsqueeze(2).to_broadcast([128, 4, 128]), ALU.mult, [B_const, B_dta], [it.B_d2])

            def sB(it):
                t, g = it.t, it.g
                it.pseg, it.B_pseg = PSP("seg")
                mmgroup([(it.pseg[:], identb[:], neg4b[:].rearrange("p a b -> p (a b)"), True, False),
                         (it.pseg[:], onesf, it.d2.rearrange("p a b -> p (a b)"), False, True)], [B_const, it.B_d2], [it.B_pseg])
                it.pcb, it.B_pcb = PSP("cb")
                mmgroup([(it.pcb[:, 0:128], bT[:, g, it.tsl], cT[:, g, it.tsl], True, True)], [B_bT, B_cT], [it.B_pcb])

            def sC(it):
                t, g = it.t, it.g
                it.dc, it.B_dc = dcs[it.i % 3]
                it.cbs, it.B_cbs = cbss[it.i % 3]
                act(it.cbs, it.pcb[:, 0:128], AF.Copy, [it.B_pcb], [it.B_cbs])
                psv = it.pseg[:].rearrange("p (a b) -> p a b", a=4)
                for hh in range(4):
                    act(it.dc[:, hh, :], psv[:, hh, :], AF.Exp, [it.B_pseg, B_nb] if hh in (0, 3) else [], [it.B_dc] if hh in (0, 3) else [],
                        bias=nb[:, t, g * 4 + hh:g * 4 + hh + 1])

            def sD(it):
                it.mt, it.B_mt = mts[it.i % 3]
                tt("dve", it.mt, it.dc, it.cbs.unsqueeze(1).to_broadcast([128, 4, 128]), ALU.mult, [it.B_dc, it.B_cbs], [it.B_mt])

            def sE(it):
                t, g = it.t, it.g
                it.pyy, it.B_pyy = PSP("py")
                pairs = []
                for hh in range(4):
                    hd = g * 4 + hh
                    col = hh * 64
                    pairs.append((it.pyy[:, col:col + 64], it.mt[:, hh, :], xs_tok[:, t, hd * 64:(hd + 1) * 64], hh == 0, False))
                    pairs.append((it.pyy[:, col:col + 64], diagD[:, hd, :], xs_tok[:, t, hd * 64:(hd + 1) * 64], False, False))
                pairs.append((it.pyy[:, 256:512], cT[:, g, it.tsl], hTb[:, g * 256:(g + 1) * 256], False, True))
                mmgroup(pairs, [it.B_mt, B_xs, B_const, B_cT, B_hTb], [it.B_pyy])
                qq = g // 2
                hs = hT[:, qq * 512:(qq + 1) * 512]
                if g % 2 == 0:
                    tt("pool", hs.rearrange("p (a b) -> p a b", a=8), hs.rearrange("p (a b) -> p a b", a=8),
                       elast[:, t, qq * 8:(qq + 1) * 8].unsqueeze(2).to_broadcast([128, 8, 64]), ALU.mult, [B_hT, B_elast], [B_hT])
                else:
                    xw, B_xw = xws[t % 2]
                    pu, B_pu = PSP("gen")
                    mmgroup([(pu[:, gg * 256:(gg + 1) * 256], b_tok[:, t, (2 * qq + gg) * 128:(2 * qq + gg + 1) * 128],
                              xw[:, (2 * qq + gg) * 256:(2 * qq + gg + 1) * 256], gg == 0, gg == 1) for gg in range(2)],
                            [B_bt, B_xw], [B_pu])
                    tt("dve", hs, hs, pu[:], ALU.add, [B_hT, B_pu], [B_hT])
                    act(hTb[:, qq * 512:(qq + 1) * 512], hs, AF.Copy, [B_hT], [B_hTb])

            def sF(it):
                t, g = it.t, it.g
                yt, B_yt = yts[it.i % 3]
                sq4, B_sq4 = sq4s[t % 2]
                tt("dve", yt.rearrange("p (a b) -> p a b", a=4), it.pyy[:, 256:512].rearrange("p (a b) -> p a b", a=4),
                   eac[:, t, g * 4:(g + 1) * 4].unsqueeze(2).to_broadcast([128, 4, 64]), ALU.mult, [it.B_pyy, B_eac], [B_yt])
                tt("dve", yt, yt, it.pyy[:, 0:256], ALU.add, [B_yt, it.B_pyy], [B_yt])
                tt("dve", yz[:, g * 256:(g + 1) * 256], yt, szb[:, t, g * 256:(g + 1) * 256], ALU.mult, [B_yt, B_szb], [B_yz])
                act_sq('', yz[:, g * 256:(g + 1) * 256], [B_yz], [B_sq4], accum_out=sq4[:, g:g + 1])
                if g == NG - 1:
                    S.op("dve", lambda h, sq4=sq4: h.tensor_reduce(out=sq4[:, 8:9], in_=sq4[:, 0:8], axis=mybir.AxisListType.X, op=ALU.add),
                         [B_sq4], [B_sq4])
                    rstd_from(sq4[:, 8:9], sq4[:, 9:10], DI, [B_sq4], B_sq4)
                    it.yn, it.B_yn = yns[0]
                    act(it.yn, yz, AF.Copy, [B_yz, B_sq4], [it.B_yn], scale=sq4[:, 9:10])

            def sG(it):
                if it.g != NG - 1:
                    return
                for c0 in range(0, 16, 8):
                    pt, B_pt = PSP("gen")
                    ptb = pt[:].bitcast(BF16).rearrange("p (c t) -> p c t", c=8)
                    trgroup([(ptb[:, c, :], it.yn[:, (c0 + c) * 128:(c0 + c + 1) * 128]) for c in range(8)], identb[:], [it.B_yn, B_const], [B_pt])
                    for c in range(8):
                        act(yaT[:, c0 + c, it.tsl], ptb[:, c, :], AF.Copy, [B_pt, B_const] if c in (0, 7) else [], [B_yaT] if c in (0, 7) else [],
                            scale=gssm_c[:, c0 + c:c0 + c + 1])

            pipeline(items, [sA, sB, sC, sD, sE, sF, None, sG], lag=2)
            if last:
                for c0 in range(0, 16, 4):
                    pt, B_pt = PS()
                    trgroup([(pt[:, c * 128:(c + 1) * 128], hT[:, (c0 + c) * 128:(c0 + c + 1) * 128]) for c in range(4)], identf, [B_hT, B_const], [B_pt])
                    o4, B_o4 = o4b[(c0 // 4) % 2]
                    cp("dve", o4, pt[:], [B_pt], [B_o4])
                    S.dma("sp", nsp[c0 * 128:(c0 + 4) * 128, :].rearrange("(c p) n -> p c n", p=128), o4.rearrange("p (a b) -> p a b", a=4),
                          reads=[B_o4], writes=[Buf()])

        def ssd_sample(T, xT, B_xT, xs_tok, B_xs, b_tok, B_bt, bT, B_bT, cT, B_cT, szb, B_szb, yaT, B_yaT):
            sm_, B_sm = R2.alloc([8, NH], F32, parts=T)
            v, l_, dt_, dta, da, av, e_, _u = [sm_[:, i, :] for i in range(8)]
            pa, B_pa = PS()
            mmgroup([(pa[0:T, 0:32], xT[:, k, :], wdt[:, k, :], k == 0, k == 7) for k in range(8)], [B_xT, B_const, B_wres], [B_pa])
            tt("dve", v, pa[0:T, 0:32], dtb_bc[0:T, :], ALU.add, [B_pa, B_const], [B_sm])
            act(av, v, AF.Abs, [B_sm], [B_sm])
            act(e_, av, AF.Exp, [B_sm], [B_sm], scale=-1.0)
            act(l_, e_, AF.Ln, [B_sm], [B_sm], bias=1.0)
            stt("dve", dt_, v, 0.0, l_, ALU.max, ALU.add, [B_sm], [B_sm])
            tt("dve", dta, dt_, abc[0:T, :], ALU.mult, [B_sm, B_const], [B_sm])
            act(da, dta, AF.Exp, [B_sm], [B_sm])
            rep, B_rep = R2.alloc([DI], F32, parts=T)
            repv = rep.rearrange("b (p c) -> b c p", c=16)
            cols, B_cols = R2.alloc([3, 16, T], F32)
            for j, srcv in enumerate((da, dt_, dsk_bc[0:T, :])):
                cp("dve", rep.rearrange("p (a b) -> p a b", a=NH), srcv.unsqueeze(2).to_broadcast([T, NH, 64]), [B_sm, B_const], [B_rep])
                pt, B_pt = PS()
                ptv = pt[:, 0:16 * T].rearrange("p (c t) -> p c t", c=16)
                trgroup([(ptv[:, c, :], repv[:, c, :]) for c in range(16)], identf[0:T, 0:T], [B_rep, B_const], [B_pt])
                cp("dve", cols[:, j, :, :], ptv, [B_pt], [B_cols])
            xcol, B_xcol = R2.alloc([16, T], F32)
            xsv = xs_tok[:, 0, :].rearrange("b (p c) -> b c p", c=16)
            pt, B_pt = PS()
            ptb = pt[:].bitcast(BF16)[:, 0:16 * T].rearrange("p (c t) -> p c t", c=16)
            trgroup([(ptb[:, c, :], xsv[:, c, :]) for c in range(16)], identb[0:T, 0:T], [B_xs, B_const], [B_pt])
            cp("dve", xcol, ptb, [B_pt], [B_xcol])
            dtx, B_dtx = R2.alloc([16, T], F32)
            tt("dve", dtx, xcol, cols[:, 1, :, :], ALU.mult, [B_xcol, B_cols], [B_dtx])
            bc2, B_bc2 = R2.alloc([2, 128], BF16)
            pt, B_pt = PS()
            ptb = pt[:].bitcast(BF16)[:, 0:256].rearrange("p (a n) -> p a n", a=2)
            trgroup([(ptb[:, 0, :], bT.rearrange("n g b -> n (g b)")), (ptb[:, 1, :], cT.rearrange("n g b -> n (g b)"))], identb[:],
                    [B_bT, B_cT, B_const], [B_pt])
            cp("dve", bc2, ptb, [B_pt], [B_bc2])
            self32, B_self32 = R2.alloc([T, 128], F32)
            sel, B_sel = R2.alloc([T, 128], BF16)
            S.dma("sp", self32.rearrange("p a b -> p (a b)"), selc[:, :], reads=[B_in], writes=[B_self32])
            cp("dve", sel, self32, [B_self32], [B_sel])
            ycol, B_ycol = R2.alloc([16, T], F32)
            junkfs = [R2.alloc([128], F32) for _ in range(3)]
            hbuf = [R1.alloc([16, DS], F32) for _ in range(4)]
            def load_state(bq):
                hbq, B_hbq = hbuf[bq % 4]
                S.dma("sp", hbq, sssm[bq].rearrange("(p c) n -> p c n", c=16), reads=[B_in], writes=[B_hbq])

            for bq in range(3):
                load_state(bq)
            for b in range(T):
                hb, B_hb = hbuf[b % 4]
                pB, B_pB = PS()
                pC, B_pC = PS()
                mmgroup([(pB[:, 0:128], sel[:, b, :], bc2[:, 0, :], True, True)], [B_sel, B_bc2], [B_pB])
                mmgroup([(pC[:, 0:128], sel[:, b, :], bc2[:, 1, :], True, True)], [B_sel, B_bc2], [B_pC])
                for c in range(16):
                    act(hb[:, c, :], hb[:, c, :], AF.Copy, [B_hb, B_cols] if c in (0, 15) else [],
                        [B_hb] if c in (0, 15) else [], scale=cols[:, 0, c, b:b + 1])
                for c in range(16):
                    S.op("dve", lambda h, c=c, b=b, pB=pB, hb=hb: h.scalar_tensor_tensor(
                        out=hb[:, c, :], in0=pB[:, 0:128], scalar=dtx[:, c, b:b + 1], in1=hb[:, c, :],
                        op0=ALU.mult, op1=ALU.add), [B_hb, B_pB, B_dtx] if c in (0, 15) else [],
                        [B_hb] if c in (0, 15) else [])
                S.dma("sp", nss[b].rearrange("(p c) n -> p c n", c=16), hb, reads=[B_hb], writes=[Buf()])
                if b + 3 < T:
                    load_state(b + 3)
                for c in range(16):
                    jf, B_jf = junkfs[c % 3]
                    S.op("dve", lambda h, c=c, b=b, pC=pC, hb=hb, jf=jf: h.scalar_tensor_tensor(
                        out=jf[:, 0:128], in0=hb[:, c, :], scalar=1.0, in1=pC[:, 0:128],
                        op0=ALU.mult, op1=ALU.mult, accum_out=ycol[:, c, b:b + 1]), [B_hb, B_pC] if c in (0, 15) else [],
                        ([B_ycol] if c in (0, 15) else []) + [B_jf])
            dcol, B_dcol = R2.alloc([16, T], F32)
            tt("dve", dcol, cols[:, 2, :, :], xcol, ALU.mult, [B_cols, B_xcol], [B_dcol])
            tt("dve", ycol, ycol, dcol, ALU.add, [B_ycol, B_dcol], [B_ycol])
            ytok, B_ytok = R2.alloc([DI], F32, parts=T)
            ytv = ytok.rearrange("b (p c) -> b c p", c=16)
            szv = szb[:, 0, :].rearrange("b (p c) -> b c p", c=16)
            for c0 in range(0, 16, 4):
                pt, B_pt = PS()
                trgroup([(pt[0:T, c * 128:(c + 1) * 128], ycol[:, c0 + c, :]) for c in range(4)], identf, [B_ycol, B_const], [B_pt])
                tt("dve", ytv[:, c0:c0 + 4, :], pt[0:T, :].rearrange("b (c p) -> b c p", c=4), szv[:, c0:c0 + 4, :], ALU.mult,
                   [B_pt, B_szb], [B_ytok])
            sq, B_sq = R2.alloc([4], F32, parts=T)
            act_sq('', ytok[:, 0:1024], [B_ytok], [B_sq], accum_out=sq[:, 0:1])
            act_sq('', ytok[:, 1024:2048], [B_ytok], [B_sq], accum_out=sq[:, 2:3])
            tt("pool", sq[:, 0:1], sq[:, 0:1], sq[:, 2:3], ALU.add, [B_sq], [B_sq])
            rstd_from(sq[:, 0:1], sq[:, 1:2], DI, [B_sq], B_sq)
            yn, B_yn = R2.alloc([DI], BF16, parts=T)
            act(yn, ytok, AF.Copy, [B_ytok, B_sq], [B_yn], scale=sq[:, 1:2])
            pt, B_pt = PS()
            ptb = pt[:].bitcast(BF16)[:, 0:16 * T].rearrange("p (c t) -> p c t", c=16)
            trgroup([(ptb[:, c, :], yn[:, c * 128:(c + 1) * 128]) for c in range(16)], identb[0:T, 0:T], [B_yn, B_const], [B_pt])
            tt("dve", yaT, ptb, gssm_c.unsqueeze(2).to_broadcast([128, 16, T]), ALU.mult, [B_pt, B_const], [B_yaT])

        try:
            checkpoint()
            for blk in range(SEQ // 512):
                run_block("prompt", blk)
            run_block("sample", 0)
        except _Stop:
            pass
        S.finish()
    return nc


def _consts():
    c = np.zeros((128, NCST), np.float32)
    k = np.arange(128)[:, None]
    i = np.arange(128)[None, :]
    c[:, CS_ID:CS_ID + 128] = (k == i)
    c[:, CS_TRI:CS_TRI + 128] = (k <= i)
    c[:, CS_NEG:CS_NEG + 128] = np.where(i < k, NEG, 0.0)
    c[:, CS_ONE:CS_ONE + 128] = 1.0
    inv = np.zeros((8, 16), np.float32)
    for ch in range(8):
        w = POOLW[ch // 2]
        inv[ch] = 1.0 / np.minimum(np.arange(16) + 1, w)
    c[:, CS_INV:CS_INV + 128] = inv.reshape(1, 128)
    return c


_NC_CACHE = {}


def kernel(x_prompt, x_sample, state_conv, state_ssm, state_pool,
           norm_mix_pre, norm_mix_post, norm_ffn_pre, norm_ffn_post,
           w_in, conv_w, conv_b, dt_bias, a_log, d_skip, ssm_norm,
           w_pool_group, pool_scale, w_branch_a, w_branch_b, w_out,
           w_ffn_in, w_ffn_out):
    f = lambda a: np.ascontiguousarray(np.asarray(a, dtype=np.float32))
    fm = lambda v: f(v).reshape(-1, 128).T
    colp = np.zeros((128, NCOL), np.float32)
    colp[:, CP_GPRE:CP_GPRE + 8] = fm(norm_mix_pre[0])
    colp[:, CP_GFFN:CP_GFFN + 8] = fm(norm_ffn_pre[0])
    colp[:, CP_PS:CP_PS + 8] = fm(pool_scale[0])
    colp[:, CP_GSSM:CP_GSSM + 16] = fm(ssm_norm[0])
    cw = f(conv_w[0]).reshape(4, 32, 128).transpose(2, 1, 0)
    colp[:, CP_CW:CP_CW + 128] = cw.reshape(128, 128)
    colp[:, CP_CB:CP_CB + 32] = fm(conv_b[0])
    rowp = np.concatenate([f(norm_mix_post[0]), f(norm_ffn_post[0]), f(dt_bias[0]), f(a_log[0]), f(d_skip[0])])[None, :]
    rowp = f(rowp)
    cst = _consts()
    kk = np.arange(128)[:, None, None]
    bb = np.arange(NSB)[None, :, None]
    mm_ = np.arange(128)[None, None, :]
    selc = (((kk % NSB) == bb) & ((kk // NSB) == (mm_ // 16))).astype(np.float32).reshape(128, NSB * 128)
    shared = {
        "w_in": f(w_in[0]), "w_pg": f(w_pool_group[0]).reshape(D, 256), "w_a": f(w_branch_a[0]), "w_b": f(w_branch_b[0]),
        "w_o": f(w_out[0]), "w_fi": f(w_ffn_in[0]), "w_fo": f(w_ffn_out[0]), "colp": colp, "rowp": rowp, "cst": cst, "selc": selc,
    }
    xpr = f(x_prompt)
    xsr = f(x_sample).reshape(128, D)
    sc = f(state_conv[0])
    ss = f(state_ssm[0]).reshape(128, DI, DS)
    sp = f(state_pool[0])
    in_maps = []
    for c in range(NCORES):
        m = dict(shared)
        m["xp"] = xpr[c]
        m["xs"] = xsr[c * NSB:(c + 1) * NSB]
        m["sconv"] = sc[c * NSB:(c + 1) * NSB]
        m["sssm"] = ss[c * NSB:(c + 1) * NSB]
        m["spool"] = sp[c * NSB:(c + 1) * NSB]
        in_maps.append(m)
    if "nc" not in _NC_CACHE:
        _NC_CACHE["nc"] = build_program()
    nc = _NC_CACHE["nc"]
    res = run_bass_kernel_spmd(nc, in_maps, core_ids=list(range(NCORES)))
    R = res.results
    y_prompt = np.stack([R[c]["yp"] for c in range(NCORES)])
    y_sample = np.concatenate([R[c]["ys"] for c in range(NCORES)]).reshape(128, 1, D)
    ncp_ = np.stack([R[c]["ncp"] for c in range(NCORES)])[None]
    nsp_ = np.stack([R[c]["nsp"].reshape(NH, HD, DS) for c in range(NCORES)])[None]
    npp_ = np.stack([R[c]["npp"] for c in range(NCORES)])[None]
    ncs_ = np.concatenate([R[c]["ncs"] for c in range(NCORES)])[None]
    nss_ = np.concatenate([R[c]["nss"] for c in range(NCORES)]).reshape(1, 128, NH, HD, DS)
    nps_ = np.concatenate([R[c]["nps"] for c in range(NCORES)])[None]
    return (y_prompt.astype(np.float32), y_sample.astype(np.float32), ncp_.astype(np.float32), nsp_.astype(np.float32),
            npp_.astype(np.float32), ncs_.astype(np.float32), nss_.astype(np.float32), nps_.astype(np.float32))
```

```python
import numpy as np
from contextlib import ExitStack
import concourse.bass as bass
import concourse.mybir as mybir
from concourse.bass_utils import run_bass_kernel_spmd

F32 = mybir.dt.float32
BF16 = mybir.dt.bfloat16
AF = mybir.ActivationFunctionType
ALU = mybir.AluOpType

NCORES = 8
D = 1024
DI = 2048
NH = 32
HD = 64
NG = 8
DS = 128
CD = 4096
DFF = 2816
SEQ = 2048
NSB = 16
IN_DIM = 9248
C_Z, C_XBC, C_DT, C_POOL, C_GA, C_GB = 0, 2048, 6144, 6176, 7200, 8224
EPS = 1e-6
NEG = -30000.0
POOLW = (2, 4, 8, 16)

CP_GPRE, CP_GFFN, CP_PS, CP_GSSM, CP_CW, CP_CB = 0, 8, 16, 24, 40, 168
NCOL = 200
RP_GPOST, RP_GFPOST, RP_DTB, RP_ALOG, RP_DSK = 0, 1024, 2048, 2080, 2112
NROW = 2144
CS_ID, CS_TRI, CS_NEG, CS_ONE, CS_INV = 0, 128, 256, 384, 512
NCST = 512 + 128


class Buf:
    __slots__ = ("name", "w", "r")

    def __init__(self, name=""):
        self.name = name
        self.w = None
        self.r = {}


class Eng:
    def __init__(self, name, sem):
        self.name = name
        self.sem = sem
        self.n = 0
        self.seen = {}
        self.prog = []
        self.hist = {}


class Sched:
    def __init__(self, nc, stack, n_dma_sems=40):
        self.nc = nc
        self.stack = stack
        self.fence_toks = []
        self.n_once = 0
        self.E = {}
        for nm in ("pe", "act", "dve", "pool", "sp"):
            self.E[nm] = Eng(nm, stack.enter_context(nc.semaphore("s_" + nm)))
        self.dsems = [stack.enter_context(nc.semaphore("d%d" % i)) for i in range(n_dma_sems)]
        self.dval = [0] * n_dma_sems
        self.dhist = {}
        self.dnext = 0
        self.count = 0
        self.budget = None
        self.skip = set()

    def _wait(self, eng, tok):
        key, val, sem = tok
        if eng.seen.get(key, 0) >= val:
            return
        eng.prog.append(("wait", sem, val))
        eng.seen[key] = val
        h = self.dhist.get((key, val)) if isinstance(key, tuple) else self.E[key].hist.get(val)
        if h:
            for k, v in h.items():
                if eng.seen.get(k, 0) < v:
                    eng.seen[k] = v

    def _deps(self, eng, reads, writes):
        toks = []
        for b in reads:
            if b.w is not None:
                toks.append(b.w)
        for b in writes:
            if b.w is not None:
                toks.append(b.w)
            toks.extend(b.r.values())
        for t in toks:
            if eng.name == "pe" and t[0] == "pe":
                continue
            self._wait(eng, t)

    def _update(self, tok, reads, writes):
        for b in reads:
            o = b.r.get(tok[0])
            if o is None or o[1] < tok[1]:
                b.r[tok[0]] = tok
        for b in writes:
            b.w = tok
            b.r = {}

    def op(self, en, fn, reads=(), writes=()):
        self.count += 1
        if (self.budget is not None and self.count > self.budget) or self.count in self.skip:
            return None
        eng = self.E[en]
        self._deps(eng, reads, writes)
        eng.n += 1
        eng.prog.append(("op", fn, (eng.sem, 1)))
        eng.hist[eng.n] = dict(eng.seen)
        tok = (en, eng.n, eng.sem)
        self._update(tok, reads, writes)
        return tok

    def dma(self, en, out, in_, reads=(), writes=(), wait_toks=(), fence=True, once=False, **kw):
        self.count += 1
        if self.budget is not None and self.count > self.budget:
            return None
        eng = self.E[en]
        for wt_ in wait_toks:
            if wt_ is not None:
                self._wait(eng, wt_)
        if once:
            self.n_once += 1
            sem = self.stack.enter_context(self.nc.semaphore("o%d" % self.n_once))
            key = ("o", self.n_once)
            self._deps(eng, reads, writes)
            val = 16
        else:
            i = self.dnext
            self.dnext = (self.dnext + 1) % len(self.dsems)
            key = ("d", i)
            sem = self.dsems[i]
            if self.dval[i] > 0:
                self._wait(eng, (key, self.dval[i], sem))
            self._deps(eng, reads, writes)
            self.dval[i] += 16
            val = self.dval[i]

        def fn(h, out=out, in_=in_, kw=kw):
            return h.dma_start(out=out, in_=in_, **kw)

        eng.prog.append(("op", fn, (sem, 16)))
        self.dhist[(key, val)] = dict(eng.seen)
        tok = (key, val, sem)
        self._update(tok, reads, writes)
        if fence:
            self.fence_toks.append(tok)
        else:
            self.loose_toks = getattr(self, "loose_toks", {})
            self.loose_toks[key] = tok
        return tok

    def barrier(self, fence_buf, fence_fn):
        return

    def finish(self):
        sp = self.E["sp"]
        for i, s in enumerate(self.dsems):
            if self.dval[i] > 0:
                self._wait(sp, (("d", i), self.dval[i], s))
        for tok in getattr(self, "loose_toks", {}).values():
            self._wait(sp, tok)
        for nm, e in self.E.items():
            if nm != "sp" and e.n > 0:
                self._wait(sp, (nm, e.n, e.sem))
        nc = self.nc
        with nc.Block() as block:
            def replay(eng):
                def run(h):
                    for item in eng.prog:
                        if item[0] == "wait":
                            h.wait_ge(item[1], item[2])
                        else:
                            ins = item[1](h)
                            ins.then_inc(item[2][0], item[2][1])
                return run
            block.tensor(replay(self.E["pe"]))
            block.scalar(replay(self.E["act"]))
            block.vector(replay(self.E["dve"]))
            block.gpsimd(replay(self.E["pool"]))
            block.sync(replay(self.E["sp"]))


class Arena:
    def __init__(self, t, nbytes):
        self.t = t
        self.nbytes = nbytes
        self.off = 0
        self.hist = []

    def reset(self, to=0):
        self.off = to

    def alloc(self, shape, dt, parts=128, at=None):
        esz = 4 if dt == F32 else 2
        n = 1
        for s in shape:
            n *= s
        nb = n * esz
        nb_al = (nb + 63) // 64 * 64
        base = self.off if at is None else at
        assert base + nb_al <= self.nbytes, ("arena overflow", base, nb_al, self.nbytes)
        lo, hi = base, base + nb_al
        a = self.t[0:parts, base // 2:(base + nb) // 2]
        if at is None:
            self.off += nb_al
        buf = Buf()
        keep = []
        for (o0, o1, ob) in self.hist:
            if o0 < hi and lo < o1:
                toks = list(ob.r.values())
                if ob.w is not None:
                    toks.append(ob.w)
                for tok in toks:
                    o = buf.r.get(tok[0])
                    if o is None or o[1] < tok[1]:
                        buf.r[tok[0]] = tok
                if lo <= o0 and o1 <= hi:
                    continue
            keep.append((o0, o1, ob))
        keep.append((lo, hi, buf))
        self.hist = keep
        if dt == F32:
            a = a.bitcast(F32)
        if len(shape) == 2:
            a = a.rearrange("p (a b) -> p a b", a=shape[0])
        elif len(shape) == 3:
            a = a.rearrange("p (a b c) -> p a b c", a=shape[0], b=shape[1])
        return a, buf


class _Stop(Exception):
    pass


S_ref = [None]


def build_program(stop=None, budget=None, verbose=False, skip=()):
    nc = bass.Bass("TRN2", target_bir_lowering=False)
    stage_ctr = [0]

    def checkpoint():
        stage_ctr[0] += 1
        if verbose:
            print("checkpoint", stage_ctr[0], "ops so far", S_ref[0].count)
        if stop is not None and stage_ctr[0] > stop:
            raise _Stop()

    def din(name, shape):
        return nc.dram_tensor(name, shape, F32, kind="ExternalInput").ap()

    def dout(name, shape):
        return nc.dram_tensor(name, shape, F32, kind="ExternalOutput").ap()

    xp = din("xp", [SEQ, D])
    xsm = din("xs", [NSB, D])
    sconv = din("sconv", [NSB, 3, CD])
    sssm = din("sssm", [NSB, DI, DS])
    spool = din("spool", [NSB, 15, D])
    w_in = din("w_in", [D, IN_DIM])
    w_pg = din("w_pg", [D, 256])
    w_a = din("w_a", [DI, D])
    w_b = din("w_b", [D, D])
    w_o = din("w_o", [D, D])
    w_fi = din("w_fi", [D, 2 * DFF])
    w_fo = din("w_fo", [DFF, D])
    colp = din("colp", [128, NCOL])
    rowp = din("rowp", [1, NROW])
    cst = din("cst", [128, NCST])
    selc = din("selc", [128, NSB * 128])

    yp = dout("yp", [SEQ, D])
    ysm = dout("ys", [NSB, D])
    ncp = dout("ncp", [3, CD])
    nsp = dout("nsp", [DI, DS])
    npp = dout("npp", [15, D])
    ncs = dout("ncs", [NSB, 3, CD])
    nss = dout("nss", [NSB, DI, DS])
    nps = dout("nps", [NSB, 15, D])

    def dscr(name, shape):
        return nc.dram_tensor(name, shape, BF16, kind="Internal").ap()

    sc_in = dscr("sc_in", [D, IN_DIM])
    sc_pg = dscr("sc_pg", [D, 256])
    sc_a = dscr("sc_a", [DI, D])
    sc_b = dscr("sc_b", [D, D])
    sc_o = dscr("sc_o", [D, D])
    sc_fi = dscr("sc_fi", [D, 2 * DFF])
    sc_fo = dscr("sc_fo", [DFF, D])

    with ExitStack() as st:
        S = Sched(nc, st)
        S.budget = budget
        S.skip = set(skip)
        S_ref[0] = S

        def sbuf(name, shape, dt):
            return st.enter_context(nc.sbuf_tensor(name, shape, dt))

        cstf = sbuf("cstf", [128, NCST], F32)
        colt = sbuf("colt", [128, NCOL], F32)
        rowbc = sbuf("rowbc", [128, NROW], F32)
        identb = sbuf("identb", [128, 128], BF16)
        neg4b = sbuf("neg4b", [128, 4, 128], BF16)
        diagD = sbuf("diagD", [128, NH, 128], BF16)
        abc = sbuf("abc", [128, NH], F32)
        mhalf = sbuf("mhalf", [128, 1], F32)
        wdt = sbuf("wdt", [128, 8, 32], BF16)
        wpg = sbuf("wpg", [128, 8, 256], BF16)
        hT = sbuf("hT", [128, DI], F32)
        hTb = sbuf("hTb", [128, DI], BF16)
        histc = sbuf("histc", [128, 32, 3], F32)
        histp = sbuf("histp", [128, 8, 15], F32)
        junk_t = sbuf("junk", [128, 3, 1024], BF16)
        junk_bufs = [Buf("junk%d" % i) for i in range(3)]
        junk_n = [0]

        def JK():
            i = junk_n[0] % 3
            junk_n[0] += 1
            return junk_t[:, i, :], junk_bufs[i]
        fence = sbuf("fence", [128, 1], F32)
        NSLOT = 4
        slots = [sbuf("wslot%d" % i, [128, 8, 512], BF16) for i in range(NSLOT)]
        slot_bufs = [Buf("slot%d" % i) for i in range(NSLOT)]
        R1B, R2B, R3B = 40 * 1024, 50 * 1024, 40 * 1024
        R1 = Arena(sbuf("R1", [128, R1B // 2], BF16), R1B)
        R2 = Arena(sbuf("R2", [128, R2B // 2], BF16), R2B)
        R3 = Arena(sbuf("R3", [128, R3B // 2], BF16), R3B)

        identf = cstf[:, CS_ID:CS_ID + 128]
        trif = cstf[:, CS_TRI:CS_TRI + 128]
        negf = cstf[:, CS_NEG:CS_NEG + 128]
        onesf = cstf[:, CS_ONE:CS_ONE + 128]
        invc = cstf[:, CS_INV:CS_INV + 128].rearrange("p (c k) -> p c k", c=8)
        gpost_bc = rowbc[:, RP_GPOST:RP_GPOST + D]
        gfpost_bc = rowbc[:, RP_GFPOST:RP_GFPOST + D]
        dtb_bc = rowbc[:, RP_DTB:RP_DTB + NH]
        alog_bc = rowbc[:, RP_ALOG:RP_ALOG + NH]
        dsk_bc = rowbc[:, RP_DSK:RP_DSK + NH]

        B_const = Buf("const")
        B_wres, B_wres2 = Buf("wdt"), Buf("wpg")
        B_diag = Buf("diagD")
        B_hT, B_hTb, B_histc, B_histp, B_fence = Buf(), Buf(), Buf(), Buf(), Buf()
        B_in = Buf("inputs")

        psb = [st.enter_context(nc.psum_tensor("ps%d" % i, [128, 512], F32)) for i in range(8)]
        psbuf = [Buf("ps%d" % i) for i in range(8)]
        psn = [0]

        def PS():
            i = psn[0] % 8
            psn[0] += 1
            return psb[i], psbuf[i]

        def act(out, in_, func, reads, writes, **kw):
            return S.op("act", lambda h: h.activation(out=out, in_=in_, func=func, **kw), reads, writes)

        def act_sq(sl, in_, reads, writes, accum_out):
            jk, B_jk = JK()
            parts = in_.shape[0]
            n = 1
            for d_ in in_.shape[1:]:
                n *= d_
            o = jk[0:parts, 0:n]
            if len(in_.shape) == 3:
                o = o.rearrange("p (a b) -> p a b", a=in_.shape[1])
            return act(o, in_, AF.Square, list(reads), list(writes) + [B_jk], accum_out=accum_out)

        def tt(en, out, in0, in1, op, reads, writes):
            return S.op(en, lambda h: h.tensor_tensor(out=out, in0=in0, in1=in1, op=op), reads, writes)

        def ts(en, out, in0, s1, s2, op0, op1, reads, writes):
            if s2 is None:
                return S.op(en, lambda h: h.tensor_scalar(out=out, in0=in0, scalar1=s1, scalar2=None, op0=op0), reads, writes)
            return S.op(en, lambda h: h.tensor_scalar(out=out, in0=in0, scalar1=s1, scalar2=s2, op0=op0, op1=op1), reads, writes)

        def stt(en, out, in0, sc, in1, op0, op1, reads, writes):
            return S.op(en, lambda h: h.scalar_tensor_tensor(out=out, in0=in0, scalar=sc, in1=in1, op0=op0, op1=op1), reads, writes)

        def cp(en, out, in_, reads, writes):
            return S.op(en, lambda h: h.tensor_copy(out=out, in_=in_), reads, writes)

        def mmgroup(pairs, reads, writes):
            def fn(h):
                ins = None
                for (o, l, r, s0, s1) in pairs:
                    ins = h.matmul(o, lhsT=l, rhs=r, start=s0, stop=s1, skip_group_check=True)
                return ins
            return S.op("pe", fn, reads, writes)

        def trgroup(items, ident, reads, writes):
            def fn(h):
                ins = None
                for (o, i_) in items:
                    ins = h.transpose(out=o, in_=i_, identity=ident)
                return ins
            return S.op("pe", fn, reads, writes)

        def rstd_from(ssq_ap, rstd_ap, n, bufs_r, buf_w):
            ts("pool", rstd_ap, ssq_ap, 1.0 / n, EPS, ALU.mult, ALU.add, bufs_r, [buf_w])
            tt("pool", rstd_ap, rstd_ap, mhalf[0:rstd_ap.shape[0], :], ALU.pow, [buf_w, B_const], [buf_w])

        def do_barrier():
            S.barrier(B_fence, lambda h: h.memset(fence[:], 0.0))
            R1.reset()
            R2.reset()
            R3.reset()

        S.dma("sp", cstf[:], cst[:, :], writes=[B_const])
        S.dma("sp", colt[:], colp[:, :], writes=[B_const])
        S.dma("sp", rowbc[:], rowp.partition_broadcast(128).rearrange("p a n -> p (a n)"), writes=[B_const])
        cp("dve", identb[:], identf, [B_const], [B_const])
        cp("dve", neg4b[:], negf.unsqueeze(1).to_broadcast([128, 4, 128]), [B_const], [B_const])
        S.op("pool", lambda h: h.memset(mhalf[:], -0.5), writes=[B_const])
        act(abc[:], alog_bc, AF.Exp, [B_const], [B_const])
        ts("dve", abc[:], abc[:], -1.0, None, ALU.mult, None, [B_const], [B_const])
        for hh in range(NH):
            ts("dve", diagD[:, hh, :], identf, dsk_bc[:, hh:hh + 1], None, ALU.mult, None, [B_const], [B_diag] if hh in (0, NH - 1) else [])
        S.op("dve", lambda h: h.memset(hT[:], 0.0), writes=[B_hT])
        S.op("pool", lambda h: h.memset(hTb[:], 0.0), writes=[B_hTb])
        S.op("dve", lambda h: h.memset(histc[:], 0.0), writes=[B_histc])
        S.op("pool", lambda h: h.memset(histp[:], 0.0), writes=[B_histp])

        class WB:
            pass

        scr_uid = [0]

        def mk_blocks(src, scr, K, c0, c1, cw=512):
            out = []
            nkc = K // 128
            for k0 in range(0, nkc, 8):
                nk = min(8, nkc - k0)
                row = []
                for cc in range(c0, c1, cw):
                    b = WB()
                    b.src, b.scr = src, scr
                    b.r0, b.r1 = k0 * 128, (k0 + nk) * 128
                    b.c0, b.c1 = cc, min(cc + cw, c1)
                    b.nk, b.nc = nk, b.c1 - b.c0
                    b.buf = Buf()
                    scr_uid[0] += 1
                    b.scr_t = nc.dram_tensor("wt%d" % scr_uid[0], [128, b.nk, b.nc], BF16, kind="Internal").ap()
                    row.append(b)
                out.append(row)
            return out

        W_xbc = mk_blocks(w_in, sc_in, D, C_XBC, C_DT)[0]
        W_z = mk_blocks(w_in, sc_in, D, C_Z, C_XBC)[0]
        W_dt = mk_blocks(w_in, sc_in, D, C_DT, C_POOL)[0]
        W_pool = mk_blocks(w_in, sc_in, D, C_POOL, C_GA)[0]
        W_ga = mk_blocks(w_in, sc_in, D, C_GA, C_GB)[0]
        W_gb = mk_blocks(w_in, sc_in, D, C_GB, IN_DIM)[0]
        W_pg = mk_blocks(w_pg, sc_pg, D, 0, 256)[0]
        W_a = mk_blocks(w_a, sc_a, DI, 0, D)
        W_b = mk_blocks(w_b, sc_b, D, 0, D)[0]
        W_o = mk_blocks(w_o, sc_o, D, 0, D)[0]
        W_fg = mk_blocks(w_fi, sc_fi, D, 0, DFF)[0]
        W_fu = mk_blocks(w_fi, sc_fi, D, DFF, 2 * DFF)[0]
        W_fo = mk_blocks(w_fo, sc_fo, DFF, 0, D)

        cast_order = W_dt + W_pg + W_xbc + W_z + W_pool + W_ga + W_gb
        for cbk in range(2):
            cast_order += [W_a[0][cbk], W_a[1][cbk], W_b[cbk]]
        cast_order += W_o
        for i in range(len(W_fg)):
            cast_order += [W_fg[i], W_fu[i]]
        for cbk in range(2):
            cast_order += [W_fo[0][cbk], W_fo[1][cbk], W_fo[2][cbk]]
        for i, b in enumerate(cast_order):
            b.cast_idx = i
        cast_toks = []

        def ensure_cast(n):
            n = min(n, len(cast_order))
            while len(cast_toks) < n:
                b = cast_order[len(cast_toks)]
                prev = cast_toks[-8] if len(cast_toks) >= 8 else None
                cast_toks.append(S.dma("pool", b.scr_t[:, :, :], b.src[b.r0:b.r1, b.c0:b.c1].rearrange("(k p) f -> p k f", p=128),
                                       reads=[B_in], writes=[b.buf],
                                       wait_toks=[prev], fence=False, once=True))

        ensure_cast(6)
        if stop is not None and stop <= 0:
            S.finish()
            return nc

        slot_n = [0]

        w_prefetch = []

        def prefetch_w(b):
            sl, sb_ = get_w(b, _direct=True)
            w_prefetch.append((b, sl, sb_))

        def get_w(b, _direct=False):
            if not _direct and w_prefetch:
                pb, sl, sb_ = w_prefetch.pop(0)
                assert pb is b, "weight prefetch order mismatch"
                return sl, sb_
            ensure_cast(b.cast_idx + 10)
            i = slot_n[0] % NSLOT
            slot_n[0] += 1
            sl, sb_ = slots[i], slot_bufs[i]
            S.dma("sp", sl[:, 0:b.nk, 0:b.nc], b.scr_t[:, :, :],
                  reads=[b.buf], writes=[sb_], fence=False, max_dma_last_dim=2048)
            return sl, sb_

        S.dma("sp", wdt[:], W_dt[0].scr_t[:, :, :], reads=[W_dt[0].buf], writes=[B_wres])
        S.dma("sp", wpg[:], W_pg[0].scr_t[:, :, :], reads=[W_pg[0].buf], writes=[B_wres2])

        gpre_c = colt[:, CP_GPRE:CP_GPRE + 8]
        gffn_c = colt[:, CP_GFFN:CP_GFFN + 8]
        ps_c = colt[:, CP_PS:CP_PS + 8]
        gssm_c = colt[:, CP_GSSM:CP_GSSM + 16]
        cw_c = colt[:, CP_CW:CP_CW + 128].rearrange("p (c k) -> p c k", c=32)
        cb_c = colt[:, CP_CB:CP_CB + 32]

        def pipeline(items, stages, lag=1):
            n, ns = len(items), len(stages)
            for step in range(n + (ns - 1) * lag):
                for k in range(ns - 1, -1, -1):
                    i = step - k * lag
                    if 0 <= i < n and stages[k] is not None:
                        stages[k](items[i])

        class Ctx:
            pass

        pre_xT = [None]
        A_R2_BASE = 25 * 1024 + 512
        A_R3_XT = 32 * 1024

        def stageA_begin(mode, blk):
            prompt = mode == "prompt"
            T = 128 if prompt else NSB
            NT = 4 if prompt else 1
            TB = T * NT
            tok0 = blk * TB
            x_src = xp if prompt else xsm
            c = Ctx()
            c.T, c.NT = T, NT
            c.xT, c.B_xT = R3.alloc([8, TB], BF16, at=A_R3_XT)
            off = A_R2_BASE
            a_xt, a_xn, a_sq = [], [], []
            for i in range(NT):
                a_xt.append(R2.alloc([D], F32, parts=T, at=off))
                off += 4096
            for i in range(NT):
                a_xn.append(R2.alloc([D], BF16, parts=T, at=off))
                off += 2048
            for i in range(NT):
                a_sq.append(R2.alloc([2], F32, parts=T, at=off))
                off += 64
            c.xn = a_xn
            for t in range(NT):
                xt, B_xt = a_xt[t]
                xn, B_xn = a_xn[t]
                sq, B_sq = a_sq[t]
                S.dma("sp", xt, x_src[tok0 + t * T: tok0 + (t + 1) * T, :], reads=[B_in], writes=[B_xt])
                act_sq('', xt, [B_xt], [B_sq], accum_out=sq[:, 0:1])
                rstd_from(sq[:, 0:1], sq[:, 1:2], D, [B_sq], B_sq)
                act(xn, xt, AF.Copy, [B_xt, B_sq], [B_xn], scale=sq[:, 1:2])
            return c

        def stageA_finish(c):
            T, NT = c.T, c.NT
            for t in range(NT):
                xn, B_xn = c.xn[t]
                pt, B_pt = PS()
                ptb = pt[:].bitcast(BF16).rearrange("p (c t) -> p c t", c=8)
                trgroup([(ptb[:, cc, 0:T], xn[:, cc * 128:(cc + 1) * 128]) for cc in range(8)], identb[0:T, 0:T], [B_xn, B_const], [B_pt])
                tt("dve", c.xT[:, :, t * T:(t + 1) * T], ptb[:, :, 0:T], gpre_c.unsqueeze(2).to_broadcast([128, 8, T]), ALU.mult,
                   [B_pt, B_const], [c.B_xT])

        def run_block(mode, blk):
            prompt = mode == "prompt"
            T = 128 if prompt else NSB
            NT = 4 if prompt else 1
            TB = T * NT
            tok0 = blk * TB
            x_src = xp if prompt else xsm
            y_dst = yp if prompt else ysm
            first = prompt and blk == 0
            last = prompt and blk == SEQ // 512 - 1

            do_barrier()
            szb, B_szb = R3.alloc([NT, DI], BF16, parts=T)
            yaT, B_yaT = R3.alloc([16, TB], BF16)
            if pre_xT[0] is not None:
                xT, B_xT = pre_xT[0]
                pre_xT[0] = None
            else:
                actx = stageA_begin(mode, blk)
                stageA_finish(actx)
                xT, B_xT = actx.xT, actx.B_xT
            dtv = None
            if prompt:
                dtv = [R2.alloc([NT, NH], F32) for _ in range(12)]
            r2_mark = R2.off
            o3s = [R2.alloc([512], F32) for _ in range(1)]

            checkpoint()
            dtc = None
            if prompt:
                dtc = dt_chain(NT, T, xT, B_xT, dtv)
            xs_tok, B_xs = R1.alloc([NT, DI], BF16, parts=T)
            b_tok, B_bt = R1.alloc([NT, NG * DS], BF16, parts=T)
            bT, B_bT = R1.alloc([NG, TB], BF16)
            cT, B_cT = R1.alloc([NG, TB], BF16)
            if not prompt:
                sct, B_sct = R3.alloc([CD], F32, parts=T)
                scT, B_scT = R2.alloc([3, 32, T], F32)
                uraw, B_uraw = R2.alloc([32, T], F32)
                for k in range(3):
                    S.dma("sp", sct, sconv[:, k, :], reads=[B_in, B_fence], writes=[B_sct])
                    for c0 in range(0, 32, 16):
                        pt, B_pt = PS()
                        ptv = pt[:, 0:16 * T].rearrange("p (c t) -> p c t", c=16)
                        trgroup([(ptv[:, c, :], sct[:, (c0 + c) * 128:(c0 + c + 1) * 128]) for c in range(16)],
                                identf[0:T, 0:T], [B_sct, B_const], [B_pt])
                        cp("dve", scT[:, k, c0:c0 + 16, :], ptv, [B_pt], [B_scT])
            ub = [R2.alloc([TB + 3], F32) for _ in range(5)]
            accs = [R2.alloc([TB], F32) for _ in range(5)]
            xfm = [R2.alloc([TB], BF16) for _ in range(2)]
            items = []
            for wi, wb in enumerate(W_xbc):
                for cc in range(4):
                    it = Ctx()
                    it.c, it.cc, it.wb = wi * 4 + cc, cc, wb
                    items.append(it)
            cur_slot = [None]

            def sB0(it):
                if it.cc == 0:
                    cur_slot[0] = get_w(it.wb)
                sl, B_sl = cur_slot[0]
                it.ps, it.B_ps = PS()
                mmgroup([(it.ps[:, 0:TB], sl[:, k, it.cc * 128:(it.cc + 1) * 128], xT[:, k, :], k == 0, k == 7) for k in range(8)],
                        [B_sl, B_xT], [it.B_ps])

            def sB1(it):
                c = it.c
                it.u, it.B_u = ub[c % 5]
                it.acc, it.B_acc = accs[c % 5]
                if prompt:
                    cp("pool", it.u[:, 0:3], histc[:, c, :], [B_histc], [it.B_u])
                    act(it.u[:, 3:3 + TB], it.ps[:, 0:TB], AF.Copy, [it.B_ps], [it.B_u])
                    cp("pool", histc[:, c, :], it.u[:, TB:TB + 3], [it.B_u], [B_histc])
                    it.taps = [it.u[:, k:k + TB] for k in range(4)]
                    it.tap_r = [it.B_u]
                else:
                    act(uraw[:, c, :], it.ps[:, 0:TB], AF.Copy, [it.B_ps], [B_uraw])
                    it.taps = [scT[:, 0, c, :], scT[:, 1, c, :], scT[:, 2, c, :], uraw[:, c, :]]
                    it.tap_r = [B_scT, B_uraw]

            def mk_tap(k):
                def f(it):
                    c = it.c
                    if k == 0:
                        act(it.acc, it.taps[0], AF.Identity, it.tap_r + [B_const], [it.B_acc], scale=cw_c[:, c, 0:1], bias=cb_c[:, c:c + 1])
                    else:
                        stt("dve", it.acc, it.taps[k], cw_c[:, c, k:k + 1], it.acc, ALU.mult, ALU.add, it.tap_r + [B_const, it.B_acc], [it.B_acc])
                return f

            def sB6(it):
                c = it.c
                if c < 16:
                    xf, B_xf = xfm[c % 2]
                    act(xf, it.acc, AF.Silu, [it.B_acc], [B_xf])
                    it.tr = (xf, B_xf, xs_tok, B_xs, c * 128)
                elif c < 24:
                    act(bT[:, c - 16, :], it.acc, AF.Silu, [it.B_acc], [B_bT])
                    it.tr = (bT[:, c - 16, :], B_bT, b_tok, B_bt, (c - 16) * 128)
                else:
                    act(cT[:, c - 24, :], it.acc, AF.Silu, [it.B_acc], [B_cT])
                    it.tr = None

            def sB7(it):
                if it.tr is None:
                    return
                src, B_src, dst, B_dst, col = it.tr
                it.pt, it.B_pt = PS()
                it.ptb = it.pt[:].bitcast(BF16)[0:T, 0:NT * 128].rearrange("p (t f) -> p t f", t=NT)
                trgroup([(it.ptb[:, t, :], src[:, t * T:(t + 1) * T]) for t in range(NT)], identb[:], [B_src, B_const], [it.B_pt])

            def sB8(it):
                if it.tr is None:
                    return
                src, B_src, dst, B_dst, col = it.tr
                act(dst[:, :, col:col + 128], it.ptb, AF.Copy, [it.B_pt], [B_dst])

            pipeline(items, [sB0, sB1, mk_tap(0), mk_tap(1), mk_tap(2), mk_tap(3), sB6, sB7, sB8])
            if prompt:
                dt_chain2(NT, dtv, *dtc)
            for q, wb in enumerate(W_z):
                sl, B_sl = get_w(wb)
                for t in range(NT):
                    ps_, B_ps = PS()
                    mmgroup([(ps_[0:T, :], xT[:, k, t * T:(t + 1) * T], sl[:, k, :], k == 0, k == 7) for k in range(8)],
                            [B_sl, B_xT], [B_ps])
                    act(szb[:, t, q * 512:(q + 1) * 512], ps_[0:T, :], AF.Silu, [B_ps], [B_szb])
            if last:
                for c0 in range(0, 32, 4):
                    pt, B_pt = PS()
                    trgroup([(pt[0:3, c * 128:(c + 1) * 128], histc[:, c0 + c, :]) for c in range(4)], identf, [B_histc, B_const], [B_pt])
                    o3, B_o3 = o3s[0]
                    o3 = o3[0:3, :]
                    cp("dve", o3, pt[0:3, :], [B_pt], [B_o3])
                    S.dma("sp", ncp[:, c0 * 128:(c0 + 4) * 128], o3, reads=[B_o3], writes=[Buf()])
            if not prompt:
                S.dma("sp", ncs[:, 0:2, :], sconv[:, 1:3, :], reads=[B_in], writes=[Buf()])
                for c0 in range(0, 32, 4):
                    pt, B_pt = PS()
                    trgroup([(pt[0:T, c * 128:(c + 1) * 128], uraw[:, c0 + c, :]) for c in range(4)], identf, [B_uraw, B_const], [B_pt])
                    o3, B_o3 = o3s[0]
                    o3 = o3[0:T, :]
                    cp("dve", o3, pt[0:T, :], [B_pt], [B_o3])
                    S.dma("sp", ncs[:, 2, c0 * 128:(c0 + 4) * 128], o3, reads=[B_o3], writes=[Buf()])

            checkpoint()
            S.barrier(B_fence, lambda h: h.memset(fence[:], 0.0))
            R2.reset(r2_mark)
            if prompt:
                ssd_prompt(T, NT, TB, xT, B_xT, xs_tok, B_xs, b_tok, B_bt, bT, B_bT, cT, B_cT, szb, B_szb, yaT, B_yaT, last, dtv)
            else:
                ssd_sample(T, xT, B_xT, xs_tok, B_xs, b_tok, B_bt, bT, B_bT, cT, B_cT, szb, B_szb, yaT, B_yaT)

            checkpoint()
            S.barrier(B_fence, lambda h: h.memset(fence[:], 0.0))
            R1.reset()
            R2.reset()
            dT, B_dT = R1.alloc([8, TB], BF16)
            ybT, B_ybT = R1.alloc([8, TB], BF16)
            mT, B_mT = R1.alloc([8, TB], BF16)
            sga, B_sga = R1.alloc([8, TB], BF16)
            sgb, B_sgb = R1.alloc([8, TB], BF16)
            if prompt:
                ubp = [R2.alloc([TB + 15], F32) for _ in range(2)]
                sA = [R2.alloc([TB + 15], F32) for _ in range(2)]
                sB = [R2.alloc([TB + 15], F32) for _ in range(2)]
                t16 = [R2.alloc([16], F32) for _ in range(2)]
                ci = 0
                for wb in W_pool:
                    sl, B_sl = get_w(wb)
                    for cc in range(4):
                        c = ci
                        ci += 1
                        u, B_u = ubp[c % 2]
                        a_, B_a = sA[c % 2]
                        b_, B_b = sB[c % 2]
                        ps_, B_ps = PS()
                        mmgroup([(ps_[:, 0:TB], sl[:, k, cc * 128:(cc + 1) * 128], xT[:, k, :], k == 0, k == 7) for k in range(8)],
                                [B_sl, B_xT], [B_ps])
                        cp("pool", u[:, 0:15], histp[:, c, :], [B_histp], [B_u])
                        act(u[:, 15:15 + TB], ps_[:, 0:TB], AF.Copy, [B_ps], [B_u])
                        cp("pool", histp[:, c, :], u[:, TB:TB + 15], [B_u], [B_histp])
                        w = POOLW[c // 2]
                        L = TB + 15
                        tt("dve", a_[:, 1:L], u[:, 1:L], u[:, 0:L - 1], ALU.add, [B_u], [B_a])
                        cur, B_cur, oth, B_oth = a_, B_a, b_, B_b
                        sh = 2
                        while sh < w:
                            lo = 2 * sh - 1
                            tt("dve", oth[:, lo:L], cur[:, lo:L], cur[:, lo - sh:L - sh], ALU.add, [B_cur], [B_oth])
                            cur, B_cur, oth, B_oth = oth, B_oth, cur, B_cur
                            sh *= 2
                        stt("dve", dT[:, c, :], cur[:, 15:L], 1.0 / w, u[:, 15:L], ALU.mult, ALU.subtract, [B_cur, B_u], [B_dT])
                        if first:
                            tq, B_tq = t16[c % 2]
                            tt("dve", tq, cur[:, 15:31], invc[:, c, :], ALU.mult, [B_cur, B_const], [B_tq])
                            tt("dve", dT[:, c, 0:16], tq, u[:, 15:31], ALU.subtract, [B_tq, B_u, B_dT], [B_dT])
                if last:
                    for c0 in range(0, 8, 4):
                        pt, B_pt = PS()
                        trgroup([(pt[0:15, c * 128:(c + 1) * 128], histp[:, c0 + c, :]) for c in range(4)], identf, [B_histp, B_const], [B_pt])
                        o3, B_o3 = R2.alloc([512], F32, parts=15)
                        cp("dve", o3, pt[0:15, :], [B_pt], [B_o3])
                        S.dma("sp", npp[:, c0 * 128:(c0 + 4) * 128], o3, reads=[B_o3], writes=[Buf()])
            else:
                spt, B_spt = R2.alloc([15, 256], F32, parts=T)
                ut, B_ut = R2.alloc([D], F32, parts=T)
                sm, B_sm = R2.alloc([D], F32, parts=T)
                dtk, B_dtk = R2.alloc([D], BF16, parts=T)
                for q, wb in enumerate(W_pool):
                    sl, B_sl = get_w(wb)
                    ps_, B_ps = PS()
                    mmgroup([(ps_[0:T, :], xT[:, k, :], sl[:, k, :], k == 0, k == 7) for k in range(8)], [B_sl, B_xT], [B_ps])
                    cp("dve", ut[:, q * 512:(q + 1) * 512], ps_[0:T, :], [B_ps], [B_ut])
                S.dma("sp", nps[:, 0:14, :], spool[:, 1:15, :], reads=[B_in], writes=[Buf()])
                S.dma("sp", nps[:, 14, :], ut, reads=[B_ut], writes=[Buf()])
                for gi, w in enumerate(POOLW):
                    fs = slice(gi * 256, (gi + 1) * 256)
                    S.dma("sp", spt[:, 0:w - 1, :], spool[:, 16 - w:15, fs], reads=[B_in, B_fence], writes=[B_spt])
                    tt("dve", sm[:, fs], spt[:, 0, :], ut[:, fs], ALU.add, [B_spt, B_ut], [B_sm])
                    for kk in range(1, w - 1):
                        tt("dve", sm[:, fs], sm[:, fs], spt[:, kk, :], ALU.add, [B_spt, B_sm], [B_sm])
                    stt("dve", dtk[:, fs], sm[:, fs], 1.0 / w, ut[:, fs], ALU.mult, ALU.subtract, [B_sm, B_ut], [B_dtk])
                pt, B_pt = PS()
                ptb = pt[:].bitcast(BF16)[:, 0:8 * T].rearrange("p (c t) -> p c t", c=8)
                trgroup([(ptb[:, c, :], dtk[:, c * 128:(c + 1) * 128]) for c in range(8)], identb[0:T, 0:T], [B_dtk, B_const], [B_pt])
                cp("dve", dT, ptb, [B_pt], [B_dT])
            for (wl, dst, B_dst) in ((W_ga, sga, B_sga), (W_gb, sgb, B_sgb)):
                ci = 0
                for wb in wl:
                    sl, B_sl = get_w(wb)
                    for cc in range(4):
                        ps_, B_ps = PS()
                        mmgroup([(ps_[:, 0:TB], sl[:, k, cc * 128:(cc + 1) * 128], xT[:, k, :], k == 0, k == 7) for k in range(8)],
                                [B_sl, B_xT], [B_ps])
                        act(dst[:, ci, :], ps_[:, 0:TB], AF.Sigmoid, [B_ps], [B_dst])
                        ci += 1
            for oc in range(8):
                gi = oc // 2
                ps_, B_ps = PS()
                mmgroup([(ps_[:, 0:TB], wpg[:, gi * 2 + kc, (oc % 2) * 128:(oc % 2 + 1) * 128], dT[:, gi * 2 + kc, :], kc == 0, kc == 1)
                         for kc in range(2)], [B_const, B_wres2, B_dT], [B_ps])
                act(ybT[:, oc, :], ps_[:, 0:TB], AF.Copy, [B_ps, B_const], [B_ybT], scale=ps_c[:, oc:oc + 1])
            t1s = [R2.alloc([TB], F32) for _ in range(2)]
            t2s = [R2.alloc([TB], F32) for _ in range(2)]
            for cbk in range(2):
                sa0, B_sa0 = get_w(W_a[0][cbk])
                sa1, B_sa1 = get_w(W_a[1][cbk])
                sbb, B_sbb = get_w(W_b[cbk])
                for cc in range(4):
                    oc = cbk * 4 + cc
                    t1, B_t1 = t1s[oc % 2]
                    t2, B_t2 = t2s[oc % 2]
                    pa, B_pa = PS()
                    mmgroup([(pa[:, 0:TB], (sa0 if k < 8 else sa1)[:, k % 8, cc * 128:(cc + 1) * 128], yaT[:, k, :], k == 0, k == 15)
                             for k in range(16)], [B_sa0, B_sa1, B_yaT], [B_pa])
                    tt("dve", t1, pa[:, 0:TB], sga[:, oc, :], ALU.mult, [B_pa, B_sga], [B_t1])
                    pb, B_pb = PS()
                    mmgroup([(pb[:, 0:TB], sbb[:, k, cc * 128:(cc + 1) * 128], ybT[:, k, :], k == 0, k == 7) for k in range(8)],
                            [B_sbb, B_ybT], [B_pb])
                    tt("dve", t2, pb[:, 0:TB], sgb[:, oc, :], ALU.mult, [B_pb, B_sgb], [B_t2])
                    tt("pool", mT[:, oc, :], t1, t2, ALU.add, [B_t1, B_t2], [B_mT])

            checkpoint()
            S.barrier(B_fence, lambda h: h.memset(fence[:], 0.0))
            R2.reset()
            R3.reset()
            htl = [R3.alloc([D], F32, parts=T) for _ in range(NT)]
            hnT, B_hnT = R3.alloc([8, TB], BF16)
            so0, B_so0 = get_w(W_o[0])
            so1, B_so1 = get_w(W_o[1])
            fitems = []
            for t in range(NT):
                it = Ctx()
                it.t = t
                it.ht, it.B_ht = htl[t]
                it.sq, it.B_sq = R2.alloc([4], F32, parts=T)
                it.tmp, it.B_tmp = R2.alloc([D], F32, parts=T)
                it.sq2, it.B_sq2 = R2.alloc([2], F32, parts=T)
                it.hn, it.B_hn = R2.alloc([D], BF16, parts=T)
                fitems.append(it)
            so_l = ((so0, B_so0), (so1, B_so1))

            def sF0(it):
                t = it.t
                S.dma("sp", it.ht, x_src[tok0 + t * T: tok0 + (t + 1) * T, :], reads=[B_in, B_fence], writes=[it.B_ht])
                it.po = []
                for cbk, (so, B_so) in enumerate(so_l):
                    po, B_po = PS()
                    mmgroup([(po[0:T, :], mT[:, k, t * T:(t + 1) * T], so[:, k, :], k == 0, k == 7) for k in range(8)],
                            [B_so, B_mT], [B_po])
                    it.po.append((po, B_po))

            def sF1(it):
                for cbk, (po, B_po) in enumerate(it.po):
                    act_sq('0:T, 0:512', po[0:T, :], [B_po], [it.B_sq], accum_out=it.sq[:, cbk:cbk + 1])
                    tt("dve", it.tmp[:, cbk * 512:(cbk + 1) * 512], po[0:T, :], gpost_bc[0:T, cbk * 512:(cbk + 1) * 512], ALU.mult,
                       [B_po, B_const, it.B_sq], [it.B_tmp])

            def sF2(it):
                tt("pool", it.sq[:, 2:3], it.sq[:, 0:1], it.sq[:, 1:2], ALU.add, [it.B_sq], [it.B_sq])
                rstd_from(it.sq[:, 2:3], it.sq[:, 3:4], D, [it.B_sq], it.B_sq)
                stt("dve", it.ht, it.tmp, it.sq[:, 3:4], it.ht, ALU.mult, ALU.add, [it.B_tmp, it.B_sq, it.B_ht], [it.B_ht])

            def sF3(it):
                act_sq('0:T, 0:D', it.ht, [it.B_ht], [it.B_sq2], accum_out=it.sq2[:, 0:1])

            def sF4(it):
                rstd_from(it.sq2[:, 0:1], it.sq2[:, 1:2], D, [it.B_sq2], it.B_sq2)
                act(it.hn, it.ht, AF.Copy, [it.B_ht, it.B_sq2], [it.B_hn], scale=it.sq2[:, 1:2])

            def sF5(it):
                it.pt, it.B_pt = PS()
                it.ptb = it.pt[:].bitcast(BF16).rearrange("p (c t) -> p c t", c=8)
                trgroup([(it.ptb[:, c, 0:T], it.hn[:, c * 128:(c + 1) * 128]) for c in range(8)], identb[0:T, 0:T], [it.B_hn, B_const], [it.B_pt])

            def sF6(it):
                t = it.t
                tt("dve", hnT[:, :, t * T:(t + 1) * T], it.ptb[:, :, 0:T], gffn_c.unsqueeze(2).to_broadcast([128, 8, T]), ALU.mult,
                   [it.B_pt, B_const], [B_hnT])

            pipeline(fitems, [sF0, sF1, sF2, sF3, sF4, sF5, sF6])
            checkpoint()
            R2.reset()
            R1.reset()
            S.barrier(B_fence, lambda h: h.memset(fence[:], 0.0))
            actT, B_actT = R1.alloc([22, TB], BF16)
            sgs = [R2.alloc([TB], F32) for _ in range(2)]
            fc = 0
            for i in range(len(W_fg)):
                sg_, B_sg_ = get_w(W_fg[i])
                su_, B_su_ = get_w(W_fu[i])
                for cc in range(W_fg[i].nc // 128):
                    sgt, B_sgt = sgs[fc % 2]
                    pg, B_pg = PS()
                    mmgroup([(pg[:, 0:TB], sg_[:, k, cc * 128:(cc + 1) * 128], hnT[:, k, :], k == 0, k == 7) for k in range(8)],
                            [B_sg_, B_hnT], [B_pg])
                    act(sgt, pg[:, 0:TB], AF.Silu, [B_pg], [B_sgt])
                    pu, B_pu = PS()
                    mmgroup([(pu[:, 0:TB], su_[:, k, cc * 128:(cc + 1) * 128], hnT[:, k, :], k == 0, k == 7) for k in range(8)],
                            [B_su_, B_hnT], [B_pu])
                    tt("dve", actT[:, fc, :], sgt, pu[:, 0:TB], ALU.mult, [B_sgt, B_pu], [B_actT])
                    fc += 1
            fsb = [R2.alloc([D], F32, parts=T) for _ in range(NT)]
            sq3 = [R2.alloc([4], F32, parts=T) for _ in range(NT)]
            nxt = None
            if prompt:
                nxt = ("prompt", blk + 1) if blk + 1 < SEQ // 512 else ("sample", 0)
            nctx = None
            for cbk in range(2):
                sws = [get_w(W_fo[kg][cbk]) for kg in range(3)]
                if cbk == 0 and nxt is not None:
                    nctx = stageA_begin(*nxt)
                if cbk == 1 and nctx is not None:
                    stageA_finish(nctx)
                    pre_xT[0] = (nctx.xT, nctx.B_xT)
                for t in range(NT):
                    pf, B_pf = PS()
                    mmgroup([(pf[0:T, :], actT[:, k, t * T:(t + 1) * T], sws[k // 8][0][:, k % 8, :], k == 0, k == 21) for k in range(22)],
                            [sws[0][1], sws[1][1], sws[2][1], B_actT], [B_pf])
                    act_sq('', pf[0:T, :], [B_pf], [sq3[t][1]], accum_out=sq3[t][0][:, cbk:cbk + 1])
                    tt("dve", fsb[t][0][:, cbk * 512:(cbk + 1) * 512], pf[0:T, :], gfpost_bc[0:T, cbk * 512:(cbk + 1) * 512], ALU.mult,
                       [B_pf, B_const, sq3[t][1]], [fsb[t][1]])
            if nxt is not None:
                for wb in W_xbc[:3]:
                    prefetch_w(wb)
            for t in range(NT):
                sq, B_sq = sq3[t]
                f_, B_f = fsb[t]
                ht, B_ht = htl[t]
                tt("pool", sq[:, 2:3], sq[:, 0:1], sq[:, 1:2], ALU.add, [B_sq], [B_sq])
                rstd_from(sq[:, 2:3], sq[:, 3:4], D, [B_sq], B_sq)
                stt("dve", f_, f_, sq[:, 3:4], ht, ALU.mult, ALU.add, [B_f, B_sq, B_ht], [B_f])
                S.dma("sp", y_dst[tok0 + t * T: tok0 + (t + 1) * T, :], f_, reads=[B_f], writes=[Buf()])

        def dt_chain(NT, T, xT, B_xT, dtv):
            (v, B_v), (av, B_av), (e_, B_e), (l_, B_l), (dt_, B_dt), (dta, B_dta), (lndt, B_lndt), (nb, B_nb), \
                (eac, B_eac), (wl, B_wl), (wend, B_wend), (elast, B_elast) = dtv
            pa, B_pa = PS()
            pdt = pa[:, 0:NT * NH].rearrange("p (t h) -> p t h", t=NT)
            pac = pa[:, 128:128 + NT * NH].rearrange("p (t h) -> p t h", t=NT)
            ptot = pa[:, 256:256 + NT * NH].rearrange("p (t h) -> p t h", t=NT)
            mmgroup([(pdt[:, t, :], xT[:, k, t * T:(t + 1) * T], wdt[:, k, :], k == 0, k == 7) for t in range(NT) for k in range(8)],
                    [B_xT, B_const, B_wres], [B_pa])
            tt("dve", v, pdt, dtb_bc.unsqueeze(1).to_broadcast([128, NT, NH]), ALU.add, [B_pa, B_const], [B_v])
            act(av, v, AF.Abs, [B_v], [B_av])
            act(e_, av, AF.Exp, [B_av], [B_e], scale=-1.0)
            act(l_, e_, AF.Ln, [B_e], [B_l], bias=1.0)
            stt("dve", dt_, v, 0.0, l_, ALU.max, ALU.add, [B_v, B_l], [B_dt])
            tt("dve", dta, dt_, abc[:].unsqueeze(1).to_broadcast([128, NT, NH]), ALU.mult, [B_dt, B_const], [B_dta])
            act(lndt, dt_, AF.Ln, [B_dt], [B_lndt])
            return pa, B_pa, pac, ptot

        def dt_chain2(NT, dtv, pa, B_pa, pac, ptot):
            (v, B_v), (av, B_av), (e_, B_e), (l_, B_l), (dt_, B_dt), (dta, B_dta), (lndt, B_lndt), (nb, B_nb), \
                (eac, B_eac), (wl, B_wl), (wend, B_wend), (elast, B_elast) = dtv
            pa, B_pa = PS()
            pac = pa[:, 128:128 + NT * NH].rearrange("p (t h) -> p t h", t=NT)
            ptot = pa[:, 256:256 + NT * NH].rearrange("p (t h) -> p t h", t=NT)
            pairs = []
            for t in range(NT):
                pairs.append((pac[:, t, :], trif, dta[:, t, :], True, True))
                pairs.append((ptot[:, t, :], onesf, dta[:, t, :], True, True))
            mmgroup(pairs, [B_dta, B_const], [B_pa])
            tt("dve", nb, lndt, pac, ALU.subtract, [B_lndt, B_pa], [B_nb])
            act(eac, pac, AF.Exp, [B_pa, B_nb], [B_eac])
            tt("dve", wl, ptot, nb, ALU.add, [B_pa, B_nb, B_eac], [B_wl])
            act(elast, ptot, AF.Exp, [B_pa, B_wl], [B_elast])
            act(wend, wl, AF.Exp, [B_wl], [B_wend])

        def ssd_prompt(T, NT, TB, xT, B_xT, xs_tok, B_xs, b_tok, B_bt, bT, B_bT, cT, B_cT, szb, B_szb, yaT, B_yaT, last, dtv):
            (v, B_v), (av, B_av), (e_, B_e), (l_, B_l), (dt_, B_dt), (dta, B_dta), (lndt, B_lndt), (nb, B_nb), \
                (eac, B_eac), (wl, B_wl), (wend, B_wend), (elast, B_elast) = dtv
            d2s = [R2.alloc([4, 128], F32) for _ in range(3)]
            dcs = [R2.alloc([4, 128], BF16) for _ in range(3)]
            mts = [R2.alloc([4, 128], BF16) for _ in range(3)]
            cbss = [R2.alloc([128], F32) for _ in range(3)]
            yts = [R2.alloc([256], F32) for _ in range(3)]
            yz, B_yz = R2.alloc([DI], F32)
            yns = [R2.alloc([DI], BF16) for _ in range(1)]
            xws = [R2.alloc([DI], BF16) for _ in range(2)]
            sq4s = [R2.alloc([16], F32) for _ in range(2)]
            o4b = [R2.alloc([512], F32) for _ in range(2)]
            pool_n = {"seg": 0, "cb": 0, "py": 0, "gen": 0}
            pool_banks = {"seg": [0, 1, 2], "cb": [3, 4], "py": [5, 6], "gen": [7]}

            def PSP(kind):
                lst = pool_banks[kind]
                i = lst[pool_n[kind] % len(lst)]
                pool_n[kind] += 1
                return psb[i], psbuf[i]

            items = []
            for t in range(NT):
                for g in range(NG):
                    it = Ctx()
                    it.t, it.g, it.i = t, g, t * NG + g
                    it.tsl = slice(t * T, (t + 1) * T)
                    items.append(it)

            def sA(it):
                t, g = it.t, it.g
                if g == 0:
                    xw, B_xw = xws[t % 2]
                    tt("pool", xw.rearrange("p (a b) -> p a b", a=NH), xs_tok[:, t, :].rearrange("p (a b) -> p a b", a=NH),
                       wend[:, t, :].unsqueeze(2).to_broadcast([128, NH, 64]), ALU.mult, [B_xs, B_wend], [B_xw])
                it.d2, it.B_d2 = d2s[it.i % 3]
                tt("pool", it.d2, trif.unsqueeze(1).to_broadcast([128, 4, 128]),
                   dta[:, t, g * 4:(g + 1) * 4].unsqueeze(2).to_broadcast([128, 4, 128]), ALU.mult, [B_const, B_dta], [it.B_d2])

            def sB(it):
                t, g = it.t, it.g
                it.pseg, it.B_pseg = PSP("seg")
                mmgroup([(it.pseg[:], identb[:], neg4b[:].rearrange("p a b -> p (a b)"), True, False),
                         (it.pseg[:], onesf, it.d2.rearrange("p a b -> p (a b)"), False, True)], [B_const, it.B_d2], [it.B_pseg])
                it.pcb, it.B_pcb = PSP("cb")
                mmgroup([(it.pcb[:, 0:128], bT[:, g, it.tsl], cT[:, g, it.tsl], True, True)], [B_bT, B_cT], [it.B_pcb])

            def sC(it):
                t, g = it.t, it.g
                it.dc, it.B_dc = dcs[it.i % 3]
                it.cbs, it.B_cbs = cbss[it.i % 3]
                act(it.cbs, it.pcb[:, 0:128], AF.Copy, [it.B_pcb], [it.B_cbs])
                psv = it.pseg[:].rearrange("p (a b) -> p a b", a=4)
                for hh in range(4):
                    act(it.dc[:, hh, :], psv[:, hh, :], AF.Exp, [it.B_pseg, B_nb] if hh in (0, 3) else [], [it.B_dc] if hh in (0, 3) else [],
                        bias=nb[:, t, g * 4 + hh:g * 4 + hh + 1])

            def sD(it):
                it.mt, it.B_mt = mts[it.i % 3]
                tt("dve", it.mt, it.dc, it.cbs.unsqueeze(1).to_broadcast([128, 4, 128]), ALU.mult, [it.B_dc, it.B_cbs], [it.B_mt])

            def sE(it):
                t, g = it.t, it.g
                it.pyy, it.B_pyy = PSP("py")
                pairs = []
                for hh in range(4):
                    hd = g * 4 + hh
                    col = hh * 64
                    pairs.append((it.pyy[:, col:col + 64], it.mt[:, hh, :], xs_tok[:, t, hd * 64:(hd + 1) * 64], hh == 0, False))
                    pairs.append((it.pyy[:, col:col + 64], diagD[:, hd, :], xs_tok[:, t, hd * 64:(hd + 1) * 64], False, False))
                pairs.append((it.pyy[:, 256:512], cT[:, g, it.tsl], hTb[:, g * 256:(g + 1) * 256], False, True))
                mmgroup(pairs, [it.B_mt, B_xs, B_const, B_diag, B_cT, B_hTb], [it.B_pyy])
                qq = g // 2
                hs = hT[:, qq * 512:(qq + 1) * 512]
                if g % 2 == 0:
                    tt("pool", hs.rearrange("p (a b) -> p a b", a=8), hs.rearrange("p (a b) -> p a b", a=8),
                       elast[:, t, qq * 8:(qq + 1) * 8].unsqueeze(2).to_broadcast([128, 8, 64]), ALU.mult, [B_hT, B_elast], [B_hT])
                else:
                    xw, B_xw = xws[t % 2]
                    pu, B_pu = PSP("gen")
                    mmgroup([(pu[:, gg * 256:(gg + 1) * 256], b_tok[:, t, (2 * qq + gg) * 128:(2 * qq + gg + 1) * 128],
                              xw[:, (2 * qq + gg) * 256:(2 * qq + gg + 1) * 256], gg == 0, gg == 1) for gg in range(2)],
                            [B_bt, B_xw], [B_pu])
                    tt("dve", hs, hs, pu[:], ALU.add, [B_hT, B_pu], [B_hT])
                    act(hTb[:, qq * 512:(qq + 1) * 512], hs, AF.Copy, [B_hT], [B_hTb])

            def sF(it):
                t, g = it.t, it.g
                yt, B_yt = yts[it.i % 3]
                sq4, B_sq4 = sq4s[t % 2]
                tt("dve", yt.rearrange("p (a b) -> p a b", a=4), it.pyy[:, 256:512].rearrange("p (a b) -> p a b", a=4),
                   eac[:, t, g * 4:(g + 1) * 4].unsqueeze(2).to_broadcast([128, 4, 64]), ALU.mult, [it.B_pyy, B_eac], [B_yt])
                tt("dve", yt, yt, it.pyy[:, 0:256], ALU.add, [B_yt, it.B_pyy], [B_yt])
                tt("dve", yz[:, g * 256:(g + 1) * 256], yt, szb[:, t, g * 256:(g + 1) * 256], ALU.mult, [B_yt, B_szb], [B_yz])
                act_sq('', yz[:, g * 256:(g + 1) * 256], [B_yz], [B_sq4], accum_out=sq4[:, g:g + 1])
                if g == NG - 1:
                    S.op("dve", lambda h, sq4=sq4: h.tensor_reduce(out=sq4[:, 8:9], in_=sq4[:, 0:8], axis=mybir.AxisListType.X, op=ALU.add),
                         [B_sq4], [B_sq4])
                    rstd_from(sq4[:, 8:9], sq4[:, 9:10], DI, [B_sq4], B_sq4)
                    it.yn, it.B_yn = yns[0]
                    act(it.yn, yz, AF.Copy, [B_yz, B_sq4], [it.B_yn], scale=sq4[:, 9:10])

            def sG(it):
                if it.g != NG - 1:
                    return
                for c0 in range(0, 16, 8):
                    pt, B_pt = PSP("gen")
                    ptb = pt[:].bitcast(BF16).rearrange("p (c t) -> p c t", c=8)
                    trgroup([(ptb[:, c, :], it.yn[:, (c0 + c) * 128:(c0 + c + 1) * 128]) for c in range(8)], identb[:], [it.B_yn, B_const], [B_pt])
                    for c in range(8):
                        act(yaT[:, c0 + c, it.tsl], ptb[:, c, :], AF.Copy, [B_pt, B_const] if c in (0, 7) else [], [B_yaT] if c in (0, 7) else [],
                            scale=gssm_c[:, c0 + c:c0 + c + 1])

            pipeline(items, [sA, sB, sC, sD, sE, sF, None, sG], lag=2)
            if last:
                for c0 in range(0, 16, 4):
                    pt, B_pt = PS()
                    trgroup([(pt[:, c * 128:(c + 1) * 128], hT[:, (c0 + c) * 128:(c0 + c + 1) * 128]) for c in range(4)], identf, [B_hT, B_const], [B_pt])
                    o4, B_o4 = o4b[(c0 // 4) % 2]
                    cp("dve", o4, pt[:], [B_pt], [B_o4])
                    S.dma("sp", nsp[c0 * 128:(c0 + 4) * 128, :].rearrange("(c p) n -> p c n", p=128), o4.rearrange("p (a b) -> p a b", a=4),
                          reads=[B_o4], writes=[Buf()])

        def ssd_sample(T, xT, B_xT, xs_tok, B_xs, b_tok, B_bt, bT, B_bT, cT, B_cT, szb, B_szb, yaT, B_yaT):
            sm_, B_sm = R2.alloc([8, NH], F32, parts=T)
            v, l_, dt_, dta, da, av, e_, _u = [sm_[:, i, :] for i in range(8)]
            pa, B_pa = PS()
            mmgroup([(pa[0:T, 0:32], xT[:, k, :], wdt[:, k, :], k == 0, k == 7) for k in range(8)], [B_xT, B_const, B_wres], [B_pa])
            tt("dve", v, pa[0:T, 0:32], dtb_bc[0:T, :], ALU.add, [B_pa, B_const], [B_sm])
            act(av, v, AF.Abs, [B_sm], [B_sm])
            act(e_, av, AF.Exp, [B_sm], [B_sm], scale=-1.0)
            act(l_, e_, AF.Ln, [B_sm], [B_sm], bias=1.0)
            stt("dve", dt_, v, 0.0, l_, ALU.max, ALU.add, [B_sm], [B_sm])
            tt("dve", dta, dt_, abc[0:T, :], ALU.mult, [B_sm, B_const], [B_sm])
            act(da, dta, AF.Exp, [B_sm], [B_sm])
            rep, B_rep = R2.alloc([DI], F32, parts=T)
            repv = rep.rearrange("b (p c) -> b c p", c=16)
            cols, B_cols = R2.alloc([3, 16, T], F32)
            for j, srcv in enumerate((da, dt_, dsk_bc[0:T, :])):
                cp("dve", rep.rearrange("p (a b) -> p a b", a=NH), srcv.unsqueeze(2).to_broadcast([T, NH, 64]), [B_sm, B_const], [B_rep])
                pt, B_pt = PS()
                ptv = pt[:, 0:16 * T].rearrange("p (c t) -> p c t", c=16)
                trgroup([(ptv[:, c, :], repv[:, c, :]) for c in range(16)], identf[0:T, 0:T], [B_rep, B_const], [B_pt])
                cp("dve", cols[:, j, :, :], ptv, [B_pt], [B_cols])
            xcol, B_xcol = R2.alloc([16, T], F32)
            xsv = xs_tok[:, 0, :].rearrange("b (p c) -> b c p", c=16)
            pt, B_pt = PS()
            ptb = pt[:].bitcast(BF16)[:, 0:16 * T].rearrange("p (c t) -> p c t", c=16)
            trgroup([(ptb[:, c, :], xsv[:, c, :]) for c in range(16)], identb[0:T, 0:T], [B_xs, B_const], [B_pt])
            cp("dve", xcol, ptb, [B_pt], [B_xcol])
            dtx, B_dtx = R2.alloc([16, T], F32)
            tt("dve", dtx, xcol, cols[:, 1, :, :], ALU.mult, [B_xcol, B_cols], [B_dtx])
            bc2, B_bc2 = R2.alloc([2, 128], BF16)
            pt, B_pt = PS()
            ptb = pt[:].bitcast(BF16)[:, 0:256].rearrange("p (a n) -> p a n", a=2)
            trgroup([(ptb[:, 0, :], bT.rearrange("n g b -> n (g b)")), (ptb[:, 1, :], cT.rearrange("n g b -> n (g b)"))], identb[:],
                    [B_bT, B_cT, B_const], [B_pt])
            cp("dve", bc2, ptb, [B_pt], [B_bc2])
            self32, B_self32 = R2.alloc([T, 128], F32)
            sel, B_sel = R2.alloc([T, 128], BF16)
            S.dma("sp", self32.rearrange("p a b -> p (a b)"), selc[:, :], reads=[B_in], writes=[B_self32])
            cp("dve", sel, self32, [B_self32], [B_sel])
            ycol, B_ycol = R2.alloc([16, T], F32)
            junkfs = [R2.alloc([128], F32) for _ in range(3)]
            hbuf = [R1.alloc([16, DS], F32) for _ in range(4)]
            def load_state(bq):
                hbq, B_hbq = hbuf[bq % 4]
                S.dma("sp", hbq, sssm[bq].rearrange("(p c) n -> p c n", c=16), reads=[B_in], writes=[B_hbq])

            for bq in range(3):
                load_state(bq)
            for b in range(T):
                hb, B_hb = hbuf[b % 4]
                pB, B_pB = PS()
                pC, B_pC = PS()
                mmgroup([(pB[:, 0:128], sel[:, b, :], bc2[:, 0, :], True, True)], [B_sel, B_bc2], [B_pB])
                mmgroup([(pC[:, 0:128], sel[:, b, :], bc2[:, 1, :], True, True)], [B_sel, B_bc2], [B_pC])
                for c in range(16):
                    act(hb[:, c, :], hb[:, c, :], AF.Copy, [B_hb, B_cols] if c in (0, 15) else [],
                        [B_hb] if c in (0, 15) else [], scale=cols[:, 0, c, b:b + 1])
                for c in range(16):
                    S.op("dve", lambda h, c=c, b=b, pB=pB, hb=hb: h.scalar_tensor_tensor(
                        out=hb[:, c, :], in0=pB[:, 0:128], scalar=dtx[:, c, b:b + 1], in1=hb[:, c, :],
                        op0=ALU.mult, op1=ALU.add), [B_hb, B_pB, B_dtx] if c in (0, 15) else [],
                        [B_hb] if c in (0, 15) else [])
                S.dma("sp", nss[b].rearrange("(p c) n -> p c n", c=16), hb, reads=[B_hb], writes=[Buf()])
                if b + 3 < T:
                    load_state(b + 3)
                for c in range(16):
                    jf, B_jf = junkfs[c % 3]
                    S.op("dve", lambda h, c=c, b=b, pC=pC, hb=hb, jf=jf: h.scalar_tensor_tensor(
                        out=jf[:, 0:128], in0=hb[:, c, :], scalar=1.0, in1=pC[:, 0:128],
                        op0=ALU.mult, op1=ALU.mult, accum_out=ycol[:, c, b:b + 1]), [B_hb, B_pC] if c in (0, 15) else [],
                        ([B_ycol] if c in (0, 15) else []) + [B_jf])
            dcol, B_dcol = R2.alloc([16, T], F32)
            tt("dve", dcol, cols[:, 2, :, :], xcol, ALU.mult, [B_cols, B_xcol], [B_dcol])
            tt("dve", ycol, ycol, dcol, ALU.add, [B_ycol, B_dcol], [B_ycol])
            ytok, B_ytok = R2.alloc([DI], F32, parts=T)
            ytv = ytok.rearrange("b (p c) -> b c p", c=16)
            szv = szb[:, 0, :].rearrange("b (p c) -> b c p", c=16)
            for c0 in range(0, 16, 4):
                pt, B_pt = PS()
                trgroup([(pt[0:T, c * 128:(c + 1) * 128], ycol[:, c0 + c, :]) for c in range(4)], identf, [B_ycol, B_const], [B_pt])
                tt("dve", ytv[:, c0:c0 + 4, :], pt[0:T, :].rearrange("b (c p) -> b c p", c=4), szv[:, c0:c0 + 4, :], ALU.mult,
                   [B_pt, B_szb], [B_ytok])
            sq, B_sq = R2.alloc([4], F32, parts=T)
            act_sq('', ytok[:, 0:1024], [B_ytok], [B_sq], accum_out=sq[:, 0:1])
            act_sq('', ytok[:, 1024:2048], [B_ytok], [B_sq], accum_out=sq[:, 2:3])
            tt("pool", sq[:, 0:1], sq[:, 0:1], sq[:, 2:3], ALU.add, [B_sq], [B_sq])
            rstd_from(sq[:, 0:1], sq[:, 1:2], DI, [B_sq], B_sq)
            yn, B_yn = R2.alloc([DI], BF16, parts=T)
            act(yn, ytok, AF.Copy, [B_ytok, B_sq], [B_yn], scale=sq[:, 1:2])
            pt, B_pt = PS()
            ptb = pt[:].bitcast(BF16)[:, 0:16 * T].rearrange("p (c t) -> p c t", c=16)
            trgroup([(ptb[:, c, :], yn[:, c * 128:(c + 1) * 128]) for c in range(16)], identb[0:T, 0:T], [B_yn, B_const], [B_pt])
            tt("dve", yaT, ptb, gssm_c.unsqueeze(2).to_broadcast([128, 16, T]), ALU.mult, [B_pt, B_const], [B_yaT])

        try:
            checkpoint()
            for blk in range(SEQ // 512):
                run_block("prompt", blk)
            run_block("sample", 0)
        except _Stop:
            pass
        S.finish()
    return nc


def _consts():
    c = np.zeros((128, NCST), np.float32)
    k = np.arange(128)[:, None]
    i = np.arange(128)[None, :]
    c[:, CS_ID:CS_ID + 128] = (k == i)
    c[:, CS_TRI:CS_TRI + 128] = (k <= i)
    c[:, CS_NEG:CS_NEG + 128] = np.where(i < k, NEG, 0.0)
    c[:, CS_ONE:CS_ONE + 128] = 1.0
    inv = np.zeros((8, 16), np.float32)
    for ch in range(8):
        w = POOLW[ch // 2]
        inv[ch] = 1.0 / np.minimum(np.arange(16) + 1, w)
    c[:, CS_INV:CS_INV + 128] = inv.reshape(1, 128)
    return c


_NC_CACHE = {}


def kernel(x_prompt, x_sample, state_conv, state_ssm, state_pool,
           norm_mix_pre, norm_mix_post, norm_ffn_pre, norm_ffn_post,
           w_in, conv_w, conv_b, dt_bias, a_log, d_skip, ssm_norm,
           w_pool_group, pool_scale, w_branch_a, w_branch_b, w_out,
           w_ffn_in, w_ffn_out):
    f = lambda a: np.ascontiguousarray(np.asarray(a, dtype=np.float32))
    fm = lambda v: f(v).reshape(-1, 128).T
    colp = np.zeros((128, NCOL), np.float32)
    colp[:, CP_GPRE:CP_GPRE + 8] = fm(norm_mix_pre[0])
    colp[:, CP_GFFN:CP_GFFN + 8] = fm(norm_ffn_pre[0])
    colp[:, CP_PS:CP_PS + 8] = fm(pool_scale[0])
    colp[:, CP_GSSM:CP_GSSM + 16] = fm(ssm_norm[0])
    cw = f(conv_w[0]).reshape(4, 32, 128).transpose(2, 1, 0)
    colp[:, CP_CW:CP_CW + 128] = cw.reshape(128, 128)
    colp[:, CP_CB:CP_CB + 32] = fm(conv_b[0])
    rowp = np.concatenate([f(norm_mix_post[0]), f(norm_ffn_post[0]), f(dt_bias[0]), f(a_log[0]), f(d_skip[0])])[None, :]
    rowp = f(rowp)
    cst = _consts()
    kk = np.arange(128)[:, None, None]
    bb = np.arange(NSB)[None, :, None]
    mm_ = np.arange(128)[None, None, :]
    selc = (((kk % NSB) == bb) & ((kk // NSB) == (mm_ // 16))).astype(np.float32).reshape(128, NSB * 128)
    shared = {
        "w_in": f(w_in[0]), "w_pg": f(w_pool_group[0]).reshape(D, 256), "w_a": f(w_branch_a[0]), "w_b": f(w_branch_b[0]),
        "w_o": f(w_out[0]), "w_fi": f(w_ffn_in[0]), "w_fo": f(w_ffn_out[0]), "colp": colp, "rowp": rowp, "cst": cst, "selc": selc,
    }
    xpr = f(x_prompt)
    xsr = f(x_sample).reshape(128, D)
    sc = f(state_conv[0])
    ss = f(state_ssm[0]).reshape(128, DI, DS)
    sp = f(state_pool[0])
    in_maps = []
    for c in range(NCORES):
        m = dict(shared)
        m["xp"] = xpr[c]
        m["xs"] = xsr[c * NSB:(c + 1) * NSB]
        m["sconv"] = sc[c * NSB:(c + 1) * NSB]
        m["sssm"] = ss[c * NSB:(c + 1) * NSB]
        m["spool"] = sp[c * NSB:(c + 1) * NSB]
        in_maps.append(m)
    if "nc" not in _NC_CACHE:
        _NC_CACHE["nc"] = build_program()
    nc = _NC_CACHE["nc"]
    res = run_bass_kernel_spmd(nc, in_maps, core_ids=list(range(NCORES)))
    R = res.results
    y_prompt = np.stack([R[c]["yp"] for c in range(NCORES)])
    y_sample = np.concatenate([R[c]["ys"] for c in range(NCORES)]).reshape(128, 1, D)
    ncp_ = np.stack([R[c]["ncp"] for c in range(NCORES)])[None]
    nsp_ = np.stack([R[c]["nsp"].reshape(NH, HD, DS) for c in range(NCORES)])[None]
    npp_ = np.stack([R[c]["npp"] for c in range(NCORES)])[None]
    ncs_ = np.concatenate([R[c]["ncs"] for c in range(NCORES)])[None]
    nss_ = np.concatenate([R[c]["nss"] for c in range(NCORES)]).reshape(1, 128, NH, HD, DS)
    nps_ = np.concatenate([R[c]["nps"] for c in range(NCORES)])[None]
    return (y_prompt.astype(np.float32), y_sample.astype(np.float32), ncp_.astype(np.float32), nsp_.astype(np.float32),
            npp_.astype(np.float32), ncs_.astype(np.float32), nss_.astype(np.float32), nps_.astype(np.float32))
```
